# Optimizing a Trainium2 kernel written in Bass

```python
import math
import jax, jax.numpy as jnp
from jax import lax
import numpy as np

D_MODEL = 2048
BATCH = 16
SEQ = 2048
DEPTH = 2

GRID_W = 64
CTX_LEN = 256
N_MIXERS = 2
N_LRU = (DEPTH + 1) // 2
N_ATT = DEPTH // 2
LRU_WIDTH = D_MODEL
LRU_BLOCKS = 16
LRU_BLOCK = LRU_WIDTH // LRU_BLOCKS
LRU_C = 8.0
CONV_W = 4
CONV_PAD = (2, 1)
ATT_HEAD_DIM = 64
ATT_HEADS = D_MODEL // (2 * ATT_HEAD_DIM)
ATT_V_DIM = 2 * ATT_HEAD_DIM
ATT_WIDTH = ATT_HEADS * ATT_V_DIM
ROPE_FREQS = ATT_HEAD_DIM // 4
ROPE_BASE = 10000.0
Q_BLOCK = 128
EPS = 1e-6

kernel_name = "hybrid_rglru_diffattn_dit_block"


def rms_norm(x, g):
    xf = x.astype(jnp.float32)
    y = xf * lax.rsqrt(jnp.mean(xf * xf, axis=-1, keepdims=True) + EPS)
    return (y * g.astype(jnp.float32)).astype(x.dtype)


def dwconv(u, w, b):
    y = lax.conv_general_dilated(u, w[:, None, :], window_strides=(1,), padding=[CONV_PAD],
                                 dimension_numbers=("NWC", "WIO", "NWC"),
                                 feature_group_count=u.shape[-1])
    return y + b


def lru_coeffs(xc, gw, gb, lam):
    B_, L, _ = xc.shape
    xb = xc.reshape(B_, L, LRU_BLOCKS, LRU_BLOCK)
    gates = jax.nn.sigmoid((jnp.einsum("blnc,gncd->gblnd", xb, gw) + gb[:, None, None]).astype(jnp.float32))
    gates = gates.reshape(2, B_, L, LRU_WIDTH)
    r, i = gates[0], gates[1]
    log_a = -LRU_C * r * jax.nn.softplus(-lam.astype(jnp.float32))
    a = jnp.exp(log_a)
    b = jnp.sqrt(-jnp.expm1(2.0 * log_a)) * (i * xc.astype(jnp.float32))
    return a, b


def linear_scan(a, b, h0, reverse):
    def step(h, ab):
        at, bt = ab
        h = at * h + bt
        return h, h
    h_end, hs = lax.scan(step, h0, (jnp.swapaxes(a, 0, 1), jnp.swapaxes(b, 0, 1)), reverse=reverse)
    return h_end, jnp.swapaxes(hs, 0, 1)


def rglru_mixer(h_lat, h_ctx, w_in, conv_w, conv_b, gate_w, gate_b, lam, w_out, need_ctx_out):
    u_lat, g_lat = jnp.split(h_lat @ w_in, 2, axis=-1)
    u_ctx, g_ctx = jnp.split(h_ctx @ w_in, 2, axis=-1)
    xc_lat = dwconv(u_lat, conv_w, conv_b)
    xc_ctx = dwconv(u_ctx, conv_w, conv_b)
    h0 = jnp.zeros((h_lat.shape[0], LRU_WIDTH), jnp.float32)
    outs_lat, outs_ctx = [], []
    for d, rev in enumerate((False, True)):
        a_c, b_c = lru_coeffs(xc_ctx, gate_w[d], gate_b[d], lam[d])
        h_end, hs_ctx = linear_scan(a_c, b_c, h0, rev)
        a_l, b_l = lru_coeffs(xc_lat, gate_w[d], gate_b[d], lam[d])
        _, hs_lat = linear_scan(a_l, b_l, h_end, rev)
        outs_lat.append(hs_lat)
        outs_ctx.append(hs_ctx)
    y_lat = ((outs_lat[0] + outs_lat[1]).astype(h_lat.dtype) * jax.nn.silu(g_lat)) @ w_out
    y_ctx = None
    if need_ctx_out:
        y_ctx = ((outs_ctx[0] + outs_ctx[1]).astype(h_ctx.dtype) * jax.nn.silu(g_ctx)) @ w_out
    return y_lat, y_ctx


def axial_rope_tables(S):
    rows = S // GRID_W
    row = jnp.broadcast_to(jnp.arange(rows)[:, None], (rows, GRID_W)).reshape(S).astype(jnp.float32)
    col = jnp.broadcast_to(jnp.arange(GRID_W)[None, :], (rows, GRID_W)).reshape(S).astype(jnp.float32)
    inv = ROPE_BASE ** (-jnp.arange(ROPE_FREQS, dtype=jnp.float32) / ROPE_FREQS)
    ang = jnp.stack([row[:, None] * inv, col[:, None] * inv], axis=1)
    return jnp.cos(ang), jnp.sin(ang)


def apply_axial_rope(x, cos, sin):
    xs = x.astype(jnp.float32).reshape(x.shape[:-1] + (2, 2, ROPE_FREQS))
    x1, x2 = xs[..., 0, :], xs[..., 1, :]
    out = jnp.stack([x1 * cos - x2 * sin, x2 * cos + x1 * sin], axis=-2)
    return out.reshape(x.shape).astype(x.dtype)


def diff_attend(q, k, v, lam):
    s = jnp.einsum("bhcqd,bhckd->bhcqk", q, k).astype(jnp.float32)
    p = jax.nn.softmax(s, axis=-1)
    attn = p[:, :, 0] - lam * p[:, :, 1]
    return jnp.einsum("bhqk,bhkv->bhqv", attn.astype(v.dtype), v)


def diff_attn_mixer(h_lat, h_ctx, w_in, q_g, k_g, lam_vecs, subln_g, w_out, lam_init, cos, sin, need_ctx_out):
    def project(h):
        B_, L, _ = h.shape
        q, k, v, g = jnp.split(h @ w_in, 4, axis=-1)
        q = rms_norm(q.reshape(B_, L, ATT_HEADS, 2, ATT_HEAD_DIM), q_g).transpose(0, 2, 3, 1, 4)
        k = rms_norm(k.reshape(B_, L, ATT_HEADS, 2, ATT_HEAD_DIM), k_g).transpose(0, 2, 3, 1, 4)
        v = v.reshape(B_, L, ATT_HEADS, ATT_V_DIM).transpose(0, 2, 1, 3)
        return q * (ATT_HEAD_DIM ** -0.5), k, v, g

    def finish(o, g):
        B_, _, L, _ = o.shape
        o = rms_norm(o, subln_g) * (1.0 - lam_init)
        o = o.transpose(0, 2, 1, 3).reshape(B_, L, ATT_WIDTH)
        return (o * jax.nn.silu(g)) @ w_out

    lv = lam_vecs.astype(jnp.float32)
    lam = jnp.exp(jnp.sum(lv[0] * lv[1])) - jnp.exp(jnp.sum(lv[2] * lv[3])) + lam_init

    q_l, k_l, v_l, g_l = project(h_lat)
    q_c, k_c, v_c, g_c = project(h_ctx)
    q_l = apply_axial_rope(q_l, cos, sin)
    k_l = apply_axial_rope(k_l, cos, sin)
    k_all = jnp.concatenate([k_l, k_c], axis=3)
    v_all = jnp.concatenate([v_l, v_c], axis=2)

    B_, H, _, S, d = q_l.shape
    nblk = S // Q_BLOCK
    qb = jnp.moveaxis(q_l.reshape(B_, H, 2, nblk, Q_BLOCK, d), 3, 0)
    ob = lax.map(lambda qq: diff_attend(qq, k_all, v_all, lam), qb)
    o_l = jnp.moveaxis(ob, 0, 2).reshape(B_, H, S, ATT_V_DIM)
    y_lat = finish(o_l, g_l)
    y_ctx = None
    if need_ctx_out:
        y_ctx = finish(diff_attend(q_c, k_c, v_c, lam), g_c)
    return y_lat, y_ctx


def setup_inputs(seed: int = 0) -> dict:
    key = jax.random.key(seed)
    ks = jax.random.split(key, 24)
    f32 = jnp.float32
    D, W = D_MODEL, LRU_WIDTH
    u = jax.random.uniform(ks[11], (N_LRU, 2, W), f32, 0.9, 0.999)
    a = u ** (1.0 / LRU_C)
    return {
        "x": jax.random.normal(ks[0], (BATCH, SEQ, D), f32),
        "c": jax.random.normal(ks[1], (BATCH, D), f32),
        "ctx": jax.random.normal(ks[2], (BATCH, CTX_LEN, D), f32),
        "c_ctx": jax.random.normal(ks[3], (D,), f32),
        "mod_w": jax.random.normal(ks[4], (DEPTH, D, 3 * D), f32) * (0.5 * D ** -0.5),
        "mod_b": jax.random.normal(ks[5], (DEPTH, 3 * D), f32) * 0.01,
        "norm_g": 1.0 + 0.1 * jax.random.normal(ks[6], (DEPTH, D), f32),
        "lru_w_in": jax.random.normal(ks[7], (N_LRU, D, 2 * W), f32) * D ** -0.5,
        "lru_conv_w": jax.random.normal(ks[8], (N_LRU, CONV_W, W), f32) * CONV_W ** -0.5,
        "lru_conv_b": jax.random.normal(ks[9], (N_LRU, W), f32) * 0.01,
        "lru_gate_w": jax.random.normal(ks[10], (N_LRU, 2, 2, LRU_BLOCKS, LRU_BLOCK, LRU_BLOCK), f32) * LRU_BLOCK ** -0.5,
        "lru_gate_b": jax.random.normal(ks[12], (N_LRU, 2, 2, LRU_BLOCKS, LRU_BLOCK), f32) * 0.01,
        "lru_lambda": jnp.log(a) - jnp.log1p(-a),
        "lru_w_out": jax.random.normal(ks[13], (N_LRU, W, D), f32) * W ** -0.5,
        "att_w_in": jax.random.normal(ks[14], (N_ATT, D, 4 * ATT_WIDTH), f32) * D ** -0.5,
        "att_q_norm": 1.0 + 0.1 * jax.random.normal(ks[15], (N_ATT, ATT_HEAD_DIM), f32),
        "att_k_norm": 1.0 + 0.1 * jax.random.normal(ks[16], (N_ATT, ATT_HEAD_DIM), f32),
        "att_lambda": 0.1 * jax.random.normal(ks[17], (N_ATT, 4, ATT_HEAD_DIM), f32),
        "att_subln": 1.0 + 0.1 * jax.random.normal(ks[18], (N_ATT, ATT_V_DIM), f32),
        "att_w_out": jax.random.normal(ks[19], (N_ATT, ATT_WIDTH, D), f32) * ATT_WIDTH ** -0.5,
    }


def reference(x, c, ctx, c_ctx, mod_w, mod_b, norm_g, lru_w_in, lru_conv_w, lru_conv_b, lru_gate_w,
              lru_gate_b, lru_lambda, lru_w_out, att_w_in, att_q_norm, att_k_norm, att_lambda,
              att_subln, att_w_out):
    S = x.shape[1]
    cos, sin = axial_rope_tables(S)
    sc = jax.nn.silu(c)
    scc = jax.nn.silu(c_ctx)
    for i in range(DEPTH):
        need_ctx_out = i < DEPTH - 1
        shift, scale, gate = jnp.split(sc @ mod_w[i] + mod_b[i], 3, axis=-1)
        cshift, cscale, cgate = jnp.split(scc @ mod_w[i] + mod_b[i], 3, axis=-1)
        h_lat = rms_norm(x, norm_g[i]) * (1.0 + scale[:, None]) + shift[:, None]
        h_ctx = rms_norm(ctx, norm_g[i]) * (1.0 + cscale) + cshift
        j = i // N_MIXERS
        if i % N_MIXERS == 0:
            y_lat, y_ctx = rglru_mixer(h_lat, h_ctx, lru_w_in[j], lru_conv_w[j], lru_conv_b[j],
                                       lru_gate_w[j], lru_gate_b[j], lru_lambda[j], lru_w_out[j],
                                       need_ctx_out)
        else:
            lam_init = 0.8 - 0.6 * math.exp(-0.3 * i)
            y_lat, y_ctx = diff_attn_mixer(h_lat, h_ctx, att_w_in[j], att_q_norm[j], att_k_norm[j],
                                           att_lambda[j], att_subln[j], att_w_out[j], lam_init,
                                           cos, sin, need_ctx_out)
        x = x + gate[:, None] * y_lat
        if need_ctx_out:
            ctx = ctx + cgate * y_ctx
    return x
```

```python
import math
import contextlib
import numpy as np
import concourse.bass as bass
import concourse.mybir as mybir
from concourse.bass_utils import run_bass_kernel_spmd

F32 = mybir.dt.float32
BF16 = mybir.dt.bfloat16
ALU = mybir.AluOpType
AF = mybir.ActivationFunctionType

NCORES = 8
BPC = 2
D = 2048
S_LAT = 2048
S_CTX = 256
NT = S_LAT + S_CTX
NTILE = NT // 128
CH = [(0, 256), (256, 768), (768, 1280), (1280, 1792), (1792, 2304)]
LAM_INIT = 0.8 - 0.6 * math.exp(-0.3 * 1)
EPS = 1e-6
ARENA_BYTES = 211600
PERS_BYTES = 18432
H_BYTES = 16 * NT * 2


_ALL_TK = []


class Tk:
    __slots__ = ("w", "r", "excl", "small")

    def __init__(self, excl=False, small=False):
        self.w = None
        self.r = {}
        self.excl = excl
        self.small = small
        _ALL_TK.append(self)


class Buf:
    __slots__ = ("ap", "tk")

    def __init__(self, ap, small=False):
        self.ap = ap
        self.tk = Tk(small=small)


class Op:
    __slots__ = ("eng", "fn", "deps", "signal", "count", "dma", "dsem", "dval", "hard", "phase")


class Sched:
    COMPUTE = ("pe", "act", "dve", "pool")
    QUEUES = ("sp", "act", "pool")

    def __init__(self, nc, ndma_sems=8):
        self.nc = nc
        self.names = ("pe", "act", "dve", "pool", "sp")
        self.ops = {k: [] for k in self.names}
        self.ndma = {k: 0 for k in self.names}
        self.ndma_sems = ndma_sems
        self.dma_ops = {k: [] for k in self.names}
        self.bar_deps = {k: [] for k in self.names}
        self.phase = 0

    def op(self, eng, fn, reads=(), writes=(), dma=False, sreads=()):
        o = Op()
        o.eng = eng; o.fn = fn; o.deps = []; o.signal = False; o.count = None; o.dma = dma; o.hard = None
        o.phase = self.phase
        if self.bar_deps[eng]:
            o.deps.extend(self.bar_deps[eng])
            self.bar_deps[eng] = []
        if dma:
            i = self.ndma[eng]; self.ndma[eng] += 1
            o.dsem = i % self.ndma_sems
            o.dval = 16 * (i // self.ndma_sems + 1)
            if i >= self.ndma_sems:
                o.deps.append(self.dma_ops[eng][i - self.ndma_sems])
            self.dma_ops[eng].append(o)
        rkey = ("d", id(o)) if dma else eng
        for t in sreads:
            if t.w is not None:
                o.deps.append(t.w)
                if t.w.eng == eng and not t.w.dma:
                    if o.hard is None:
                        o.hard = set()
                    o.hard.add(id(t.w))
                    t.w.signal = True
            t.r[rkey] = o
        for t in reads:
            if t.w is not None:
                o.deps.append(t.w)
                if t.small and t.w.eng == eng and not t.w.dma:
                    if o.hard is None:
                        o.hard = set()
                    o.hard.add(id(t.w))
                    t.w.signal = True
            if t.excl:
                for kk, ro in t.r.items():
                    if kk != rkey:
                        o.deps.append(ro)
            t.r[rkey] = o
        for t in writes:
            if t.w is not None:
                o.deps.append(t.w)
            o.deps.extend(t.r.values())
            t.w = o
            t.r = {}
        for d in o.deps:
            if not d.dma and d.eng != eng:
                d.signal = True
        self.ops[eng].append(o)
        return o

    def barrier(self):
        new = []
        for k in self.COMPUTE:
            last = None
            for o in reversed(self.ops[k]):
                if not o.dma:
                    last = o
                    break
            if last is not None and last.phase == self.phase:
                last.signal = True
                new.append(last)
        for k in self.QUEUES:
            n = len(self.dma_ops[k])
            for o in self.dma_ops[k][max(0, n - self.ndma_sems):]:
                new.append(o)
        for k in self.names:
            self.bar_deps[k] = list(new)
        for t in _ALL_TK:
            t.w = None
            t.r = {}
        self.phase += 1

    def emit(self):
        nc = self.nc
        with contextlib.ExitStack() as st:
            csem = [{k: st.enter_context(nc.semaphore(f"c{s_}_{k}")) for k in self.COMPUTE} for s_ in range(3)]
            dsem = {k: [st.enter_context(nc.semaphore(f"d_{k}{j}")) for j in range(self.ndma_sems)]
                    for k in self.QUEUES}
            block = st.enter_context(nc.Block())
            self.maxcount = {}
            for k in self.COMPUTE:
                c = 0; ph = -1; mx = 0
                for o in self.ops[k]:
                    if o.phase != ph:
                        ph = o.phase; c = 0
                    if o.signal and not o.dma:
                        c += 1
                        o.count = c
                        mx = max(mx, c)
                self.maxcount[k] = (mx, len(self.ops[k]))

            def run(k, e):
                waited = {}
                ph = 0
                for o in self.ops[k]:
                    need = {}
                    for d in o.deps:
                        if d is o:
                            continue
                        if d.dma:
                            key = ("d", d.eng, d.dsem); sem = dsem[d.eng][d.dsem]; val = d.dval
                        else:
                            if d.eng == k and (o.hard is None or id(d) not in o.hard):
                                continue
                            key = ("c", d.eng, d.phase); sem = csem[d.phase % 3][d.eng]; val = d.count
                        if waited.get(key, 0) >= val:
                            continue
                        if key not in need or need[key][1] < val:
                            need[key] = (sem, val)
                    for key, (sem, val) in need.items():
                        e.wait_ge(sem, val)
                        waited[key] = val
                    if k == "pool" and o.phase != ph:
                        assert o.phase == ph + 1, "pool needs an op in every phase"
                        ph = o.phase
                        if ph >= 2:
                            for kk in self.COMPUTE:
                                e.sem_clear(csem[(ph + 1) % 3][kk])
                    ins = o.fn(e)
                    if o.dma:
                        ins.then_inc(dsem[k][o.dsem], 16)
                    elif o.signal:
                        ins.then_inc(csem[o.phase % 3][k], 1)
                if k in self.QUEUES:
                    n = len(self.dma_ops[k])
                    for d in self.dma_ops[k][max(0, n - self.ndma_sems):]:
                        if waited.get(("d", k, d.dsem), 0) < d.dval:
                            e.wait_ge(dsem[k][d.dsem], d.dval)
                            waited[("d", k, d.dsem)] = d.dval

            block.sync(lambda e: run("sp", e))
            block.scalar(lambda e: run("act", e))
            block.vector(lambda e: run("dve", e))
            block.gpsimd(lambda e: run("pool", e))
            block.tensor(lambda e: run("pe", e))


def build_program(nb=BPC, debug=False, stop=None, nheads=16, b1step=9, qk=9, vg=9):
    nc = bass.Bass("TRN2", target_bir_lowering=False)
    del _ALL_TK[:]

    def din(name, shape):
        return nc.dram_tensor(name, list(shape), F32, kind="ExternalInput").ap()

    x_d = din("x", [nb, S_LAT, D])
    ctx_d = din("ctx", [nb, S_CTX, D])
    cc_d = din("cc", [3, D])
    mod_w_d = din("mod_w", [2, D, 3 * D])
    mod_b_d = din("mod_b", [2, 3 * D])
    norm_g_d = din("norm_g", [2, D])
    lru_w_in_d = din("lru_w_in", [1, D, 2 * D])
    lru_conv_w_d = din("lru_conv_w", [1, 4, D])
    lru_conv_b_d = din("lru_conv_b", [1, D])
    lru_gate_w_d = din("lru_gate_w", [1, 2, 2, 16, 128, 128])
    lru_gate_b_d = din("lru_gate_b", [1, 2, 2, 16, 128])
    lru_lambda_d = din("lru_lambda", [1, 2, D])
    lru_w_out_d = din("lru_w_out", [1, D, D])
    att_w_in_d = din("att_w_in", [1, D, 4 * D])
    att_q_norm_d = din("att_q_norm", [1, 64])
    att_k_norm_d = din("att_k_norm", [1, 64])
    att_lambda_d = din("att_lambda", [1, 4, 64])
    att_subln_d = din("att_subln", [1, 128])
    att_w_out_d = din("att_w_out", [1, D, D])
    k_tc_d = din("k_tc", [128, S_LAT])
    k_ts_d = din("k_ts", [128, S_LAT])
    k_mats_d = din("k_mats", [3, 128, 128])
    out_d = nc.dram_tensor("out", [nb, S_LAT, D], F32, kind="ExternalOutput").ap()
    x1_s = nc.dram_tensor("x1_s", [NT, D], F32, kind=("ExternalOutput" if debug else "Internal")).ap()
    yT0_s = nc.dram_tensor("yT0_s", [16, 128, NT], BF16, kind=("ExternalOutput" if debug else "Internal")).ap()
    yT1_s = nc.dram_tensor("yT1_s", [16, 128, S_LAT], BF16, kind=("ExternalOutput" if debug else "Internal")).ap()
    if debug:
        dbg_h = nc.dram_tensor("dbg_h", [128, 16, NT], BF16, kind="ExternalOutput").ap()
        dbg_m = nc.dram_tensor("dbg_m", [128, 2 * 48 * 3], F32, kind="ExternalOutput").ap()
        dbg_q = nc.dram_tensor("dbg_q", [128, S_LAT], BF16, kind="ExternalOutput").ap()
        dbg_k = nc.dram_tensor("dbg_k", [128, NT], BF16, kind="ExternalOutput").ap()
        dbg_v = nc.dram_tensor("dbg_v", [128, NTILE, 132], BF16, kind="ExternalOutput").ap()
        dbg_sg = nc.dram_tensor("dbg_sg", [128, 16, 128], F32, kind="ExternalOutput").ap()
        dbg_y = nc.dram_tensor("dbg_y", [128, S_LAT], BF16, kind="ExternalOutput").ap()
        dbg_o = nc.dram_tensor("dbg_o", [128, 128], F32, kind="ExternalOutput").ap()
        dbg_c = nc.dram_tensor("dbg_c", [128, 32 * 3 + 4 + 4], F32, kind="ExternalOutput").ap()

    S = Sched(nc)
    st = contextlib.ExitStack()
    arena = st.enter_context(nc.sbuf_tensor("arena", [128, ARENA_BYTES // 4], F32))
    psum = st.enter_context(nc.psum_tensor("psum", [128, 8, 512], F32))
    pt = [Tk(excl=True) for _ in range(8)]

    class Alloc:
        def __init__(self, lo, hi):
            self.lo = lo; self.hi = hi; self.cur = lo

        def reset(self):
            self.cur = self.lo

        def __call__(self, shape, dt):
            n = int(np.prod(shape[1:]))
            nbytes = n * (4 if dt == F32 else 2)
            nbytes = (nbytes + 31) // 32 * 32
            off = self.cur
            self.cur += nbytes
            assert self.cur <= self.hi, (shape, self.cur, self.hi)
            if dt == F32:
                v = arena[:, off // 4: off // 4 + n]
            else:
                v = arena[:, off // 4: off // 4 + (n + 1) // 2].bitcast(BF16)[:, 0:n]
            if len(shape) == 3:
                v = v.rearrange("p (a b) -> p a b", b=shape[2])
            elif len(shape) == 4:
                v = v.rearrange("p (a b c) -> p a b c", b=shape[2], c=shape[3])
            if shape[0] != 128:
                v = v[0:shape[0]]
            return Buf(v, small=(n <= 256))

    P = Alloc(0, PERS_BYTES)
    Hh = Alloc(PERS_BYTES, PERS_BYTES + H_BYTES)
    W = Alloc(PERS_BYTES + H_BYTES, ARENA_BYTES)

    def E(eng, meth, *args, reads=(), writes=(), sreads=(), **kw):
        return S.op(eng, lambda e: getattr(e, meth)(*args, **kw), reads=reads, writes=writes, sreads=sreads)

    def DMA(q, out, in_, reads=(), writes=(), **kw):
        return S.op(q, lambda e: e.dma_start(out=out, in_=in_, **kw), reads=reads, writes=writes, dma=True)

    def MM(out, lhsT, rhs, start, stop, reads, writes, **kw):
        return S.op("pe", lambda e: e.matmul(out, lhsT=lhsT, rhs=rhs, start=start, stop=stop, **kw),
                    reads=reads, writes=writes)

    bank_ctr = [0]

    def BAR():
        S.barrier()
        E("pool", "memset", cols.ap[:, 7:8], 0.0)

    def nb_(mod=6):
        b_ = bank_ctr[0] % mod
        bank_ctr[0] += 1
        return b_

    identf = P([128, 128], F32); pswapf = P([128, 128], F32); bonesf = P([128, 128], F32); onesf = P([128, 128], F32)
    ident = P([128, 128], BF16); pswap = P([128, 128], BF16); bones = P([128, 128], BF16)
    modT = P([128, 2, 48, 3], F32)
    AT = P([128, 2, 16, 3], F32)
    gcol = P([128, 2, 16], F32)
    cw = P([128, 16, 4], F32); cb = P([128, 16], F32); gb = P([128, 64], F32)
    lamc = P([128, 32], F32); sp8 = P([128, 32], F32); sp16 = P([128, 32], F32)
    cols = P([128, 8], F32)
    G4 = P([128, 4], F32); gcraw = P([128, 2], F32)
    lamcol = P([128, 4], F32)
    sublnG = P([128, 128], F32)
    lv = P([128, 256], F32); lvp = P([128, 128], F32); lvs = P([128, 2], F32)
    scT = P([128, 16, 3], F32)
    ssq = P([128, 32], F32); rsq = P([128, 32], F32); tsq = P([128, 32], F32)
    gate_bc = P([128, D], F32)
    hT = Hh([128, 16, NT], BF16)
    hT_tk = [Tk() for _ in range(NTILE)]
    CONST = Tk()

    def hT_tks(lo, hi):
        return hT_tk[lo // 128: hi // 128]

    def phase0():
        W.reset()
        wb = [W([128, 16, 512], F32) for _ in range(2)]
        mb = [W([3, 512], F32) for _ in range(2)]
        rowc = [W([3, 512], F32) for _ in range(2)]
        DMA("sp", identf.ap, k_mats_d[0], writes=[identf.tk])
        DMA("sp", pswapf.ap, k_mats_d[1], writes=[pswapf.tk])
        DMA("sp", bonesf.ap, k_mats_d[2], writes=[bonesf.tk])
        E("pool", "memset", onesf.ap, 1.0, writes=[onesf.tk])
        E("pool", "memset", cols.ap[:, 0:1], EPS, writes=[cols.tk])
        E("pool", "memset", cols.ap[:, 1:2], 1.0, writes=[cols.tk])
        E("pool", "tensor_copy", out=ident.ap, in_=identf.ap, reads=[identf.tk], writes=[ident.tk])
        E("pool", "tensor_copy", out=pswap.ap, in_=pswapf.ap, reads=[pswapf.tk], writes=[pswap.tk])
        E("pool", "tensor_copy", out=bones.ap, in_=bonesf.ap, reads=[bonesf.tk], writes=[bones.tk])
        nsl = dict(allow_slow_non_contiguous=True)
        for r in range(3):
            DMA("sp", scT.ap[:, :, r], cc_d[r].rearrange("(k p) -> p k", p=128), writes=[scT.tk], **nsl)
        for l in range(2):
            DMA("sp", gcol.ap[:, l, :], norm_g_d[l].rearrange("(k p) -> p k", p=128), writes=[gcol.tk], **nsl)
        for j in range(4):
            DMA("sp", cw.ap[:, :, j], lru_conv_w_d[0, j].rearrange("(n p) -> p n", p=128), writes=[cw.tk], **nsl)
        DMA("sp", cb.ap, lru_conv_b_d[0].rearrange("(n p) -> p n", p=128), writes=[cb.tk], **nsl)
        DMA("sp", gb.ap, lru_gate_b_d[0].rearrange("d g n p -> p (d g n)"), writes=[gb.tk], **nsl)
        DMA("sp", lamc.ap, lru_lambda_d[0].rearrange("d (n p) -> p (d n)", p=128), writes=[lamc.tk], **nsl)
        for c in range(2):
            DMA("sp", gcraw.ap[c * 64:(c + 1) * 64, 0:1], att_q_norm_d[0].rearrange("(d o) -> d o", o=1),
                writes=[gcraw.tk], **nsl)
            DMA("sp", gcraw.ap[c * 64:(c + 1) * 64, 1:2], att_k_norm_d[0].rearrange("(d o) -> d o", o=1),
                writes=[gcraw.tk], **nsl)
        DMA("sp", lv.ap, att_lambda_d[0].rearrange("a d -> (a d)").partition_broadcast(128), writes=[lv.tk])
        DMA("sp", sublnG.ap, att_subln_d[0].partition_broadcast(128), writes=[sublnG.tk])
        E("act", "activation", out=scT.ap, in_=scT.ap, func=AF.Silu, reads=[scT.tk], writes=[scT.tk])
        E("act", "activation", out=sp8.ap, in_=lamc.ap, func=AF.Exp, scale=-1.0, reads=[lamc.tk], writes=[sp8.tk])
        E("act", "activation", out=sp8.ap, in_=sp8.ap, func=AF.Ln, bias=cols.ap[:, 1:2], scale=1.0,
          reads=[sp8.tk, cols.tk], writes=[sp8.tk])
        E("dve", "tensor_scalar", out=sp16.ap, in0=sp8.ap, scalar1=-16.0, scalar2=None, op0=ALU.mult,
          reads=[sp8.tk], writes=[sp16.tk])
        E("dve", "tensor_scalar", out=sp8.ap, in0=sp8.ap, scalar1=-8.0, scalar2=None, op0=ALU.mult,
          reads=[sp8.tk, sp16.tk], writes=[sp8.tk])
        MM(psum[:, 7, 0:2], pswapf.ap, gcraw.ap, True, True, reads=[pswapf.tk, gcraw.tk], writes=[pt[7]])
        E("dve", "tensor_scalar", out=G4.ap[:, 0:1], in0=gcraw.ap[:, 0:1], scalar1=0.125, scalar2=None, op0=ALU.mult,
          reads=[gcraw.tk], writes=[G4.tk])
        E("dve", "tensor_scalar", out=G4.ap[:, 1:2], in0=psum[:, 7, 0:1], scalar1=0.125, scalar2=None, op0=ALU.mult,
          reads=[pt[7]], writes=[G4.tk])
        E("dve", "tensor_copy", out=G4.ap[:, 2:3], in_=gcraw.ap[:, 1:2], reads=[gcraw.tk], writes=[G4.tk])
        E("dve", "tensor_copy", out=G4.ap[:, 3:4], in_=psum[:, 7, 1:2], reads=[pt[7]], writes=[G4.tk])
        lv4 = lv.ap.rearrange("p (a b d) -> p a b d", a=2, b=2)
        E("dve", "tensor_tensor", out=lvp.ap.rearrange("p (a d) -> p a d", a=2), in0=lv4[:, :, 0, :], in1=lv4[:, :, 1, :],
          op=ALU.mult, reads=[lv.tk], writes=[lvp.tk])
        E("dve", "reduce_sum", out=lvs.ap, in_=lvp.ap.rearrange("p (a d) -> p a d", a=2), axis=mybir.AxisListType.X,
          reads=[lvp.tk], writes=[lvs.tk])
        E("act", "activation", out=lvs.ap, in_=lvs.ap, func=AF.Exp, reads=[lvs.tk], writes=[lvs.tk])
        E("dve", "tensor_tensor", out=lamcol.ap[:, 0:1], in0=lvs.ap[:, 0:1], in1=lvs.ap[:, 1:2], op=ALU.subtract,
          reads=[lvs.tk], writes=[lamcol.tk])
        E("dve", "tensor_scalar", out=lamcol.ap[:, 0:1], in0=lamcol.ap[:, 0:1], scalar1=LAM_INIT, scalar2=None,
          op0=ALU.add, reads=[lamcol.tk], writes=[lamcol.tk])
        E("dve", "tensor_scalar", out=lamcol.ap[:, 1:2], in0=lamcol.ap[:, 0:1], scalar1=-1.0, scalar2=None,
          op0=ALU.mult, reads=[lamcol.tk], writes=[lamcol.tk])
        E("dve", "tensor_scalar", out=sublnG.ap, in0=sublnG.ap, scalar1=1.0 - LAM_INIT, scalar2=None, op0=ALU.mult,
          reads=[sublnG.tk], writes=[sublnG.tk])
        i = 0
        for l in range(2):
            psT = psum[:, 2 + l, 0:144].rearrange("p (j r) -> p j r", r=3)
            mw = mod_w_d[l].rearrange("(k p) n -> p k n", p=128)
            for c_ in range(12):
                w = wb[i % 2]; m = mb[i % 2]; rc = rowc[i % 2]; bk = i % 2
                DMA("sp", w.ap, mw[:, :, c_ * 512:(c_ + 1) * 512], writes=[w.tk])
                DMA("sp", m.ap, mod_b_d[l, c_ * 512:(c_ + 1) * 512].partition_broadcast(3), writes=[m.tk])
                for k in range(16):
                    MM(psum[0:3, bk, :], scT.ap[:, k, :], w.ap[:, k, :], k == 0, k == 15,
                       reads=[scT.tk, w.tk], writes=[pt[bk]])
                E("dve", "tensor_tensor", out=rc.ap, in0=psum[0:3, bk, :], in1=m.ap, op=ALU.add,
                  reads=[pt[bk], m.tk], writes=[rc.tk])
                for j in range(4):
                    MM(psT[:, c_ * 4 + j, :], rc.ap[:, j * 128:(j + 1) * 128], identf.ap[0:3, 0:3], True, True,
                       reads=[rc.tk, identf.tk], writes=[pt[2 + l]])
                i += 1
            E("dve", "tensor_copy", out=modT.ap[:, l], in_=psT, reads=[pt[2 + l]], writes=[modT.tk])
            for r in range(3):
                E("dve", "scalar_tensor_tensor", out=AT.ap[:, l, :, r], in0=modT.ap[:, l, 16:32, r], scalar=1.0,
                  in1=gcol.ap[:, l, :], op0=ALU.add, op1=ALU.mult, reads=[modT.tk, gcol.tk], writes=[AT.tk])
        if debug:
            DMA("sp", dbg_m, modT.ap.rearrange("p l j r -> p (l j r)"), reads=[modT.tk])

    def stageA(b, l):
        W.reset()
        xt = [W([128, D], F32) for _ in range(2)]
        xn = [W([128, D], BF16) for _ in range(2)]
        junk = W([128, D], BF16)
        E("dve", "memset", ssq.ap, 0.0, writes=[ssq.tk])
        for t in range(NTILE):
            r = 2 if t < 2 else b
            if l == 0:
                src = ctx_d[b, t * 128:(t + 1) * 128, :] if t < 2 else x_d[b, (t - 2) * 128:(t - 1) * 128, :]
                rds = []
            else:
                src = x1_s[t * 128:(t + 1) * 128, :]
                rds = [x1_tk[t]]
            xb = xt[t % 2]; nb2 = xn[t % 2]
            DMA("sp", xb.ap, src, reads=rds, writes=[xb.tk])
            E("act", "activation", out=junk.ap, in_=xb.ap, func=AF.Square, accum_out=ssq.ap[:, t:t + 1],
              reads=[xb.tk], writes=[junk.tk, ssq.tk])
            E("act", "activation", out=tsq.ap[:, t:t + 1], in_=ssq.ap[:, t:t + 1], func=AF.Sqrt,
              bias=cols.ap[:, 0:1], scale=1.0 / D, reads=[ssq.tk], writes=[tsq.tk])
            E("dve", "reciprocal", out=rsq.ap[:, t:t + 1], in_=tsq.ap[:, t:t + 1], reads=[tsq.tk], writes=[rsq.tk])
            E("dve", "tensor_scalar", out=nb2.ap, in0=xb.ap, scalar1=rsq.ap[:, t:t + 1], scalar2=None, op0=ALU.mult,
              reads=[xb.tk], sreads=[rsq.tk], writes=[nb2.tk])
            for half in range(2):
                bk = 6 + half
                pb = psum[:, bk, :].bitcast(BF16)
                for k in range(8):
                    kk = half * 8 + k
                    E("pe", "transpose", pb[:, k * 128:(k + 1) * 128], nb2.ap[:, kk * 128:(kk + 1) * 128], ident.ap,
                      reads=[nb2.tk], writes=[pt[bk]])
                for k in range(8):
                    kk = half * 8 + k
                    o_ = hT.ap[:, kk, t * 128:(t + 1) * 128]
                    i_ = pb[:, k * 128:(k + 1) * 128]
                    if half == 0:
                        E("dve", "tensor_scalar", out=o_, in0=i_, scalar1=AT.ap[:, l, kk, r:r + 1],
                          scalar2=modT.ap[:, l, kk, r:r + 1], op0=ALU.mult, op1=ALU.add,
                          reads=[pt[bk]], writes=[hT_tk[t]])
                    else:
                        E("act", "activation", out=o_, in_=i_, func=AF.Identity, scale=AT.ap[:, l, kk, r:r + 1],
                          bias=modT.ap[:, l, kk, r:r + 1], reads=[pt[bk]], writes=[hT_tk[t]])
        if debug and (l == DBG_L or stop == 'A0'):
            DMA("sp", dbg_h, hT.ap, reads=hT_tk)

    x1_tk = [Tk() for _ in range(NTILE)]
    yT0_tk = [Tk() for _ in range(16)]
    yT1_tk = [Tk() for _ in range(16)]

    def l0B(b):
        W.reset()
        wug = [W([128, 16, 2, 128], BF16) for _ in range(2)]
        gw = [W([128, 4, 128], BF16) for _ in range(2)]
        U = [W([128, 2312], F32) for _ in range(2)]
        XC = W([128, NT], F32); XCB = W([128, NT], BF16)
        RA = W([128, NT], F32); IB = W([128, NT], F32); SQ = W([128, NT], F32)
        HS = W([128, NT], F32); HB = W([128, NT], F32)
        Y = [W([128, NT], BF16) for _ in range(2)]
        sgt = [W([128, 512], F32) for _ in range(2)]
        win = lru_w_in_d[0].rearrange("(k p) c -> p k c", p=128)
        for u_ in U:
            E("dve", "memset", u_.ap, 0.0, writes=[u_.tk])
        segs = [(0, S_CTX, 0), (259, S_LAT, 256)]
        for n in range(16):
            w = wug[n % 2]; g_ = gw[n % 2]; Ub = U[n % 2]; Yb = Y[n % 2]
            DMA("pool", w.ap[:, :, 0, :], win[:, :, n * 128:(n + 1) * 128], writes=[w.tk])
            DMA("pool", w.ap[:, :, 1, :], win[:, :, D + n * 128:D + (n + 1) * 128], writes=[w.tk])
            DMA("pool", g_.ap, lru_gate_w_d[0, :, :, n].rearrange("d g i o -> i (d g) o"), writes=[g_.tk])
            for ci, (lo, hi) in enumerate(CH):
                bk = nb_()
                for k in range(16):
                    MM(psum[:, bk, 0:hi - lo], w.ap[:, k, 0, :], hT.ap[:, k, lo:hi], k == 0, k == 15,
                       reads=[w.tk] + hT_tks(lo, hi), writes=[pt[bk]])
                dst = Ub.ap[:, 2 + lo:2 + hi] if ci == 0 else Ub.ap[:, 259 + 2 + lo - 256:259 + 2 + hi - 256]
                E("act", "activation", out=dst, in_=psum[:, bk, 0:hi - lo], func=AF.Copy,
                  reads=[pt[bk]], writes=[Ub.tk])
            for (uo, L, to) in segs:
                E("act", "activation", out=XC.ap[:, to:to + L], in_=Ub.ap[:, uo:uo + L], func=AF.Identity,
                  scale=cw.ap[:, n, 0:1], bias=cb.ap[:, n:n + 1], reads=[Ub.tk], writes=[XC.tk])
                for j in range(1, 4):
                    E("dve", "scalar_tensor_tensor", out=XC.ap[:, to:to + L], in0=Ub.ap[:, uo + j:uo + j + L],
                      scalar=cw.ap[:, n, j:j + 1], in1=XC.ap[:, to:to + L], op0=ALU.mult, op1=ALU.add,
                      reads=[Ub.tk, XC.tk], writes=[XC.tk])
            E("pool", "tensor_copy", out=XCB.ap, in_=XC.ap, reads=[XC.tk], writes=[XCB.tk])
            for d in range(2):
                for ci, (lo, hi) in enumerate(CH):
                    for g in range(2):
                        bk = nb_()
                        MM(psum[:, bk, 0:hi - lo], g_.ap[:, d * 2 + g, :], XCB.ap[:, lo:hi], True, True,
                           reads=[g_.tk, XCB.tk], writes=[pt[bk]])
                        dstb = RA if g == 0 else IB
                        gi = (d * 2 + g) * 16 + n
                        E("act", "activation", out=dstb.ap[:, lo:hi], in_=psum[:, bk, 0:hi - lo], func=AF.Sigmoid,
                          bias=gb.ap[:, gi:gi + 1], scale=1.0, reads=[pt[bk]], writes=[dstb.tk])
                si = d * 16 + n
                E("act", "activation", out=SQ.ap, in_=RA.ap, func=AF.Exp, scale=sp16.ap[:, si:si + 1],
                  reads=[RA.tk], writes=[SQ.tk])
                E("act", "activation", out=RA.ap, in_=RA.ap, func=AF.Exp, scale=sp8.ap[:, si:si + 1],
                  reads=[RA.tk], writes=[RA.tk])
                E("act", "activation", out=SQ.ap, in_=SQ.ap, func=AF.Sqrt, scale=-1.0, bias=cols.ap[:, 1:2],
                  reads=[SQ.tk], writes=[SQ.tk])
                E("dve", "tensor_tensor", out=IB.ap, in0=IB.ap, in1=SQ.ap, op=ALU.mult,
                  reads=[IB.tk, SQ.tk], writes=[IB.tk])
                E("dve", "tensor_tensor", out=IB.ap, in0=IB.ap, in1=XC.ap, op=ALU.mult,
                  reads=[IB.tk, XC.tk], writes=[IB.tk])
                if d == 0:
                    E("dve", "tensor_tensor_scan", out=HS.ap, data0=RA.ap, data1=IB.ap, initial=0.0,
                      op0=ALU.mult, op1=ALU.add, reads=[RA.tk, IB.tk], writes=[HS.tk])
                else:
                    E("dve", "tensor_tensor_scan", out=HB.ap[:, 0:256][:, ::-1], data0=RA.ap[:, 0:256][:, ::-1],
                      data1=IB.ap[:, 0:256][:, ::-1], initial=0.0, op0=ALU.mult, op1=ALU.add,
                      reads=[RA.tk, IB.tk], writes=[HB.tk])
                    E("dve", "tensor_tensor_scan", out=HB.ap[:, 256:NT][:, ::-1], data0=RA.ap[:, 256:NT][:, ::-1],
                      data1=IB.ap[:, 256:NT][:, ::-1], initial=HB.ap[:, 0:1], op0=ALU.mult, op1=ALU.add,
                      reads=[RA.tk, IB.tk], sreads=[HB.tk], writes=[HB.tk])
            E("pool", "tensor_tensor", out=HS.ap, in0=HS.ap, in1=HB.ap, op=ALU.add,
              reads=[HS.tk, HB.tk], writes=[HS.tk])
            for ci, (lo, hi) in enumerate(CH):
                bk = nb_()
                for k in range(16):
                    MM(psum[:, bk, 0:hi - lo], w.ap[:, k, 1, :], hT.ap[:, k, lo:hi], k == 0, k == 15,
                       reads=[w.tk] + hT_tks(lo, hi), writes=[pt[bk]])
                sg = sgt[ci % 2]
                E("act", "activation", out=sg.ap[:, 0:hi - lo], in_=psum[:, bk, 0:hi - lo], func=AF.Silu,
                  reads=[pt[bk]], writes=[sg.tk])
                E("dve", "tensor_tensor", out=Yb.ap[:, lo:hi], in0=HS.ap[:, lo:hi], in1=sg.ap[:, 0:hi - lo],
                  op=ALU.mult, reads=[HS.tk, sg.tk], writes=[Yb.tk])
            DMA("sp", yT0_s[n], Yb.ap, reads=[Yb.tk], writes=[yT0_tk[n]])

    def stageC(b, l):
        W.reset()
        wo = Hh
        wo_ap = arena[:, PERS_BYTES // 4: PERS_BYTES // 4 + 16 * D // 2].bitcast(BF16).rearrange("p (n d) -> p n d", d=D)
        wo_tk = Tk()
        wsrc = (lru_w_out_d if l == 0 else att_w_out_d)[0].rearrange("(n p) d -> p n d", p=128)
        for q4 in range(4):
            DMA("pool", wo_ap[:, q4 * 4:(q4 + 1) * 4, :], wsrc[:, q4 * 4:(q4 + 1) * 4, :], writes=[wo_tk])
        yt = [W([128, 16, 512], BF16) for _ in range(2)]
        xt = [W([128, D], F32) for _ in range(2)]
        xo = [W([128, D], F32) for _ in range(2)]
        dg = [W([128, 128], F32) for _ in range(2)]

        def build_gate(r):
            for j in range(16):
                dj = dg[j % 2]
                E("dve", "tensor_scalar", out=dj.ap, in0=identf.ap, scalar1=modT.ap[:, l, 32 + j, r:r + 1], scalar2=None,
                  op0=ALU.mult, reads=[identf.tk], writes=[dj.tk])
                MM(psum[:, 7, (j % 4) * 128:(j % 4 + 1) * 128], onesf.ap, dj.ap, True, True,
                   reads=[onesf.tk, dj.tk], writes=[pt[7]])
                if j % 4 == 3:
                    E("act", "activation", out=gate_bc.ap[:, (j // 4) * 512:(j // 4 + 1) * 512], in_=psum[:, 7, :],
                      func=AF.Copy, reads=[pt[7]], writes=[gate_bc.tk])

        chunks = CH if l == 0 else CH[1:]
        ti = 0
        for ci, (lo, hi) in enumerate(chunks):
            if l == 0 and ci == 0:
                build_gate(2)
            elif (l == 0 and ci == 1) or (l == 1 and ci == 0):
                build_gate(b)
            ytb = yt[ci % 2]
            if l == 0:
                DMA("sp", ytb.ap[:, :, 0:hi - lo], yT0_s[:, :, lo:hi].rearrange("n p t -> p n t"),
                    reads=yT0_tk, writes=[ytb.tk])
            else:
                DMA("sp", ytb.ap[:, :, 0:hi - lo], yT1_s[:, :, lo - 256:hi - 256].rearrange("n p t -> p n t"),
                    reads=yT1_tk, writes=[ytb.tk])
            for sub in range((hi - lo) // 128):
                t = lo // 128 + sub
                xb = xt[ti % 2]; ob = xo[ti % 2]
                if l == 0:
                    src = ctx_d[b, t * 128:(t + 1) * 128, :] if t < 2 else x_d[b, (t - 2) * 128:(t - 1) * 128, :]
                    DMA("sp", xb.ap, src, writes=[xb.tk])
                else:
                    DMA("sp", xb.ap, x1_s[t * 128:(t + 1) * 128, :], reads=[x1_tk[t]], writes=[xb.tk])
                for dc in range(4):
                    bk = nb_()
                    for n in range(16):
                        MM(psum[:, bk, :], ytb.ap[:, n, sub * 128:(sub + 1) * 128], wo_ap[:, n, dc * 512:(dc + 1) * 512],
                           n == 0, n == 15, reads=[ytb.tk, wo_tk], writes=[pt[bk]])
                    E("dve", "tensor_tensor", out=ob.ap[:, dc * 512:(dc + 1) * 512], in0=psum[:, bk, :],
                      in1=gate_bc.ap[:, dc * 512:(dc + 1) * 512], op=ALU.mult, reads=[pt[bk], gate_bc.tk], writes=[ob.tk])
                    E("pool", "tensor_tensor", out=ob.ap[:, dc * 512:(dc + 1) * 512], in0=ob.ap[:, dc * 512:(dc + 1) * 512],
                      in1=xb.ap[:, dc * 512:(dc + 1) * 512], op=ALU.add, reads=[ob.tk, xb.tk], writes=[ob.tk])
                if l == 0:
                    DMA("pool", x1_s[t * 128:(t + 1) * 128, :], ob.ap, reads=[ob.tk], writes=[x1_tk[t]])
                else:
                    DMA("pool", out_d[b, (t - 2) * 128:(t - 1) * 128, :], ob.ap, reads=[ob.tk])
                ti += 1

    def l1B(b):
        W.reset()
        Wt = [W([128, 16, 4, 128], BF16) for _ in range(2)]
        TC = W([128, S_LAT], F32); TS = W([128, S_LAT], F32)
        QT = W([128, S_LAT], BF16); KT = W([128, NT], BF16)
        VA = W([128, NTILE, 132], BF16)
        SG = W([128, 16, 128], F32)
        PT = [W([128, 2, 512], BF16) for _ in range(3)]
        YT = [W([128, S_LAT], BF16) for _ in range(2)]
        sqb = [W([128, 512], BF16) for _ in range(2)]
        qbb = [W([128, 512], BF16) for _ in range(2)]
        rst = [W([128, 512], F32) for _ in range(2)]
        t1b = [W([128, 512], F32) for _ in range(2)]
        t2b = [W([128, 512], F32) for _ in range(2)]
        ob_ = [W([128, 128], F32) for _ in range(4)]
        yb_ = [W([128, 128], BF16) for _ in range(4)]
        jk = W([128, 128], BF16)
        rr8 = W([128, 8], F32)
        Osb = [W([128, 3, 387], F32) for _ in range(2)]
        DMA("sp", TC.ap, k_tc_d, writes=[TC.tk])
        DMA("sp", TS.ap, k_ts_d, writes=[TS.tk])
        E("dve", "memset", VA.ap[:, :, 128:129], 1.0, writes=[VA.tk])
        E("dve", "memset", ssq.ap, 0.0, writes=[ssq.tk])
        win = att_w_in_d[0].rearrange("(k p) c -> p k c", p=128)
        ptO = Tk(excl=True)
        cnt = 0
        pcnt = 0
        ecnt = 0
        def load_w(h_):
            w_ = Wt[h_ % 2]
            for j_ in range(4):
                DMA("pool", w_.ap[:, :, j_, :], win[:, :, j_ * D + h_ * 128:j_ * D + (h_ + 1) * 128], writes=[w_.tk])

        load_w(0)
        for h in range(nheads):
            w = Wt[h % 2]
            if h + 1 < nheads:
                load_w(h + 1)
            for j, chunks in ((0, CH[1:]), (1, CH)):
                for (lo, hi) in chunks:
                    n_ = hi - lo
                    sq = sqb[cnt % 2]; qb = qbb[cnt % 2]; rs = rst[cnt % 2]; t1 = t1b[cnt % 2]; t2 = t2b[cnt % 2]
                    cnt += 1
                    bk = nb_(4)
                    for k in range(16):
                        MM(psum[:, bk, 0:n_], w.ap[:, k, j, :], hT.ap[:, k, lo:hi], k == 0, k == 15,
                           reads=[w.tk] + hT_tks(lo, hi), writes=[pt[bk]])
                    if b1step <= -3:
                        continue
                    E("dve", "tensor_copy", out=qb.ap[:, 0:n_], in_=psum[:, bk, 0:n_], reads=[pt[bk]], writes=[qb.tk])
                    E("pool", "tensor_tensor", out=sq.ap[:, 0:n_], in0=qb.ap[:, 0:n_], in1=qb.ap[:, 0:n_], op=ALU.mult,
                      reads=[qb.tk], writes=[sq.tk])
                    if qk < 2:
                        continue
                    bk2 = nb_(4)
                    MM(psum[:, bk2, 0:n_], bones.ap, sq.ap[:, 0:n_], True, True, reads=[bones.tk, sq.tk], writes=[pt[bk2]])
                    if qk < 3:
                        continue
                    E("act", "activation", out=rs.ap[:, 0:n_], in_=psum[:, bk2, 0:n_], func=AF.Sqrt,
                      bias=cols.ap[:, 0:1], scale=1.0 / 64, reads=[pt[bk2]], writes=[rs.tk])
                    if qk < 4:
                        continue
                    rs0 = rs
                    rs = t2
                    E("dve", "reciprocal", out=rs.ap[:, 0:n_], in_=rs0.ap[:, 0:n_], reads=[rs0.tk], writes=[rs.tk])
                    if qk < 5:
                        continue
                    if lo >= 256:
                        tl = lo - 256
                        bk3 = nb_(4)
                        MM(psum[:, bk3, 0:n_], pswap.ap, qb.ap[:, 0:n_], True, True, reads=[pswap.tk, qb.tk], writes=[pt[bk3]])
                        if qk < 6:
                            continue
                        E("dve", "scalar_tensor_tensor", out=t1.ap[:, 0:n_], in0=qb.ap[:, 0:n_],
                          scalar=G4.ap[:, 2 * j:2 * j + 1], in1=TC.ap[:, tl:tl + n_], op0=ALU.mult, op1=ALU.mult,
                          reads=[qb.tk, TC.tk], writes=[t1.tk])
                        if qk < 7:
                            continue
                        E("dve", "scalar_tensor_tensor", out=rs0.ap[:, 0:n_], in0=psum[:, bk3, 0:n_],
                          scalar=G4.ap[:, 2 * j + 1:2 * j + 2], in1=TS.ap[:, tl:tl + n_], op0=ALU.mult, op1=ALU.mult,
                          reads=[pt[bk3], TS.tk], writes=[rs0.tk])
                        if qk < 8:
                            continue
                        E("pool", "tensor_tensor", out=t1.ap[:, 0:n_], in0=t1.ap[:, 0:n_], in1=rs0.ap[:, 0:n_], op=ALU.add,
                          reads=[t1.tk, rs0.tk], writes=[t1.tk])
                        dst = QT.ap[:, tl:tl + n_] if j == 0 else KT.ap[:, lo:hi]
                        dtk = QT.tk if j == 0 else KT.tk
                        E("dve", "tensor_tensor", out=dst, in0=t1.ap[:, 0:n_], in1=rs.ap[:, 0:n_], op=ALU.mult,
                          reads=[t1.tk, rs.tk], writes=[dtk])
                    elif qk >= 9:
                        E("dve", "tensor_scalar", out=t1.ap[:, 0:n_], in0=qb.ap[:, 0:n_], scalar1=G4.ap[:, 2:3], scalar2=None,
                          op0=ALU.mult, reads=[qb.tk], writes=[t1.tk])
                        E("dve", "tensor_tensor", out=KT.ap[:, lo:hi], in0=t1.ap[:, 0:n_], in1=rs.ap[:, 0:n_], op=ALU.mult,
                          reads=[t1.tk, rs.tk], writes=[KT.tk])
            if b1step < 1:
                continue
            for t in range(NTILE):
                bk = nb_(4)
                ncol = 128 if t < 2 else 256
                for k in range(16):
                    rhs = w.ap[:, k, 2, :] if t < 2 else w.ap[:, k, 2:4, :].rearrange("p a b -> p (a b)")
                    MM(psum[:, bk, 0:ncol], hT.ap[:, k, t * 128:(t + 1) * 128], rhs, k == 0, k == 15,
                       reads=[w.tk, hT_tk[t]], writes=[pt[bk]])
                if vg < 2:
                    continue
                E("dve", "tensor_copy", out=VA.ap[:, t, 0:128], in_=psum[:, bk, 0:128], reads=[pt[bk]], writes=[VA.tk])
                if vg < 3:
                    continue
                if t >= 2:
                    E("act", "activation", out=SG.ap[:, t - 2, :], in_=psum[:, bk, 128:256], func=AF.Silu,
                      reads=[pt[bk], VA.tk], writes=[SG.tk])
            if b1step < 2:
                continue
            Yh = YT[h % 2]
            iters = [(qc, kb) for qc in range(4) for kb in range(NTILE)]

            def scores(i_):
                qc_, kb_ = iters[i_]
                sb_ = i_ % 2
                for c in range(2):
                    MM(psum[:, 2 * sb_ + c, :], KT.ap[c * 64:(c + 1) * 64, kb_ * 128:(kb_ + 1) * 128],
                       QT.ap[c * 64:(c + 1) * 64, qc_ * 512:qc_ * 512 + 512], True, True, reads=[KT.tk, QT.tk],
                       writes=[pt[2 * sb_], pt[2 * sb_ + 1]], tile_position=(c * 64, 0))

            scores(0)
            for i_, (qc, kb) in enumerate(iters):
                if i_ + 1 < len(iters):
                    scores(i_ + 1)
                sb_ = i_ % 2
                Pb = PT[pcnt % 3]
                pcnt += 1
                E("act", "activation", out=Pb.ap, in_=psum[:, 2 * sb_:2 * sb_ + 2, :], func=AF.Exp,
                  reads=[pt[2 * sb_], pt[2 * sb_ + 1]], writes=[Pb.tk])
                for c in range(2):
                    for qs in range(4):
                        idx = c * 4 + qs
                        MM(psum[:, 4 + idx // 3, (idx % 3) * 129:(idx % 3 + 1) * 129],
                           Pb.ap[:, c, qs * 128:(qs + 1) * 128], VA.ap[:, kb, 0:129],
                           (kb == 0 and idx % 3 == 0), kb == NTILE - 1, reads=[Pb.tk, VA.tk],
                           writes=[ptO], skip_group_check=True)
                if kb != NTILE - 1:
                    continue
                Ob = Osb[ecnt % 2]
                ecnt += 1
                E("dve", "tensor_copy", out=Ob.ap, in_=psum[:, 4:7, 0:387], reads=[ptO], writes=[Ob.tk])
                Of = Ob.ap.rearrange("p a b -> p (a b)")
                Or = Of[:, 0:8 * 129].rearrange("p (i n) -> p i n", n=129)
                E("dve", "reciprocal", out=rr8.ap, in_=Or[:, :, 128], reads=[Ob.tk], writes=[rr8.tk])
                E("dve", "tensor_scalar", out=rr8.ap[:, 4:8], in0=rr8.ap[:, 4:8], scalar1=lamcol.ap[:, 1:2], scalar2=None,
                  op0=ALU.mult, reads=[rr8.tk], writes=[rr8.tk])
                c0 = 16
                for qs in range(4):
                    o_ = ob_[qs]
                    E("dve", "tensor_scalar", out=o_.ap, in0=Or[:, qs, 0:128], scalar1=rr8.ap[:, qs:qs + 1], scalar2=None,
                      op0=ALU.mult, reads=[Ob.tk, rr8.tk], writes=[o_.tk])
                    E("dve", "scalar_tensor_tensor", out=o_.ap, in0=Or[:, 4 + qs, 0:128], scalar=rr8.ap[:, 4 + qs:5 + qs],
                      in1=o_.ap, op0=ALU.mult, op1=ALU.add, reads=[Ob.tk, o_.tk, rr8.tk], writes=[o_.tk])
                for qs in range(4):
                    E("act", "activation", out=jk.ap, in_=ob_[qs].ap, func=AF.Square, accum_out=ssq.ap[:, c0 + qs:c0 + qs + 1],
                      reads=[ob_[qs].tk], writes=[jk.tk, ssq.tk])
                E("act", "activation", out=tsq.ap[:, c0:c0 + 4], in_=ssq.ap[:, c0:c0 + 4], func=AF.Sqrt,
                  bias=cols.ap[:, 0:1], scale=1.0 / 128, reads=[ssq.tk], writes=[tsq.tk])
                E("dve", "reciprocal", out=rsq.ap[:, c0:c0 + 4], in_=tsq.ap[:, c0:c0 + 4], reads=[tsq.tk], writes=[rsq.tk])
                E("dve", "memset", ssq.ap[:, c0:c0 + 4], 0.0, reads=[tsq.tk], writes=[ssq.tk])
                for qs in range(4):
                    ti = qc * 4 + qs
                    o_ = ob_[qs]; y_ = yb_[qs]
                    E("dve", "scalar_tensor_tensor", out=o_.ap, in0=o_.ap, scalar=rsq.ap[:, c0 + qs:c0 + qs + 1], in1=sublnG.ap,
                      op0=ALU.mult, op1=ALU.mult, reads=[o_.tk, rsq.tk], writes=[o_.tk])
                    E("pool", "tensor_tensor", out=y_.ap, in0=o_.ap, in1=SG.ap[:, ti, :], op=ALU.mult,
                      reads=[o_.tk, SG.tk], writes=[y_.tk])
                    pb = psum[:, 7, :].bitcast(BF16)
                    E("pe", "transpose", pb[:, qs * 128:(qs + 1) * 128], y_.ap, ident.ap, reads=[y_.tk], writes=[pt[7]])
                pb = psum[:, 7, :].bitcast(BF16)
                E("act", "activation", out=Yh.ap[:, qc * 512:(qc + 1) * 512], in_=pb[:, 0:512],
                  func=AF.Copy, reads=[pt[7]], writes=[Yh.tk])
            DMA("sp", yT1_s[h], Yh.ap, reads=[Yh.tk], writes=[yT1_tk[h]])
            if debug and h == 0:
                DMA("sp", dbg_q, QT.ap, reads=[QT.tk])
                DMA("sp", dbg_k, KT.ap, reads=[KT.tk])
                DMA("sp", dbg_v, VA.ap, reads=[VA.tk])
                DMA("sp", dbg_sg, SG.ap, reads=[SG.tk])
                DMA("sp", dbg_y, Yh.ap, reads=[Yh.tk])
                DMA("sp", dbg_o, ob_[3].ap, reads=[ob_[3].tk])
                DMA("sp", dbg_c[:, 0:32], ssq.ap, reads=[ssq.tk])
                DMA("sp", dbg_c[:, 32:64], tsq.ap, reads=[tsq.tk])
                DMA("sp", dbg_c[:, 64:96], rsq.ap, reads=[rsq.tk])
                DMA("sp", dbg_c[:, 96:100], G4.ap)
                DMA("sp", dbg_c[:, 100:104], lamcol.ap)

    DBG_L = 1
    phase0()
    BAR()
    steps = [("A0", lambda b: stageA(b, 0)), ("B0", l0B), ("C0", lambda b: stageC(b, 0)),
             ("A1", lambda b: stageA(b, 1)), ("B1", l1B), ("C1", lambda b: stageC(b, 1))]
    for b in range(nb):
        if stop == "P0":
            break
        for nm, fn in steps:
            fn(b); BAR()
            if stop == nm:
                break
    S.emit()
    st.close()
    nc._sched_stats = (S.maxcount, {k: len(v) for k, v in S.dma_ops.items()})
    return nc


def _consts():
    s = np.arange(S_LAT)
    pos = np.stack([s // 64, s % 64], 0).astype(np.float32)
    inv = (10000.0 ** (-np.arange(16, dtype=np.float32) / 16)).astype(np.float32)
    p = np.arange(128)
    a = (p // 32) % 2; j = (p // 16) % 2; f = p % 16
    ang = pos[a][:, :] * inv[f][:, None]
    tc = np.cos(ang).astype(np.float32)
    ts = (np.sin(ang) * np.where(j == 0, -1.0, 1.0)[:, None]).astype(np.float32)
    ident = np.eye(128, dtype=np.float32)
    pswap = np.zeros((128, 128), np.float32); pswap[p ^ 16, p] = 1.0
    bones = np.zeros((128, 128), np.float32); bones[:64, :64] = 1.0; bones[64:, 64:] = 1.0
    return tc, ts, np.stack([ident, pswap, bones], 0)


_WNAMES = ["mod_w", "mod_b", "norm_g", "lru_w_in", "lru_conv_w", "lru_conv_b", "lru_gate_w", "lru_gate_b",
           "lru_lambda", "lru_w_out", "att_w_in", "att_q_norm", "att_k_norm", "att_lambda", "att_subln", "att_w_out"]


def make_in_maps(inputs, cores, nb=BPC):
    tc, ts, mats = _consts()
    shared = {k: np.ascontiguousarray(np.asarray(inputs[k], dtype=np.float32)) for k in _WNAMES}
    shared.update(k_tc=tc, k_ts=ts, k_mats=mats)
    x = np.asarray(inputs["x"]); ctx = np.asarray(inputs["ctx"]); c = np.asarray(inputs["c"]); c_ctx = np.asarray(inputs["c_ctx"])
    maps = []
    for i in cores:
        m = dict(shared)
        m["x"] = np.ascontiguousarray(x[i * BPC:i * BPC + nb])
        m["ctx"] = np.ascontiguousarray(ctx[i * BPC:i * BPC + nb])
        cc = np.zeros((3, D), np.float32)
        cc[0:nb] = c[i * BPC:i * BPC + nb]
        cc[2] = c_ctx
        m["cc"] = cc
        maps.append(m)
    return maps


def kernel(**inputs):
    nc = build_program()
    in_maps = make_in_maps(inputs, list(range(NCORES)))
    res = run_bass_kernel_spmd(nc, in_maps, core_ids=list(range(NCORES)))
    return np.concatenate([np.asarray(r["out"]) for r in res.results], axis=0).astype(np.float32)
```

```python
import math
import contextlib
import numpy as np
import concourse.bass as bass
import concourse.mybir as mybir
from concourse.bass_utils import run_bass_kernel_spmd

F32 = mybir.dt.float32
BF16 = mybir.dt.bfloat16
ALU = mybir.AluOpType
AF = mybir.ActivationFunctionType

NCORES = 8
BPC = 2
D = 2048
S_LAT = 2048
S_CTX = 256
NT = S_LAT + S_CTX
NTILE = NT // 128
CH = [(0, 256), (256, 768), (768, 1280), (1280, 1792), (1792, 2304)]
LAM_INIT = 0.8 - 0.6 * math.exp(-0.3 * 1)
EPS = 1e-6
ARENA_BYTES = 211600
PERS_BYTES = 18432
H_BYTES = 16 * NT * 2


_ALL_TK = []


class Tk:
    __slots__ = ("w", "r", "excl", "small")

    def __init__(self, excl=False, small=False):
        self.w = None
        self.r = {}
        self.excl = excl
        self.small = small
        _ALL_TK.append(self)


class Buf:
    __slots__ = ("ap", "tk")

    def __init__(self, ap, small=False):
        self.ap = ap
        self.tk = Tk(small=small)


class Op:
    __slots__ = ("eng", "fn", "deps", "signal", "count", "dma", "dsem", "dval", "hard", "phase")


class Sched:
    COMPUTE = ("pe", "act", "dve", "pool")
    QUEUES = ("sp", "act", "pool")

    def __init__(self, nc, ndma_sems=8):
        self.nc = nc
        self.names = ("pe", "act", "dve", "pool", "sp")
        self.ops = {k: [] for k in self.names}
        self.ndma = {k: 0 for k in self.names}
        self.ndma_sems = ndma_sems
        self.dma_ops = {k: [] for k in self.names}
        self.bar_deps = {k: [] for k in self.names}
        self.phase = 0

    def op(self, eng, fn, reads=(), writes=(), dma=False, sreads=()):
        o = Op()
        o.eng = eng; o.fn = fn; o.deps = []; o.signal = False; o.count = None; o.dma = dma; o.hard = None
        o.phase = self.phase
        if self.bar_deps[eng]:
            o.deps.extend(self.bar_deps[eng])
            self.bar_deps[eng] = []
        if dma:
            i = self.ndma[eng]; self.ndma[eng] += 1
            o.dsem = i % self.ndma_sems
            o.dval = 16 * (i // self.ndma_sems + 1)
            if i >= self.ndma_sems:
                o.deps.append(self.dma_ops[eng][i - self.ndma_sems])
            self.dma_ops[eng].append(o)
        rkey = ("d", id(o)) if dma else eng
        for t in sreads:
            if t.w is not None:
                o.deps.append(t.w)
                if t.w.eng == eng and not t.w.dma:
                    if o.hard is None:
                        o.hard = set()
                    o.hard.add(id(t.w))
                    t.w.signal = True
            t.r[rkey] = o
        for t in reads:
            if t.w is not None:
                o.deps.append(t.w)
                if t.small and t.w.eng == eng and not t.w.dma:
                    if o.hard is None:
                        o.hard = set()
                    o.hard.add(id(t.w))
                    t.w.signal = True
            if t.excl:
                for kk, ro in t.r.items():
                    if kk != rkey:
                        o.deps.append(ro)
            t.r[rkey] = o
        for t in writes:
            if t.w is not None:
                o.deps.append(t.w)
            o.deps.extend(t.r.values())
            t.w = o
            t.r = {}
        for d in o.deps:
            if not d.dma and d.eng != eng:
                d.signal = True
        self.ops[eng].append(o)
        return o

    def barrier(self):
        new = []
        for k in self.COMPUTE:
            last = None
            for o in reversed(self.ops[k]):
                if not o.dma:
                    last = o
                    break
            if last is not None and last.phase == self.phase:
                last.signal = True
                new.append(last)
        for k in self.QUEUES:
            n = len(self.dma_ops[k])
            for o in self.dma_ops[k][max(0, n - self.ndma_sems):]:
                new.append(o)
        for k in self.names:
            self.bar_deps[k] = list(new)
        for t in _ALL_TK:
            t.w = None
            t.r = {}
        self.phase += 1

    def emit(self):
        nc = self.nc
        with contextlib.ExitStack() as st:
            csem = [{k: st.enter_context(nc.semaphore(f"c{s_}_{k}")) for k in self.COMPUTE} for s_ in range(3)]
            dsem = {k: [st.enter_context(nc.semaphore(f"d_{k}{j}")) for j in range(self.ndma_sems)]
                    for k in self.QUEUES}
            block = st.enter_context(nc.Block())
            self.maxcount = {}
            for k in self.COMPUTE:
                c = 0; ph = -1; mx = 0
                for o in self.ops[k]:
                    if o.phase != ph:
                        ph = o.phase; c = 0
                    if o.signal and not o.dma:
                        c += 1
                        o.count = c
                        mx = max(mx, c)
                self.maxcount[k] = (mx, len(self.ops[k]))

            def run(k, e):
                waited = {}
                ph = 0
                for o in self.ops[k]:
                    need = {}
                    for d in o.deps:
                        if d is o:
                            continue
                        if d.dma:
                            key = ("d", d.eng, d.dsem); sem = dsem[d.eng][d.dsem]; val = d.dval
                        else:
                            if d.eng == k and (o.hard is None or id(d) not in o.hard):
                                continue
                            key = ("c", d.eng, d.phase); sem = csem[d.phase % 3][d.eng]; val = d.count
                        if waited.get(key, 0) >= val:
                            continue
                        if key not in need or need[key][1] < val:
                            need[key] = (sem, val)
                    for key, (sem, val) in need.items():
                        e.wait_ge(sem, val)
                        waited[key] = val
                    if k == "pool" and o.phase != ph:
                        assert o.phase == ph + 1, "pool needs an op in every phase"
                        ph = o.phase
                        if ph >= 2:
                            for kk in self.COMPUTE:
                                e.sem_clear(csem[(ph + 1) % 3][kk])
                    ins = o.fn(e)
                    if o.dma:
                        ins.then_inc(dsem[k][o.dsem], 16)
                    elif o.signal:
                        ins.then_inc(csem[o.phase % 3][k], 1)
                if k in self.QUEUES:
                    n = len(self.dma_ops[k])
                    for d in self.dma_ops[k][max(0, n - self.ndma_sems):]:
                        if waited.get(("d", k, d.dsem), 0) < d.dval:
                            e.wait_ge(dsem[k][d.dsem], d.dval)
                            waited[("d", k, d.dsem)] = d.dval

            block.sync(lambda e: run("sp", e))
            block.scalar(lambda e: run("act", e))
            block.vector(lambda e: run("dve", e))
            block.gpsimd(lambda e: run("pool", e))
            block.tensor(lambda e: run("pe", e))


def build_program(nb=BPC, debug=False, stop=None, nheads=16, b1step=9, qk=9, vg=9):
    nc = bass.Bass("TRN2", target_bir_lowering=False)
    del _ALL_TK[:]

    def din(name, shape):
        return nc.dram_tensor(name, list(shape), F32, kind="ExternalInput").ap()

    x_d = din("x", [nb, S_LAT, D])
    ctx_d = din("ctx", [nb, S_CTX, D])
    cc_d = din("cc", [3, D])
    mod_w_d = din("mod_w", [2, D, 3 * D])
    mod_b_d = din("mod_b", [2, 3 * D])
    norm_g_d = din("norm_g", [2, D])
    lru_w_in_d = din("lru_w_in", [1, D, 2 * D])
    lru_conv_w_d = din("lru_conv_w", [1, 4, D])
    lru_conv_b_d = din("lru_conv_b", [1, D])
    lru_gate_w_d = din("lru_gate_w", [1, 2, 2, 16, 128, 128])
    lru_gate_b_d = din("lru_gate_b", [1, 2, 2, 16, 128])
    lru_lambda_d = din("lru_lambda", [1, 2, D])
    lru_w_out_d = din("lru_w_out", [1, D, D])
    att_w_in_d = din("att_w_in", [1, D, 4 * D])
    att_q_norm_d = din("att_q_norm", [1, 64])
    att_k_norm_d = din("att_k_norm", [1, 64])
    att_lambda_d = din("att_lambda", [1, 4, 64])
    att_subln_d = din("att_subln", [1, 128])
    att_w_out_d = din("att_w_out", [1, D, D])
    k_tc_d = din("k_tc", [128, S_LAT])
    k_ts_d = din("k_ts", [128, S_LAT])
    k_mats_d = din("k_mats", [3, 128, 128])
    out_d = nc.dram_tensor("out", [nb, S_LAT, D], F32, kind="ExternalOutput").ap()
    x1_s = nc.dram_tensor("x1_s", [NT, D], F32, kind=("ExternalOutput" if debug else "Internal")).ap()
    yT0_s = nc.dram_tensor("yT0_s", [16, 128, NT], BF16, kind=("ExternalOutput" if debug else "Internal")).ap()
    yT1_s = nc.dram_tensor("yT1_s", [16, 128, S_LAT], BF16, kind=("ExternalOutput" if debug else "Internal")).ap()
    if debug:
        dbg_h = nc.dram_tensor("dbg_h", [128, 16, NT], BF16, kind="ExternalOutput").ap()
        dbg_m = nc.dram_tensor("dbg_m", [128, 2 * 48 * 3], F32, kind="ExternalOutput").ap()
        dbg_q = nc.dram_tensor("dbg_q", [128, S_LAT], BF16, kind="ExternalOutput").ap()
        dbg_k = nc.dram_tensor("dbg_k", [128, NT], BF16, kind="ExternalOutput").ap()
        dbg_v = nc.dram_tensor("dbg_v", [128, NTILE, 132], BF16, kind="ExternalOutput").ap()
        dbg_sg = nc.dram_tensor("dbg_sg", [128, 16, 128], F32, kind="ExternalOutput").ap()
        dbg_y = nc.dram_tensor("dbg_y", [128, S_LAT], BF16, kind="ExternalOutput").ap()
        dbg_o = nc.dram_tensor("dbg_o", [128, 128], F32, kind="ExternalOutput").ap()
        dbg_c = nc.dram_tensor("dbg_c", [128, 32 * 3 + 4 + 4], F32, kind="ExternalOutput").ap()

    S = Sched(nc)
    st = contextlib.ExitStack()
    arena = st.enter_context(nc.sbuf_tensor("arena", [128, ARENA_BYTES // 4], F32))
    psum = st.enter_context(nc.psum_tensor("psum", [128, 8, 512], F32))
    pt = [Tk(excl=True) for _ in range(8)]

    class Alloc:
        def __init__(self, lo, hi):
            self.lo = lo; self.hi = hi; self.cur = lo

        def reset(self):
            self.cur = self.lo

        def __call__(self, shape, dt):
            n = int(np.prod(shape[1:]))
            nbytes = n * (4 if dt == F32 else 2)
            nbytes = (nbytes + 31) // 32 * 32
            off = self.cur
            self.cur += nbytes
            assert self.cur <= self.hi, (shape, self.cur, self.hi)
            if dt == F32:
                v = arena[:, off // 4: off // 4 + n]
            else:
                v = arena[:, off // 4: off // 4 + (n + 1) // 2].bitcast(BF16)[:, 0:n]
            if len(shape) == 3:
                v = v.rearrange("p (a b) -> p a b", b=shape[2])
            elif len(shape) == 4:
                v = v.rearrange("p (a b c) -> p a b c", b=shape[2], c=shape[3])
            if shape[0] != 128:
                v = v[0:shape[0]]
            return Buf(v, small=(n <= 256))

    P = Alloc(0, PERS_BYTES)
    Hh = Alloc(PERS_BYTES, PERS_BYTES + H_BYTES)
    W = Alloc(PERS_BYTES + H_BYTES, ARENA_BYTES)

    def E(eng, meth, *args, reads=(), writes=(), sreads=(), **kw):
        return S.op(eng, lambda e: getattr(e, meth)(*args, **kw), reads=reads, writes=writes, sreads=sreads)

    def DMA(q, out, in_, reads=(), writes=(), **kw):
        return S.op(q, lambda e: e.dma_start(out=out, in_=in_, **kw), reads=reads, writes=writes, dma=True)

    def MM(out, lhsT, rhs, start, stop, reads, writes, **kw):
        return S.op("pe", lambda e: e.matmul(out, lhsT=lhsT, rhs=rhs, start=start, stop=stop, **kw),
                    reads=reads, writes=writes)

    bank_ctr = [0]

    def BAR():
        S.barrier()
        E("pool", "memset", cols.ap[:, 7:8], 0.0)

    def nb_(mod=6):
        b_ = bank_ctr[0] % mod
        bank_ctr[0] += 1
        return b_

    identf = P([128, 128], F32); pswapf = P([128, 128], F32); bonesf = P([128, 128], F32); onesf = P([128, 128], F32)
    ident = P([128, 128], BF16); pswap = P([128, 128], BF16); bones = P([128, 128], BF16)
    modT = P([128, 2, 48, 3], F32)
    AT = P([128, 2, 16, 3], F32)
    gcol = P([128, 2, 16], F32)
    cw = P([128, 16, 4], F32); cb = P([128, 16], F32); gb = P([128, 64], F32)
    lamc = P([128, 32], F32); sp8 = P([128, 32], F32); sp16 = P([128, 32], F32)
    cols = P([128, 8], F32)
    G4 = P([128, 4], F32); gcraw = P([128, 2], F32)
    lamcol = P([128, 4], F32)
    sublnG = P([128, 128], F32)
    lv = P([128, 256], F32); lvp = P([128, 128], F32); lvs = P([128, 2], F32)
    scT = P([128, 16, 3], F32)
    ssq = P([128, 32], F32); rsq = P([128, 32], F32); tsq = P([128, 32], F32)
    gate_bc = P([128, D], F32)
    hT = Hh([128, 16, NT], BF16)
    hT_tk = [Tk() for _ in range(NTILE)]
    CONST = Tk()

    def hT_tks(lo, hi):
        return hT_tk[lo // 128: hi // 128]

    def phase0():
        W.reset()
        wb = [W([128, 16, 512], F32) for _ in range(2)]
        mb = [W([3, 512], F32) for _ in range(2)]
        rowc = [W([3, 512], F32) for _ in range(2)]
        DMA("sp", identf.ap, k_mats_d[0], writes=[identf.tk])
        DMA("sp", pswapf.ap, k_mats_d[1], writes=[pswapf.tk])
        DMA("sp", bonesf.ap, k_mats_d[2], writes=[bonesf.tk])
        E("pool", "memset", onesf.ap, 1.0, writes=[onesf.tk])
        E("pool", "memset", cols.ap[:, 0:1], EPS, writes=[cols.tk])
        E("pool", "memset", cols.ap[:, 1:2], 1.0, writes=[cols.tk])
        E("pool", "tensor_copy", out=ident.ap, in_=identf.ap, reads=[identf.tk], writes=[ident.tk])
        E("pool", "tensor_copy", out=pswap.ap, in_=pswapf.ap, reads=[pswapf.tk], writes=[pswap.tk])
        E("pool", "tensor_copy", out=bones.ap, in_=bonesf.ap, reads=[bonesf.tk], writes=[bones.tk])
        nsl = dict(allow_slow_non_contiguous=True)
        for r in range(3):
            DMA("sp", scT.ap[:, :, r], cc_d[r].rearrange("(k p) -> p k", p=128), writes=[scT.tk], **nsl)
        for l in range(2):
            DMA("sp", gcol.ap[:, l, :], norm_g_d[l].rearrange("(k p) -> p k", p=128), writes=[gcol.tk], **nsl)
        for j in range(4):
            DMA("sp", cw.ap[:, :, j], lru_conv_w_d[0, j].rearrange("(n p) -> p n", p=128), writes=[cw.tk], **nsl)
        DMA("sp", cb.ap, lru_conv_b_d[0].rearrange("(n p) -> p n", p=128), writes=[cb.tk], **nsl)
        DMA("sp", gb.ap, lru_gate_b_d[0].rearrange("d g n p -> p (d g n)"), writes=[gb.tk], **nsl)
        DMA("sp", lamc.ap, lru_lambda_d[0].rearrange("d (n p) -> p (d n)", p=128), writes=[lamc.tk], **nsl)
        for c in range(2):
            DMA("sp", gcraw.ap[c * 64:(c + 1) * 64, 0:1], att_q_norm_d[0].rearrange("(d o) -> d o", o=1),
                writes=[gcraw.tk], **nsl)
            DMA("sp", gcraw.ap[c * 64:(c + 1) * 64, 1:2], att_k_norm_d[0].rearrange("(d o) -> d o", o=1),
                writes=[gcraw.tk], **nsl)
        DMA("sp", lv.ap, att_lambda_d[0].rearrange("a d -> (a d)").partition_broadcast(128), writes=[lv.tk])
        DMA("sp", sublnG.ap, att_subln_d[0].partition_broadcast(128), writes=[sublnG.tk])
        E("act", "activation", out=scT.ap, in_=scT.ap, func=AF.Silu, reads=[scT.tk], writes=[scT.tk])
        E("act", "activation", out=sp8.ap, in_=lamc.ap, func=AF.Exp, scale=-1.0, reads=[lamc.tk], writes=[sp8.tk])
        E("act", "activation", out=sp8.ap, in_=sp8.ap, func=AF.Ln, bias=cols.ap[:, 1:2], scale=1.0,
          reads=[sp8.tk, cols.tk], writes=[sp8.tk])
        E("dve", "tensor_scalar", out=sp16.ap, in0=sp8.ap, scalar1=-16.0, scalar2=None, op0=ALU.mult,
          reads=[sp8.tk], writes=[sp16.tk])
        E("dve", "tensor_scalar", out=sp8.ap, in0=sp8.ap, scalar1=-8.0, scalar2=None, op0=ALU.mult,
          reads=[sp8.tk, sp16.tk], writes=[sp8.tk])
        MM(psum[:, 7, 0:2], pswapf.ap, gcraw.ap, True, True, reads=[pswapf.tk, gcraw.tk], writes=[pt[7]])
        E("dve", "tensor_scalar", out=G4.ap[:, 0:1], in0=gcraw.ap[:, 0:1], scalar1=0.125, scalar2=None, op0=ALU.mult,
          reads=[gcraw.tk], writes=[G4.tk])
        E("dve", "tensor_scalar", out=G4.ap[:, 1:2], in0=psum[:, 7, 0:1], scalar1=0.125, scalar2=None, op0=ALU.mult,
          reads=[pt[7]], writes=[G4.tk])
        E("dve", "tensor_copy", out=G4.ap[:, 2:3], in_=gcraw.ap[:, 1:2], reads=[gcraw.tk], writes=[G4.tk])
        E("dve", "tensor_copy", out=G4.ap[:, 3:4], in_=psum[:, 7, 1:2], reads=[pt[7]], writes=[G4.tk])
        lv4 = lv.ap.rearrange("p (a b d) -> p a b d", a=2, b=2)
        E("dve", "tensor_tensor", out=lvp.ap.rearrange("p (a d) -> p a d", a=2), in0=lv4[:, :, 0, :], in1=lv4[:, :, 1, :],
          op=ALU.mult, reads=[lv.tk], writes=[lvp.tk])
        E("dve", "reduce_sum", out=lvs.ap, in_=lvp.ap.rearrange("p (a d) -> p a d", a=2), axis=mybir.AxisListType.X,
          reads=[lvp.tk], writes=[lvs.tk])
        E("act", "activation", out=lvs.ap, in_=lvs.ap, func=AF.Exp, reads=[lvs.tk], writes=[lvs.tk])
        E("dve", "tensor_tensor", out=lamcol.ap[:, 0:1], in0=lvs.ap[:, 0:1], in1=lvs.ap[:, 1:2], op=ALU.subtract,
          reads=[lvs.tk], writes=[lamcol.tk])
        E("dve", "tensor_scalar", out=lamcol.ap[:, 0:1], in0=lamcol.ap[:, 0:1], scalar1=LAM_INIT, scalar2=None,
          op0=ALU.add, reads=[lamcol.tk], writes=[lamcol.tk])
        E("dve", "tensor_scalar", out=lamcol.ap[:, 1:2], in0=lamcol.ap[:, 0:1], scalar1=-1.0, scalar2=None,
          op0=ALU.mult, reads=[lamcol.tk], writes=[lamcol.tk])
        E("dve", "tensor_scalar", out=sublnG.ap, in0=sublnG.ap, scalar1=1.0 - LAM_INIT, scalar2=None, op0=ALU.mult,
          reads=[sublnG.tk], writes=[sublnG.tk])
        i = 0
        for l in range(2):
            psT = psum[:, 2 + l, 0:144].rearrange("p (j r) -> p j r", r=3)
            mw = mod_w_d[l].rearrange("(k p) n -> p k n", p=128)
            for c_ in range(12):
                w = wb[i % 2]; m = mb[i % 2]; rc = rowc[i % 2]; bk = i % 2
                DMA("sp", w.ap, mw[:, :, c_ * 512:(c_ + 1) * 512], writes=[w.tk])
                DMA("sp", m.ap, mod_b_d[l, c_ * 512:(c_ + 1) * 512].partition_broadcast(3), writes=[m.tk])
                for k in range(16):
                    MM(psum[0:3, bk, :], scT.ap[:, k, :], w.ap[:, k, :], k == 0, k == 15,
                       reads=[scT.tk, w.tk], writes=[pt[bk]])
                E("dve", "tensor_tensor", out=rc.ap, in0=psum[0:3, bk, :], in1=m.ap, op=ALU.add,
                  reads=[pt[bk], m.tk], writes=[rc.tk])
                for j in range(4):
                    MM(psT[:, c_ * 4 + j, :], rc.ap[:, j * 128:(j + 1) * 128], identf.ap[0:3, 0:3], True, True,
                       reads=[rc.tk, identf.tk], writes=[pt[2 + l]])
                i += 1
            E("dve", "tensor_copy", out=modT.ap[:, l], in_=psT, reads=[pt[2 + l]], writes=[modT.tk])
            for r in range(3):
                E("dve", "scalar_tensor_tensor", out=AT.ap[:, l, :, r], in0=modT.ap[:, l, 16:32, r], scalar=1.0,
                  in1=gcol.ap[:, l, :], op0=ALU.add, op1=ALU.mult, reads=[modT.tk, gcol.tk], writes=[AT.tk])
        if debug:
            DMA("sp", dbg_m, modT.ap.rearrange("p l j r -> p (l j r)"), reads=[modT.tk])

    def stageA(b, l):
        W.reset()
        xt = [W([128, D], F32) for _ in range(2)]
        xn = [W([128, D], BF16) for _ in range(2)]
        junk = W([128, D], BF16)
        E("dve", "memset", ssq.ap, 0.0, writes=[ssq.tk])
        for t in range(NTILE):
            r = 2 if t < 2 else b
            if l == 0:
                src = ctx_d[b, t * 128:(t + 1) * 128, :] if t < 2 else x_d[b, (t - 2) * 128:(t - 1) * 128, :]
                rds = []
            else:
                src = x1_s[t * 128:(t + 1) * 128, :]
                rds = [x1_tk[t]]
            xb = xt[t % 2]; nb2 = xn[t % 2]
            DMA("sp", xb.ap, src, reads=rds, writes=[xb.tk])
            E("act", "activation", out=junk.ap, in_=xb.ap, func=AF.Square, accum_out=ssq.ap[:, t:t + 1],
              reads=[xb.tk], writes=[junk.tk, ssq.tk])
            E("act", "activation", out=tsq.ap[:, t:t + 1], in_=ssq.ap[:, t:t + 1], func=AF.Sqrt,
              bias=cols.ap[:, 0:1], scale=1.0 / D, reads=[ssq.tk], writes=[tsq.tk])
            E("dve", "reciprocal", out=rsq.ap[:, t:t + 1], in_=tsq.ap[:, t:t + 1], reads=[tsq.tk], writes=[rsq.tk])
            E("dve", "tensor_scalar", out=nb2.ap, in0=xb.ap, scalar1=rsq.ap[:, t:t + 1], scalar2=None, op0=ALU.mult,
              reads=[xb.tk], sreads=[rsq.tk], writes=[nb2.tk])
            for half in range(2):
                bk = 6 + half
                pb = psum[:, bk, :].bitcast(BF16)
                for k in range(8):
                    kk = half * 8 + k
                    E("pe", "transpose", pb[:, k * 128:(k + 1) * 128], nb2.ap[:, kk * 128:(kk + 1) * 128], ident.ap,
                      reads=[nb2.tk], writes=[pt[bk]])
                for k in range(8):
                    kk = half * 8 + k
                    o_ = hT.ap[:, kk, t * 128:(t + 1) * 128]
                    i_ = pb[:, k * 128:(k + 1) * 128]
                    if half == 0:
                        E("dve", "tensor_scalar", out=o_, in0=i_, scalar1=AT.ap[:, l, kk, r:r + 1],
                          scalar2=modT.ap[:, l, kk, r:r + 1], op0=ALU.mult, op1=ALU.add,
                          reads=[pt[bk]], writes=[hT_tk[t]])
                    else:
                        E("act", "activation", out=o_, in_=i_, func=AF.Identity, scale=AT.ap[:, l, kk, r:r + 1],
                          bias=modT.ap[:, l, kk, r:r + 1], reads=[pt[bk]], writes=[hT_tk[t]])
        if debug and (l == DBG_L or stop == 'A0'):
            DMA("sp", dbg_h, hT.ap, reads=hT_tk)

    x1_tk = [Tk() for _ in range(NTILE)]
    yT0_tk = [Tk() for _ in range(16)]
    yT1_tk = [Tk() for _ in range(16)]

    def l0B(b):
        W.reset()
        wug = [W([128, 16, 2, 128], BF16) for _ in range(2)]
        gw = [W([128, 4, 128], BF16) for _ in range(2)]
        U = [W([128, 2312], F32) for _ in range(2)]
        XC = W([128, NT], F32); XCB = W([128, NT], BF16)
        RA = W([128, NT], F32); IB = W([128, NT], F32); SQ = W([128, NT], F32)
        HS = W([128, NT], F32); HB = W([128, NT], F32)
        Y = [W([128, NT], BF16) for _ in range(2)]
        sgt = [W([128, 512], F32) for _ in range(2)]
        win = lru_w_in_d[0].rearrange("(k p) c -> p k c", p=128)
        for u_ in U:
            E("dve", "memset", u_.ap, 0.0, writes=[u_.tk])
        segs = [(0, S_CTX, 0), (259, S_LAT, 256)]
        for n in range(16):
            w = wug[n % 2]; g_ = gw[n % 2]; Ub = U[n % 2]; Yb = Y[n % 2]
            DMA("pool", w.ap[:, :, 0, :], win[:, :, n * 128:(n + 1) * 128], writes=[w.tk])
            DMA("pool", w.ap[:, :, 1, :], win[:, :, D + n * 128:D + (n + 1) * 128], writes=[w.tk])
            DMA("pool", g_.ap, lru_gate_w_d[0, :, :, n].rearrange("d g i o -> i (d g) o"), writes=[g_.tk])
            for ci, (lo, hi) in enumerate(CH):
                bk = nb_()
                for k in range(16):
                    MM(psum[:, bk, 0:hi - lo], w.ap[:, k, 0, :], hT.ap[:, k, lo:hi], k == 0, k == 15,
                       reads=[w.tk] + hT_tks(lo, hi), writes=[pt[bk]])
                dst = Ub.ap[:, 2 + lo:2 + hi] if ci == 0 else Ub.ap[:, 259 + 2 + lo - 256:259 + 2 + hi - 256]
                E("act", "activation", out=dst, in_=psum[:, bk, 0:hi - lo], func=AF.Copy,
                  reads=[pt[bk]], writes=[Ub.tk])
            for (uo, L, to) in segs:
                E("act", "activation", out=XC.ap[:, to:to + L], in_=Ub.ap[:, uo:uo + L], func=AF.Identity,
                  scale=cw.ap[:, n, 0:1], bias=cb.ap[:, n:n + 1], reads=[Ub.tk], writes=[XC.tk])
                for j in range(1, 4):
                    E("dve", "scalar_tensor_tensor", out=XC.ap[:, to:to + L], in0=Ub.ap[:, uo + j:uo + j + L],
                      scalar=cw.ap[:, n, j:j + 1], in1=XC.ap[:, to:to + L], op0=ALU.mult, op1=ALU.add,
                      reads=[Ub.tk, XC.tk], writes=[XC.tk])
            E("pool", "tensor_copy", out=XCB.ap, in_=XC.ap, reads=[XC.tk], writes=[XCB.tk])
            for d in range(2):
                for ci, (lo, hi) in enumerate(CH):
                    for g in range(2):
                        bk = nb_()
                        MM(psum[:, bk, 0:hi - lo], g_.ap[:, d * 2 + g, :], XCB.ap[:, lo:hi], True, True,
                           reads=[g_.tk, XCB.tk], writes=[pt[bk]])
                        dstb = RA if g == 0 else IB
                        gi = (d * 2 + g) * 16 + n
                        E("act", "activation", out=dstb.ap[:, lo:hi], in_=psum[:, bk, 0:hi - lo], func=AF.Sigmoid,
                          bias=gb.ap[:, gi:gi + 1], scale=1.0, reads=[pt[bk]], writes=[dstb.tk])
                si = d * 16 + n
                E("act", "activation", out=SQ.ap, in_=RA.ap, func=AF.Exp, scale=sp16.ap[:, si:si + 1],
                  reads=[RA.tk], writes=[SQ.tk])
                E("act", "activation", out=RA.ap, in_=RA.ap, func=AF.Exp, scale=sp8.ap[:, si:si + 1],
                  reads=[RA.tk], writes=[RA.tk])
                E("act", "activation", out=SQ.ap, in_=SQ.ap, func=AF.Sqrt, scale=-1.0, bias=cols.ap[:, 1:2],
                  reads=[SQ.tk], writes=[SQ.tk])
                E("dve", "tensor_tensor", out=IB.ap, in0=IB.ap, in1=SQ.ap, op=ALU.mult,
                  reads=[IB.tk, SQ.tk], writes=[IB.tk])
                E("dve", "tensor_tensor", out=IB.ap, in0=IB.ap, in1=XC.ap, op=ALU.mult,
                  reads=[IB.tk, XC.tk], writes=[IB.tk])
                if d == 0:
                    E("dve", "tensor_tensor_scan", out=HS.ap, data0=RA.ap, data1=IB.ap, initial=0.0,
                      op0=ALU.mult, op1=ALU.add, reads=[RA.tk, IB.tk], writes=[HS.tk])
                else:
                    E("dve", "tensor_tensor_scan", out=HB.ap[:, 0:256][:, ::-1], data0=RA.ap[:, 0:256][:, ::-1],
                      data1=IB.ap[:, 0:256][:, ::-1], initial=0.0, op0=ALU.mult, op1=ALU.add,
                      reads=[RA.tk, IB.tk], writes=[HB.tk])
                    E("dve", "tensor_tensor_scan", out=HB.ap[:, 256:NT][:, ::-1], data0=RA.ap[:, 256:NT][:, ::-1],
                      data1=IB.ap[:, 256:NT][:, ::-1], initial=HB.ap[:, 0:1], op0=ALU.mult, op1=ALU.add,
                      reads=[RA.tk, IB.tk], sreads=[HB.tk], writes=[HB.tk])
            E("pool", "tensor_tensor", out=HS.ap, in0=HS.ap, in1=HB.ap, op=ALU.add,
              reads=[HS.tk, HB.tk], writes=[HS.tk])
            for ci, (lo, hi) in enumerate(CH):
                bk = nb_()
                for k in range(16):
                    MM(psum[:, bk, 0:hi - lo], w.ap[:, k, 1, :], hT.ap[:, k, lo:hi], k == 0, k == 15,
                       reads=[w.tk] + hT_tks(lo, hi), writes=[pt[bk]])
                sg = sgt[ci % 2]
                E("act", "activation", out=sg.ap[:, 0:hi - lo], in_=psum[:, bk, 0:hi - lo], func=AF.Silu,
                  reads=[pt[bk]], writes=[sg.tk])
                E("dve", "tensor_tensor", out=Yb.ap[:, lo:hi], in0=HS.ap[:, lo:hi], in1=sg.ap[:, 0:hi - lo],
                  op=ALU.mult, reads=[HS.tk, sg.tk], writes=[Yb.tk])
            DMA("sp", yT0_s[n], Yb.ap, reads=[Yb.tk], writes=[yT0_tk[n]])

    def stageC(b, l):
        W.reset()
        wo = Hh
        wo_ap = arena[:, PERS_BYTES // 4: PERS_BYTES // 4 + 16 * D // 2].bitcast(BF16).rearrange("p (n d) -> p n d", d=D)
        wo_tk = Tk()
        wsrc = (lru_w_out_d if l == 0 else att_w_out_d)[0].rearrange("(n p) d -> p n d", p=128)
        for q4 in range(4):
            DMA("pool", wo_ap[:, q4 * 4:(q4 + 1) * 4, :], wsrc[:, q4 * 4:(q4 + 1) * 4, :], writes=[wo_tk])
        yt = [W([128, 16, 512], BF16) for _ in range(2)]
        xt = [W([128, D], F32) for _ in range(2)]
        xo = [W([128, D], F32) for _ in range(2)]
        dg = [W([128, 128], F32) for _ in range(2)]

        def build_gate(r):
            for j in range(16):
                dj = dg[j % 2]
                E("dve", "tensor_scalar", out=dj.ap, in0=identf.ap, scalar1=modT.ap[:, l, 32 + j, r:r + 1], scalar2=None,
                  op0=ALU.mult, reads=[identf.tk], writes=[dj.tk])
                MM(psum[:, 7, (j % 4) * 128:(j % 4 + 1) * 128], onesf.ap, dj.ap, True, True,
                   reads=[onesf.tk, dj.tk], writes=[pt[7]])
                if j % 4 == 3:
                    E("act", "activation", out=gate_bc.ap[:, (j // 4) * 512:(j // 4 + 1) * 512], in_=psum[:, 7, :],
                      func=AF.Copy, reads=[pt[7]], writes=[gate_bc.tk])

        chunks = CH if l == 0 else CH[1:]
        ti = 0
        for ci, (lo, hi) in enumerate(chunks):
            if l == 0 and ci == 0:
                build_gate(2)
            elif (l == 0 and ci == 1) or (l == 1 and ci == 0):
                build_gate(b)
            ytb = yt[ci % 2]
            if l == 0:
                DMA("sp", ytb.ap[:, :, 0:hi - lo], yT0_s[:, :, lo:hi].rearrange("n p t -> p n t"),
                    reads=yT0_tk, writes=[ytb.tk])
            else:
                DMA("sp", ytb.ap[:, :, 0:hi - lo], yT1_s[:, :, lo - 256:hi - 256].rearrange("n p t -> p n t"),
                    reads=yT1_tk, writes=[ytb.tk])
            for sub in range((hi - lo) // 128):
                t = lo // 128 + sub
                xb = xt[ti % 2]; ob = xo[ti % 2]
                if l == 0:
                    src = ctx_d[b, t * 128:(t + 1) * 128, :] if t < 2 else x_d[b, (t - 2) * 128:(t - 1) * 128, :]
                    DMA("sp", xb.ap, src, writes=[xb.tk])
                else:
                    DMA("sp", xb.ap, x1_s[t * 128:(t + 1) * 128, :], reads=[x1_tk[t]], writes=[xb.tk])
                for dc in range(4):
                    bk = nb_()
                    for n in range(16):
                        MM(psum[:, bk, :], ytb.ap[:, n, sub * 128:(sub + 1) * 128], wo_ap[:, n, dc * 512:(dc + 1) * 512],
                           n == 0, n == 15, reads=[ytb.tk, wo_tk], writes=[pt[bk]])
                    E("dve", "tensor_tensor", out=ob.ap[:, dc * 512:(dc + 1) * 512], in0=psum[:, bk, :],
                      in1=gate_bc.ap[:, dc * 512:(dc + 1) * 512], op=ALU.mult, reads=[pt[bk], gate_bc.tk], writes=[ob.tk])
                    E("pool", "tensor_tensor", out=ob.ap[:, dc * 512:(dc + 1) * 512], in0=ob.ap[:, dc * 512:(dc + 1) * 512],
                      in1=xb.ap[:, dc * 512:(dc + 1) * 512], op=ALU.add, reads=[ob.tk, xb.tk], writes=[ob.tk])
                if l == 0:
                    DMA("pool", x1_s[t * 128:(t + 1) * 128, :], ob.ap, reads=[ob.tk], writes=[x1_tk[t]])
                else:
                    DMA("pool", out_d[b, (t - 2) * 128:(t - 1) * 128, :], ob.ap, reads=[ob.tk])
                ti += 1

    def l1B(b):
        W.reset()
        Wt = [W([128, 16, 4, 128], BF16) for _ in range(2)]
        TC = W([128, S_LAT], F32); TS = W([128, S_LAT], F32)
        QT = W([128, S_LAT], BF16); KT = W([128, NT], BF16)
        VA = W([128, NTILE, 132], BF16)
        SG = [W([128, 16, 128], BF16) for _ in range(2)]
        OALL = W([128, 16, 128], F32)
        YB = W([128, 16, 128], BF16)
        PT = [W([128, 2, 512], BF16) for _ in range(3)]
        YT = [W([128, S_LAT], BF16) for _ in range(1)]
        sqb = [W([128, 512], BF16) for _ in range(2)]
        qbb = [W([128, 512], BF16) for _ in range(2)]
        rst = [W([128, 512], F32) for _ in range(2)]
        t1b = [W([128, 512], F32) for _ in range(2)]
        t2b = [W([128, 512], F32) for _ in range(2)]
        ob_ = [W([128, 128], F32) for _ in range(4)]
        jk = W([128, 128], BF16)
        rr8 = W([128, 8], F32)
        Osb = [W([128, 3, 387], F32) for _ in range(1)]
        Osb[0].tk.small = True
        OALL.tk.small = True
        DMA("sp", TC.ap, k_tc_d, writes=[TC.tk])
        DMA("sp", TS.ap, k_ts_d, writes=[TS.tk])
        E("dve", "memset", VA.ap[:, :, 128:129], 1.0, writes=[VA.tk])
        E("dve", "memset", ssq.ap, 0.0, writes=[ssq.tk])
        win = att_w_in_d[0].rearrange("(k p) c -> p k c", p=128)
        ptO = Tk(excl=True)
        cnt = 0
        pcnt = 0
        ecnt = 0
        def load_w(h_):
            w_ = Wt[h_ % 2]
            for j_ in range(4):
                DMA("pool", w_.ap[:, :, j_, :], win[:, :, j_ * D + h_ * 128:j_ * D + (h_ + 1) * 128], writes=[w_.tk])

        def finish_a(hp):
            c0_ = 16 * (hp % 2)
            E("act", "activation", out=tsq.ap[:, c0_:c0_ + 16], in_=ssq.ap[:, c0_:c0_ + 16], func=AF.Sqrt,
              bias=cols.ap[:, 0:1], scale=1.0 / 128, reads=[ssq.tk], writes=[tsq.tk])
            E("dve", "reciprocal", out=rsq.ap[:, c0_:c0_ + 16], in_=tsq.ap[:, c0_:c0_ + 16], reads=[tsq.tk], writes=[rsq.tk])
            for ti_ in range(16):
                o2 = ob_[ti_ % 4]
                E("dve", "scalar_tensor_tensor", out=o2.ap, in0=OALL.ap[:, ti_, :], scalar=rsq.ap[:, c0_ + ti_:c0_ + ti_ + 1],
                  in1=sublnG.ap, op0=ALU.mult, op1=ALU.mult, reads=[OALL.tk, rsq.tk], writes=[o2.tk])
                E("pool", "tensor_tensor", out=YB.ap[:, ti_, :], in0=o2.ap, in1=SG[hp % 2].ap[:, ti_, :], op=ALU.mult,
                  reads=[o2.tk, SG[hp % 2].tk], writes=[YB.tk])

        def finish_b(hp):
            Yh_ = YT[0]
            pb_ = psum[:, 7, :].bitcast(BF16)
            for ti_ in range(16):
                E("pe", "transpose", pb_[:, (ti_ % 4) * 128:(ti_ % 4 + 1) * 128], YB.ap[:, ti_, :], ident.ap,
                  reads=[YB.tk], writes=[pt[7]])
                if ti_ % 4 == 3:
                    E("act", "activation", out=Yh_.ap[:, (ti_ // 4) * 512:(ti_ // 4 + 1) * 512], in_=pb_[:, 0:512],
                      func=AF.Copy, reads=[pt[7]], writes=[Yh_.tk])
            DMA("sp", yT1_s[hp], Yh_.ap, reads=[Yh_.tk], writes=[yT1_tk[hp]])

        load_w(0)
        for h in range(nheads):
            w = Wt[h % 2]
            if h + 1 < nheads:
                load_w(h + 1)
            if h > 0:
                finish_a(h - 1)
            for j, chunks in ((0, CH[1:]), (1, CH)):
                for (lo, hi) in chunks:
                    n_ = hi - lo
                    sq = sqb[cnt % 2]; qb = qbb[cnt % 2]; rs = rst[cnt % 2]; t1 = t1b[cnt % 2]; t2 = t2b[cnt % 2]
                    cnt += 1
                    bk = nb_(4)
                    for k in range(16):
                        MM(psum[:, bk, 0:n_], w.ap[:, k, j, :], hT.ap[:, k, lo:hi], k == 0, k == 15,
                           reads=[w.tk] + hT_tks(lo, hi), writes=[pt[bk]])
                    if b1step <= -3:
                        continue
                    E("dve", "tensor_copy", out=qb.ap[:, 0:n_], in_=psum[:, bk, 0:n_], reads=[pt[bk]], writes=[qb.tk])
                    E("pool", "tensor_tensor", out=sq.ap[:, 0:n_], in0=qb.ap[:, 0:n_], in1=qb.ap[:, 0:n_], op=ALU.mult,
                      reads=[qb.tk], writes=[sq.tk])
                    if qk < 2:
                        continue
                    bk2 = nb_(4)
                    MM(psum[:, bk2, 0:n_], bones.ap, sq.ap[:, 0:n_], True, True, reads=[bones.tk, sq.tk], writes=[pt[bk2]])
                    if qk < 3:
                        continue
                    E("act", "activation", out=rs.ap[:, 0:n_], in_=psum[:, bk2, 0:n_], func=AF.Sqrt,
                      bias=cols.ap[:, 0:1], scale=1.0 / 64, reads=[pt[bk2]], writes=[rs.tk])
                    if qk < 4:
                        continue
                    rs0 = rs
                    rs = t2
                    E("dve", "reciprocal", out=rs.ap[:, 0:n_], in_=rs0.ap[:, 0:n_], reads=[rs0.tk], writes=[rs.tk])
                    if qk < 5:
                        continue
                    if lo >= 256:
                        tl = lo - 256
                        bk3 = nb_(4)
                        MM(psum[:, bk3, 0:n_], pswap.ap, qb.ap[:, 0:n_], True, True, reads=[pswap.tk, qb.tk], writes=[pt[bk3]])
                        if qk < 6:
                            continue
                        E("dve", "scalar_tensor_tensor", out=t1.ap[:, 0:n_], in0=qb.ap[:, 0:n_],
                          scalar=G4.ap[:, 2 * j:2 * j + 1], in1=TC.ap[:, tl:tl + n_], op0=ALU.mult, op1=ALU.mult,
                          reads=[qb.tk, TC.tk], writes=[t1.tk])
                        if qk < 7:
                            continue
                        E("dve", "scalar_tensor_tensor", out=rs0.ap[:, 0:n_], in0=psum[:, bk3, 0:n_],
                          scalar=G4.ap[:, 2 * j + 1:2 * j + 2], in1=TS.ap[:, tl:tl + n_], op0=ALU.mult, op1=ALU.mult,
                          reads=[pt[bk3], TS.tk], writes=[rs0.tk])
                        if qk < 8:
                            continue
                        E("pool", "tensor_tensor", out=t1.ap[:, 0:n_], in0=t1.ap[:, 0:n_], in1=rs0.ap[:, 0:n_], op=ALU.add,
                          reads=[t1.tk, rs0.tk], writes=[t1.tk])
                        dst = QT.ap[:, tl:tl + n_] if j == 0 else KT.ap[:, lo:hi]
                        dtk = QT.tk if j == 0 else KT.tk
                        E("dve", "tensor_tensor", out=dst, in0=t1.ap[:, 0:n_], in1=rs.ap[:, 0:n_], op=ALU.mult,
                          reads=[t1.tk, rs.tk], writes=[dtk])
                    elif qk >= 9:
                        E("dve", "tensor_scalar", out=t1.ap[:, 0:n_], in0=qb.ap[:, 0:n_], scalar1=G4.ap[:, 2:3], scalar2=None,
                          op0=ALU.mult, reads=[qb.tk], writes=[t1.tk])
                        E("dve", "tensor_tensor", out=KT.ap[:, lo:hi], in0=t1.ap[:, 0:n_], in1=rs.ap[:, 0:n_], op=ALU.mult,
                          reads=[t1.tk, rs.tk], writes=[KT.tk])
            if h > 0:
                finish_b(h - 1)
            for t in range(NTILE):
                bk = nb_(4)
                ncol = 128 if t < 2 else 256
                for k in range(16):
                    rhs = w.ap[:, k, 2, :] if t < 2 else w.ap[:, k, 2:4, :].rearrange("p a b -> p (a b)")
                    MM(psum[:, bk, 0:ncol], hT.ap[:, k, t * 128:(t + 1) * 128], rhs, k == 0, k == 15,
                       reads=[w.tk, hT_tk[t]], writes=[pt[bk]])
                E("dve", "tensor_copy", out=VA.ap[:, t, 0:128], in_=psum[:, bk, 0:128], reads=[pt[bk]], writes=[VA.tk])
                if t >= 2:
                    E("act", "activation", out=SG[h % 2].ap[:, t - 2, :], in_=psum[:, bk, 128:256], func=AF.Silu,
                      reads=[pt[bk], VA.tk], writes=[SG[h % 2].tk])
            E("dve", "memset", ssq.ap[:, 16 * (h % 2):16 * (h % 2) + 16], 0.0, writes=[ssq.tk])
            iters = [(qc, kb) for qc in range(4) for kb in range(NTILE)]

            def scores(i_):
                qc_, kb_ = iters[i_]
                sb_ = i_ % 2
                for c in range(2):
                    MM(psum[:, 2 * sb_ + c, :], KT.ap[c * 64:(c + 1) * 64, kb_ * 128:(kb_ + 1) * 128],
                       QT.ap[c * 64:(c + 1) * 64, qc_ * 512:qc_ * 512 + 512], True, True, reads=[KT.tk, QT.tk],
                       writes=[pt[2 * sb_], pt[2 * sb_ + 1]], tile_position=(c * 64, 0))

            scores(0)
            for i_, (qc, kb) in enumerate(iters):
                if i_ + 1 < len(iters):
                    scores(i_ + 1)
                sb_ = i_ % 2
                Pb = PT[pcnt % 3]
                pcnt += 1
                E("act", "activation", out=Pb.ap, in_=psum[:, 2 * sb_:2 * sb_ + 2, :], func=AF.Exp,
                  reads=[pt[2 * sb_], pt[2 * sb_ + 1]], writes=[Pb.tk])
                for c in range(2):
                    for qs in range(4):
                        idx = c * 4 + qs
                        MM(psum[:, 4 + idx // 3, (idx % 3) * 129:(idx % 3 + 1) * 129],
                           Pb.ap[:, c, qs * 128:(qs + 1) * 128], VA.ap[:, kb, 0:129],
                           (kb == 0 and idx % 3 == 0), kb == NTILE - 1, reads=[Pb.tk, VA.tk],
                           writes=[ptO], skip_group_check=True)
                if kb != NTILE - 1:
                    continue
                Ob = Osb[0]
                E("dve", "tensor_copy", out=Ob.ap, in_=psum[:, 4:7, 0:387], reads=[ptO], writes=[Ob.tk])
                Of = Ob.ap.rearrange("p a b -> p (a b)")
                Or = Of[:, 0:8 * 129].rearrange("p (i n) -> p i n", n=129)
                E("dve", "reciprocal", out=rr8.ap, in_=Or[:, :, 128], reads=[Ob.tk], writes=[rr8.tk])
                E("dve", "tensor_scalar", out=rr8.ap[:, 4:8], in0=rr8.ap[:, 4:8], scalar1=lamcol.ap[:, 1:2], scalar2=None,
                  op0=ALU.mult, reads=[rr8.tk], writes=[rr8.tk])
                c0 = 16 * (h % 2)
                for qs in range(4):
                    ti = qc * 4 + qs
                    E("dve", "tensor_scalar", out=OALL.ap[:, ti, :], in0=Or[:, qs, 0:128], scalar1=rr8.ap[:, qs:qs + 1],
                      scalar2=None, op0=ALU.mult, reads=[Ob.tk, rr8.tk], writes=[OALL.tk])
                    E("dve", "scalar_tensor_tensor", out=OALL.ap[:, ti, :], in0=Or[:, 4 + qs, 0:128],
                      scalar=rr8.ap[:, 4 + qs:5 + qs], in1=OALL.ap[:, ti, :], op0=ALU.mult, op1=ALU.add,
                      reads=[Ob.tk, OALL.tk, rr8.tk], writes=[OALL.tk])
                    E("act", "activation", out=jk.ap, in_=OALL.ap[:, ti, :], func=AF.Square,
                      accum_out=ssq.ap[:, c0 + ti:c0 + ti + 1], reads=[OALL.tk], writes=[jk.tk, ssq.tk])
        finish_a(nheads - 1)
        finish_b(nheads - 1)

    DBG_L = 1
    phase0()
    BAR()
    steps = [("A0", lambda b: stageA(b, 0)), ("B0", l0B), ("C0", lambda b: stageC(b, 0)),
             ("A1", lambda b: stageA(b, 1)), ("B1", l1B), ("C1", lambda b: stageC(b, 1))]
    for b in range(nb):
        if stop == "P0":
            break
        for nm, fn in steps:
            fn(b); BAR()
            if stop == nm:
                break
    S.emit()
    st.close()
    nc._sched_stats = (S.maxcount, {k: len(v) for k, v in S.dma_ops.items()})
    return nc


def _consts():
    s = np.arange(S_LAT)
    pos = np.stack([s // 64, s % 64], 0).astype(np.float32)
    inv = (10000.0 ** (-np.arange(16, dtype=np.float32) / 16)).astype(np.float32)
    p = np.arange(128)
    a = (p // 32) % 2; j = (p // 16) % 2; f = p % 16
    ang = pos[a][:, :] * inv[f][:, None]
    tc = np.cos(ang).astype(np.float32)
    ts = (np.sin(ang) * np.where(j == 0, -1.0, 1.0)[:, None]).astype(np.float32)
    ident = np.eye(128, dtype=np.float32)
    pswap = np.zeros((128, 128), np.float32); pswap[p ^ 16, p] = 1.0
    bones = np.zeros((128, 128), np.float32); bones[:64, :64] = 1.0; bones[64:, 64:] = 1.0
    return tc, ts, np.stack([ident, pswap, bones], 0)


_WNAMES = ["mod_w", "mod_b", "norm_g", "lru_w_in", "lru_conv_w", "lru_conv_b", "lru_gate_w", "lru_gate_b",
           "lru_lambda", "lru_w_out", "att_w_in", "att_q_norm", "att_k_norm", "att_lambda", "att_subln", "att_w_out"]


def make_in_maps(inputs, cores, nb=BPC):
    tc, ts, mats = _consts()
    shared = {k: np.ascontiguousarray(np.asarray(inputs[k], dtype=np.float32)) for k in _WNAMES}
    shared.update(k_tc=tc, k_ts=ts, k_mats=mats)
    x = np.asarray(inputs["x"]); ctx = np.asarray(inputs["ctx"]); c = np.asarray(inputs["c"]); c_ctx = np.asarray(inputs["c_ctx"])
    maps = []
    for i in cores:
        m = dict(shared)
        m["x"] = np.ascontiguousarray(x[i * BPC:i * BPC + nb])
        m["ctx"] = np.ascontiguousarray(ctx[i * BPC:i * BPC + nb])
        cc = np.zeros((3, D), np.float32)
        cc[0:nb] = c[i * BPC:i * BPC + nb]
        cc[2] = c_ctx
        m["cc"] = cc
        maps.append(m)
    return maps


def kernel(**inputs):
    nc = build_program()
    in_maps = make_in_maps(inputs, list(range(NCORES)))
    res = run_bass_kernel_spmd(nc, in_maps, core_ids=list(range(NCORES)))
    return np.concatenate([np.asarray(r["out"]) for r in res.results], axis=0).astype(np.float32)
```

```python
import math
import contextlib
import numpy as np
import concourse.bass as bass
import concourse.mybir as mybir
from concourse.bass_utils import run_bass_kernel_spmd

F32 = mybir.dt.float32
BF16 = mybir.dt.bfloat16
ALU = mybir.AluOpType
AF = mybir.ActivationFunctionType

NCORES = 8
BPC = 2
D = 2048
S_LAT = 2048
S_CTX = 256
NT = S_LAT + S_CTX
NTILE = NT // 128
CH = [(0, 256), (256, 768), (768, 1280), (1280, 1792), (1792, 2304)]
LAM_INIT = 0.8 - 0.6 * math.exp(-0.3 * 1)
EPS = 1e-6
ARENA_BYTES = 211600
PERS_BYTES = 18432
H_BYTES = 16 * NT * 2


_ALL_TK = []


class Tk:
    __slots__ = ("w", "r", "excl", "small")

    def __init__(self, excl=False, small=False):
        self.w = None
        self.r = {}
        self.excl = excl
        self.small = small
        _ALL_TK.append(self)


class Buf:
    __slots__ = ("ap", "tk")

    def __init__(self, ap, small=False):
        self.ap = ap
        self.tk = Tk(small=small)


class Op:
    __slots__ = ("eng", "fn", "deps", "signal", "count", "dma", "dsem", "dval", "hard", "phase")


class Sched:
    COMPUTE = ("pe", "act", "dve", "pool")
    QUEUES = ("sp", "act", "pool")

    def __init__(self, nc, ndma_sems=8):
        self.nc = nc
        self.names = ("pe", "act", "dve", "pool", "sp")
        self.ops = {k: [] for k in self.names}
        self.ndma = {k: 0 for k in self.names}
        self.ndma_sems = ndma_sems
        self.dma_ops = {k: [] for k in self.names}
        self.bar_deps = {k: [] for k in self.names}
        self.phase = 0

    def op(self, eng, fn, reads=(), writes=(), dma=False, sreads=()):
        o = Op()
        o.eng = eng; o.fn = fn; o.deps = []; o.signal = False; o.count = None; o.dma = dma; o.hard = None
        o.phase = self.phase
        if self.bar_deps[eng]:
            o.deps.extend(self.bar_deps[eng])
            self.bar_deps[eng] = []
        if dma:
            i = self.ndma[eng]; self.ndma[eng] += 1
            o.dsem = i % self.ndma_sems
            o.dval = 16 * (i // self.ndma_sems + 1)
            if i >= self.ndma_sems:
                o.deps.append(self.dma_ops[eng][i - self.ndma_sems])
            self.dma_ops[eng].append(o)
        rkey = ("d", id(o)) if dma else eng
        for t in sreads:
            if t.w is not None:
                o.deps.append(t.w)
                if t.w.eng == eng and not t.w.dma:
                    if o.hard is None:
                        o.hard = set()
                    o.hard.add(id(t.w))
                    t.w.signal = True
            t.r[rkey] = o
        for t in reads:
            if t.w is not None:
                o.deps.append(t.w)
                if t.small and t.w.eng == eng and not t.w.dma:
                    if o.hard is None:
                        o.hard = set()
                    o.hard.add(id(t.w))
                    t.w.signal = True
            if t.excl:
                for kk, ro in t.r.items():
                    if kk != rkey:
                        o.deps.append(ro)
            t.r[rkey] = o
        for t in writes:
            if t.w is not None:
                o.deps.append(t.w)
            o.deps.extend(t.r.values())
            t.w = o
            t.r = {}
        for d in o.deps:
            if not d.dma and d.eng != eng:
                d.signal = True
        self.ops[eng].append(o)
        return o

    def barrier(self):
        new = []
        for k in self.COMPUTE:
            last = None
            for o in reversed(self.ops[k]):
                if not o.dma:
                    last = o
                    break
            if last is not None and last.phase == self.phase:
                last.signal = True
                new.append(last)
        for k in self.QUEUES:
            n = len(self.dma_ops[k])
            for o in self.dma_ops[k][max(0, n - self.ndma_sems):]:
                new.append(o)
        for k in self.names:
            self.bar_deps[k] = list(new)
        for t in _ALL_TK:
            t.w = None
            t.r = {}
        self.phase += 1

    def emit(self):
        nc = self.nc
        with contextlib.ExitStack() as st:
            csem = [{k: st.enter_context(nc.semaphore(f"c{s_}_{k}")) for k in self.COMPUTE} for s_ in range(3)]
            dsem = {k: [st.enter_context(nc.semaphore(f"d_{k}{j}")) for j in range(self.ndma_sems)]
                    for k in self.QUEUES}
            block = st.enter_context(nc.Block())
            self.maxcount = {}
            for k in self.COMPUTE:
                c = 0; ph = -1; mx = 0
                for o in self.ops[k]:
                    if o.phase != ph:
                        ph = o.phase; c = 0
                    if o.signal and not o.dma:
                        c += 1
                        o.count = c
                        mx = max(mx, c)
                self.maxcount[k] = (mx, len(self.ops[k]))

            def run(k, e):
                waited = {}
                ph = 0
                for o in self.ops[k]:
                    need = {}
                    for d in o.deps:
                        if d is o:
                            continue
                        if d.dma:
                            key = ("d", d.eng, d.dsem); sem = dsem[d.eng][d.dsem]; val = d.dval
                        else:
                            if d.eng == k and (o.hard is None or id(d) not in o.hard):
                                continue
                            key = ("c", d.eng, d.phase); sem = csem[d.phase % 3][d.eng]; val = d.count
                        if waited.get(key, 0) >= val:
                            continue
                        if key not in need or need[key][1] < val:
                            need[key] = (sem, val)
                    for key, (sem, val) in need.items():
                        e.wait_ge(sem, val)
                        waited[key] = val
                    if k == "pool" and o.phase != ph:
                        assert o.phase == ph + 1, "pool needs an op in every phase"
                        ph = o.phase
                        if ph >= 2:
                            for kk in self.COMPUTE:
                                e.sem_clear(csem[(ph + 1) % 3][kk])
                    ins = o.fn(e)
                    if o.dma:
                        ins.then_inc(dsem[k][o.dsem], 16)
                    elif o.signal:
                        ins.then_inc(csem[o.phase % 3][k], 1)
                if k in self.QUEUES:
                    n = len(self.dma_ops[k])
                    for d in self.dma_ops[k][max(0, n - self.ndma_sems):]:
                        if waited.get(("d", k, d.dsem), 0) < d.dval:
                            e.wait_ge(dsem[k][d.dsem], d.dval)
                            waited[("d", k, d.dsem)] = d.dval

            block.sync(lambda e: run("sp", e))
            block.scalar(lambda e: run("act", e))
            block.vector(lambda e: run("dve", e))
            block.gpsimd(lambda e: run("pool", e))
            block.tensor(lambda e: run("pe", e))


def build_program(nb=BPC, debug=False, stop=None, nheads=16, b1step=9, qk=9, vg=9):
    nc = bass.Bass("TRN2", target_bir_lowering=False)
    del _ALL_TK[:]

    def din(name, shape):
        return nc.dram_tensor(name, list(shape), F32, kind="ExternalInput").ap()

    x_d = din("x", [nb, S_LAT, D])
    ctx_d = din("ctx", [nb, S_CTX, D])
    cc_d = din("cc", [3, D])
    mod_w_d = din("mod_w", [2, D, 3 * D])
    mod_b_d = din("mod_b", [2, 3 * D])
    norm_g_d = din("norm_g", [2, D])
    lru_w_in_d = din("lru_w_in", [1, D, 2 * D])
    lru_conv_w_d = din("lru_conv_w", [1, 4, D])
    lru_conv_b_d = din("lru_conv_b", [1, D])
    lru_gate_w_d = din("lru_gate_w", [1, 2, 2, 16, 128, 128])
    lru_gate_b_d = din("lru_gate_b", [1, 2, 2, 16, 128])
    lru_lambda_d = din("lru_lambda", [1, 2, D])
    lru_w_out_d = din("lru_w_out", [1, D, D])
    att_w_in_d = din("att_w_in", [1, D, 4 * D])
    att_q_norm_d = din("att_q_norm", [1, 64])
    att_k_norm_d = din("att_k_norm", [1, 64])
    att_lambda_d = din("att_lambda", [1, 4, 64])
    att_subln_d = din("att_subln", [1, 128])
    att_w_out_d = din("att_w_out", [1, D, D])
    k_tc_d = din("k_tc", [128, S_LAT])
    k_ts_d = din("k_ts", [128, S_LAT])
    k_mats_d = din("k_mats", [3, 128, 128])
    out_d = nc.dram_tensor("out", [nb, S_LAT, D], F32, kind="ExternalOutput").ap()
    x1_s = nc.dram_tensor("x1_s", [NT, D], F32, kind=("ExternalOutput" if debug else "Internal")).ap()
    yT0_s = nc.dram_tensor("yT0_s", [16, 128, NT], BF16, kind=("ExternalOutput" if debug else "Internal")).ap()
    yT1_s = nc.dram_tensor("yT1_s", [16, 128, S_LAT], BF16, kind=("ExternalOutput" if debug else "Internal")).ap()
    if debug:
        dbg_h = nc.dram_tensor("dbg_h", [128, 16, NT], BF16, kind="ExternalOutput").ap()
        dbg_m = nc.dram_tensor("dbg_m", [128, 2 * 48 * 3], F32, kind="ExternalOutput").ap()
        dbg_q = nc.dram_tensor("dbg_q", [128, S_LAT], BF16, kind="ExternalOutput").ap()
        dbg_k = nc.dram_tensor("dbg_k", [128, NT], BF16, kind="ExternalOutput").ap()
        dbg_v = nc.dram_tensor("dbg_v", [128, NTILE, 132], BF16, kind="ExternalOutput").ap()
        dbg_sg = nc.dram_tensor("dbg_sg", [128, 16, 128], F32, kind="ExternalOutput").ap()
        dbg_y = nc.dram_tensor("dbg_y", [128, S_LAT], BF16, kind="ExternalOutput").ap()
        dbg_o = nc.dram_tensor("dbg_o", [128, 128], F32, kind="ExternalOutput").ap()
        dbg_c = nc.dram_tensor("dbg_c", [128, 32 * 3 + 4 + 4], F32, kind="ExternalOutput").ap()

    S = Sched(nc)
    st = contextlib.ExitStack()
    arena = st.enter_context(nc.sbuf_tensor("arena", [128, ARENA_BYTES // 4], F32))
    psum = st.enter_context(nc.psum_tensor("psum", [128, 8, 512], F32))
    pt = [Tk(excl=True) for _ in range(8)]

    class Alloc:
        def __init__(self, lo, hi):
            self.lo = lo; self.hi = hi; self.cur = lo

        def reset(self):
            self.cur = self.lo

        def __call__(self, shape, dt):
            n = int(np.prod(shape[1:]))
            nbytes = n * (4 if dt == F32 else 2)
            nbytes = (nbytes + 31) // 32 * 32
            off = self.cur
            self.cur += nbytes
            assert self.cur <= self.hi, (shape, self.cur, self.hi)
            if dt == F32:
                v = arena[:, off // 4: off // 4 + n]
            else:
                v = arena[:, off // 4: off // 4 + (n + 1) // 2].bitcast(BF16)[:, 0:n]
            if len(shape) == 3:
                v = v.rearrange("p (a b) -> p a b", b=shape[2])
            elif len(shape) == 4:
                v = v.rearrange("p (a b c) -> p a b c", b=shape[2], c=shape[3])
            if shape[0] != 128:
                v = v[0:shape[0]]
            return Buf(v, small=(n <= 256))

    P = Alloc(0, PERS_BYTES)
    Hh = Alloc(PERS_BYTES, PERS_BYTES + H_BYTES)
    W = Alloc(PERS_BYTES + H_BYTES, ARENA_BYTES)

    def E(eng, meth, *args, reads=(), writes=(), sreads=(), **kw):
        return S.op(eng, lambda e: getattr(e, meth)(*args, **kw), reads=reads, writes=writes, sreads=sreads)

    def DMA(q, out, in_, reads=(), writes=(), **kw):
        return S.op(q, lambda e: e.dma_start(out=out, in_=in_, **kw), reads=reads, writes=writes, dma=True)

    def MM(out, lhsT, rhs, start, stop, reads, writes, **kw):
        return S.op("pe", lambda e: e.matmul(out, lhsT=lhsT, rhs=rhs, start=start, stop=stop, **kw),
                    reads=reads, writes=writes)

    bank_ctr = [0]

    def BAR():
        S.barrier()
        E("pool", "memset", cols.ap[:, 7:8], 0.0)

    def nb_(mod=6):
        b_ = bank_ctr[0] % mod
        bank_ctr[0] += 1
        return b_

    identf = P([128, 128], F32); pswapf = P([128, 128], F32); bonesf = P([128, 128], F32); onesf = P([128, 128], F32)
    ident = P([128, 128], BF16); pswap = P([128, 128], BF16); bones = P([128, 128], BF16)
    modT = P([128, 2, 48, 3], F32)
    AT = P([128, 2, 16, 3], F32)
    gcol = P([128, 2, 16], F32)
    cw = P([128, 16, 4], F32); cb = P([128, 16], F32); gb = P([128, 64], F32)
    lamc = P([128, 32], F32); sp8 = P([128, 32], F32); sp16 = P([128, 32], F32)
    cols = P([128, 8], F32)
    G4 = P([128, 4], F32); gcraw = P([128, 2], F32)
    lamcol = P([128, 4], F32)
    sublnG = P([128, 128], F32)
    lv = P([128, 256], F32); lvp = P([128, 128], F32); lvs = P([128, 2], F32)
    scT = P([128, 16, 3], F32)
    ssq = P([128, 32], F32); rsq = P([128, 32], F32); tsq = P([128, 32], F32)
    gate_bc = P([128, D], F32)
    hT = Hh([128, 16, NT], BF16)
    hT_tk = [Tk() for _ in range(NTILE)]
    CONST = Tk()

    def hT_tks(lo, hi):
        return hT_tk[lo // 128: hi // 128]

    def phase0():
        W.reset()
        wb = [W([128, 16, 512], F32) for _ in range(2)]
        mb = [W([3, 512], F32) for _ in range(2)]
        rowc = [W([3, 512], F32) for _ in range(2)]
        DMA("sp", identf.ap, k_mats_d[0], writes=[identf.tk])
        DMA("sp", pswapf.ap, k_mats_d[1], writes=[pswapf.tk])
        DMA("sp", bonesf.ap, k_mats_d[2], writes=[bonesf.tk])
        E("pool", "memset", onesf.ap, 1.0, writes=[onesf.tk])
        E("pool", "memset", cols.ap[:, 0:1], EPS, writes=[cols.tk])
        E("pool", "memset", cols.ap[:, 1:2], 1.0, writes=[cols.tk])
        E("pool", "tensor_copy", out=ident.ap, in_=identf.ap, reads=[identf.tk], writes=[ident.tk])
        E("pool", "tensor_copy", out=pswap.ap, in_=pswapf.ap, reads=[pswapf.tk], writes=[pswap.tk])
        E("pool", "tensor_copy", out=bones.ap, in_=bonesf.ap, reads=[bonesf.tk], writes=[bones.tk])
        nsl = dict(allow_slow_non_contiguous=True)
        for r in range(3):
            DMA("sp", scT.ap[:, :, r], cc_d[r].rearrange("(k p) -> p k", p=128), writes=[scT.tk], **nsl)
        for l in range(2):
            DMA("sp", gcol.ap[:, l, :], norm_g_d[l].rearrange("(k p) -> p k", p=128), writes=[gcol.tk], **nsl)
        for j in range(4):
            DMA("sp", cw.ap[:, :, j], lru_conv_w_d[0, j].rearrange("(n p) -> p n", p=128), writes=[cw.tk], **nsl)
        DMA("sp", cb.ap, lru_conv_b_d[0].rearrange("(n p) -> p n", p=128), writes=[cb.tk], **nsl)
        DMA("sp", gb.ap, lru_gate_b_d[0].rearrange("d g n p -> p (d g n)"), writes=[gb.tk], **nsl)
        DMA("sp", lamc.ap, lru_lambda_d[0].rearrange("d (n p) -> p (d n)", p=128), writes=[lamc.tk], **nsl)
        for c in range(2):
            DMA("sp", gcraw.ap[c * 64:(c + 1) * 64, 0:1], att_q_norm_d[0].rearrange("(d o) -> d o", o=1),
                writes=[gcraw.tk], **nsl)
            DMA("sp", gcraw.ap[c * 64:(c + 1) * 64, 1:2], att_k_norm_d[0].rearrange("(d o) -> d o", o=1),
                writes=[gcraw.tk], **nsl)
        DMA("sp", lv.ap, att_lambda_d[0].rearrange("a d -> (a d)").partition_broadcast(128), writes=[lv.tk])
        DMA("sp", sublnG.ap, att_subln_d[0].partition_broadcast(128), writes=[sublnG.tk])
        E("act", "activation", out=scT.ap, in_=scT.ap, func=AF.Silu, reads=[scT.tk], writes=[scT.tk])
        E("act", "activation", out=sp8.ap, in_=lamc.ap, func=AF.Exp, scale=-1.0, reads=[lamc.tk], writes=[sp8.tk])
        E("act", "activation", out=sp8.ap, in_=sp8.ap, func=AF.Ln, bias=cols.ap[:, 1:2], scale=1.0,
          reads=[sp8.tk, cols.tk], writes=[sp8.tk])
        E("dve", "tensor_scalar", out=sp16.ap, in0=sp8.ap, scalar1=-16.0, scalar2=None, op0=ALU.mult,
          reads=[sp8.tk], writes=[sp16.tk])
        E("dve", "tensor_scalar", out=sp8.ap, in0=sp8.ap, scalar1=-8.0, scalar2=None, op0=ALU.mult,
          reads=[sp8.tk, sp16.tk], writes=[sp8.tk])
        MM(psum[:, 7, 0:2], pswapf.ap, gcraw.ap, True, True, reads=[pswapf.tk, gcraw.tk], writes=[pt[7]])
        E("dve", "tensor_scalar", out=G4.ap[:, 0:1], in0=gcraw.ap[:, 0:1], scalar1=0.125, scalar2=None, op0=ALU.mult,
          reads=[gcraw.tk], writes=[G4.tk])
        E("dve", "tensor_scalar", out=G4.ap[:, 1:2], in0=psum[:, 7, 0:1], scalar1=0.125, scalar2=None, op0=ALU.mult,
          reads=[pt[7]], writes=[G4.tk])
        E("dve", "tensor_copy", out=G4.ap[:, 2:3], in_=gcraw.ap[:, 1:2], reads=[gcraw.tk], writes=[G4.tk])
        E("dve", "tensor_copy", out=G4.ap[:, 3:4], in_=psum[:, 7, 1:2], reads=[pt[7]], writes=[G4.tk])
        lv4 = lv.ap.rearrange("p (a b d) -> p a b d", a=2, b=2)
        E("dve", "tensor_tensor", out=lvp.ap.rearrange("p (a d) -> p a d", a=2), in0=lv4[:, :, 0, :], in1=lv4[:, :, 1, :],
          op=ALU.mult, reads=[lv.tk], writes=[lvp.tk])
        E("dve", "reduce_sum", out=lvs.ap, in_=lvp.ap.rearrange("p (a d) -> p a d", a=2), axis=mybir.AxisListType.X,
          reads=[lvp.tk], writes=[lvs.tk])
        E("act", "activation", out=lvs.ap, in_=lvs.ap, func=AF.Exp, reads=[lvs.tk], writes=[lvs.tk])
        E("dve", "tensor_tensor", out=lamcol.ap[:, 0:1], in0=lvs.ap[:, 0:1], in1=lvs.ap[:, 1:2], op=ALU.subtract,
          reads=[lvs.tk], writes=[lamcol.tk])
        E("dve", "tensor_scalar", out=lamcol.ap[:, 0:1], in0=lamcol.ap[:, 0:1], scalar1=LAM_INIT, scalar2=None,
          op0=ALU.add, reads=[lamcol.tk], writes=[lamcol.tk])
        E("dve", "tensor_scalar", out=lamcol.ap[:, 1:2], in0=lamcol.ap[:, 0:1], scalar1=-1.0, scalar2=None,
          op0=ALU.mult, reads=[lamcol.tk], writes=[lamcol.tk])
        E("dve", "tensor_scalar", out=sublnG.ap, in0=sublnG.ap, scalar1=1.0 - LAM_INIT, scalar2=None, op0=ALU.mult,
          reads=[sublnG.tk], writes=[sublnG.tk])
        i = 0
        for l in range(2):
            psT = psum[:, 2 + l, 0:144].rearrange("p (j r) -> p j r", r=3)
            mw = mod_w_d[l].rearrange("(k p) n -> p k n", p=128)
            for c_ in range(12):
                w = wb[i % 2]; m = mb[i % 2]; rc = rowc[i % 2]; bk = i % 2
                DMA("sp", w.ap, mw[:, :, c_ * 512:(c_ + 1) * 512], writes=[w.tk])
                DMA("sp", m.ap, mod_b_d[l, c_ * 512:(c_ + 1) * 512].partition_broadcast(3), writes=[m.tk])
                for k in range(16):
                    MM(psum[0:3, bk, :], scT.ap[:, k, :], w.ap[:, k, :], k == 0, k == 15,
                       reads=[scT.tk, w.tk], writes=[pt[bk]])
                E("dve", "tensor_tensor", out=rc.ap, in0=psum[0:3, bk, :], in1=m.ap, op=ALU.add,
                  reads=[pt[bk], m.tk], writes=[rc.tk])
                for j in range(4):
                    MM(psT[:, c_ * 4 + j, :], rc.ap[:, j * 128:(j + 1) * 128], identf.ap[0:3, 0:3], True, True,
                       reads=[rc.tk, identf.tk], writes=[pt[2 + l]])
                i += 1
            E("dve", "tensor_copy", out=modT.ap[:, l], in_=psT, reads=[pt[2 + l]], writes=[modT.tk])
            for r in range(3):
                E("dve", "scalar_tensor_tensor", out=AT.ap[:, l, :, r], in0=modT.ap[:, l, 16:32, r], scalar=1.0,
                  in1=gcol.ap[:, l, :], op0=ALU.add, op1=ALU.mult, reads=[modT.tk, gcol.tk], writes=[AT.tk])
        if debug:
            DMA("sp", dbg_m, modT.ap.rearrange("p l j r -> p (l j r)"), reads=[modT.tk])

    def stageA(b, l):
        W.reset()
        xt = [W([128, D], F32) for _ in range(2)]
        xn = [W([128, D], BF16) for _ in range(2)]
        junk = W([128, D], BF16)
        E("dve", "memset", ssq.ap, 0.0, writes=[ssq.tk])
        for t in range(NTILE):
            r = 2 if t < 2 else b
            if l == 0:
                src = ctx_d[b, t * 128:(t + 1) * 128, :] if t < 2 else x_d[b, (t - 2) * 128:(t - 1) * 128, :]
                rds = []
            else:
                src = x1_s[t * 128:(t + 1) * 128, :]
                rds = [x1_tk[t]]
            xb = xt[t % 2]; nb2 = xn[t % 2]
            DMA("sp", xb.ap, src, reads=rds, writes=[xb.tk])
            E("act", "activation", out=junk.ap, in_=xb.ap, func=AF.Square, accum_out=ssq.ap[:, t:t + 1],
              reads=[xb.tk], writes=[junk.tk, ssq.tk])
            E("act", "activation", out=tsq.ap[:, t:t + 1], in_=ssq.ap[:, t:t + 1], func=AF.Sqrt,
              bias=cols.ap[:, 0:1], scale=1.0 / D, reads=[ssq.tk], writes=[tsq.tk])
            E("dve", "reciprocal", out=rsq.ap[:, t:t + 1], in_=tsq.ap[:, t:t + 1], reads=[tsq.tk], writes=[rsq.tk])
            E("dve", "tensor_scalar", out=nb2.ap, in0=xb.ap, scalar1=rsq.ap[:, t:t + 1], scalar2=None, op0=ALU.mult,
              reads=[xb.tk], sreads=[rsq.tk], writes=[nb2.tk])
            for half in range(2):
                bk = 6 + half
                pb = psum[:, bk, :].bitcast(BF16)
                for k in range(8):
                    kk = half * 8 + k
                    E("pe", "transpose", pb[:, k * 128:(k + 1) * 128], nb2.ap[:, kk * 128:(kk + 1) * 128], ident.ap,
                      reads=[nb2.tk], writes=[pt[bk]])
                for k in range(8):
                    kk = half * 8 + k
                    o_ = hT.ap[:, kk, t * 128:(t + 1) * 128]
                    i_ = pb[:, k * 128:(k + 1) * 128]
                    if half == 0:
                        E("dve", "tensor_scalar", out=o_, in0=i_, scalar1=AT.ap[:, l, kk, r:r + 1],
                          scalar2=modT.ap[:, l, kk, r:r + 1], op0=ALU.mult, op1=ALU.add,
                          reads=[pt[bk]], writes=[hT_tk[t]])
                    else:
                        E("act", "activation", out=o_, in_=i_, func=AF.Identity, scale=AT.ap[:, l, kk, r:r + 1],
                          bias=modT.ap[:, l, kk, r:r + 1], reads=[pt[bk]], writes=[hT_tk[t]])
        if debug and (l == DBG_L or stop == 'A0'):
            DMA("sp", dbg_h, hT.ap, reads=hT_tk)

    x1_tk = [Tk() for _ in range(NTILE)]
    yT0_tk = [Tk() for _ in range(16)]
    yT1_tk = [Tk() for _ in range(16)]

    def l0B(b):
        W.reset()
        wug = [W([128, 16, 2, 128], BF16) for _ in range(2)]
        gw = [W([128, 4, 128], BF16) for _ in range(2)]
        U = [W([128, 2312], F32) for _ in range(2)]
        XC = W([128, NT], F32); XCB = W([128, NT], BF16)
        RA = W([128, NT], F32); IB = W([128, NT], F32); SQ = W([128, NT], F32)
        HS = W([128, NT], F32); HB = W([128, NT], F32)
        Y = [W([128, NT], BF16) for _ in range(2)]
        sgt = [W([128, 512], F32) for _ in range(2)]
        win = lru_w_in_d[0].rearrange("(k p) c -> p k c", p=128)
        for u_ in U:
            E("dve", "memset", u_.ap, 0.0, writes=[u_.tk])
        segs = [(0, S_CTX, 0), (259, S_LAT, 256)]
        for n in range(16):
            w = wug[n % 2]; g_ = gw[n % 2]; Ub = U[n % 2]; Yb = Y[n % 2]
            DMA("pool", w.ap[:, :, 0, :], win[:, :, n * 128:(n + 1) * 128], writes=[w.tk])
            DMA("pool", w.ap[:, :, 1, :], win[:, :, D + n * 128:D + (n + 1) * 128], writes=[w.tk])
            DMA("pool", g_.ap, lru_gate_w_d[0, :, :, n].rearrange("d g i o -> i (d g) o"), writes=[g_.tk])
            for ci, (lo, hi) in enumerate(CH):
                bk = nb_()
                for k in range(16):
                    MM(psum[:, bk, 0:hi - lo], w.ap[:, k, 0, :], hT.ap[:, k, lo:hi], k == 0, k == 15,
                       reads=[w.tk] + hT_tks(lo, hi), writes=[pt[bk]])
                dst = Ub.ap[:, 2 + lo:2 + hi] if ci == 0 else Ub.ap[:, 259 + 2 + lo - 256:259 + 2 + hi - 256]
                E("act", "activation", out=dst, in_=psum[:, bk, 0:hi - lo], func=AF.Copy,
                  reads=[pt[bk]], writes=[Ub.tk])
            for (uo, L, to) in segs:
                E("act", "activation", out=XC.ap[:, to:to + L], in_=Ub.ap[:, uo:uo + L], func=AF.Identity,
                  scale=cw.ap[:, n, 0:1], bias=cb.ap[:, n:n + 1], reads=[Ub.tk], writes=[XC.tk])
                for j in range(1, 4):
                    E("dve", "scalar_tensor_tensor", out=XC.ap[:, to:to + L], in0=Ub.ap[:, uo + j:uo + j + L],
                      scalar=cw.ap[:, n, j:j + 1], in1=XC.ap[:, to:to + L], op0=ALU.mult, op1=ALU.add,
                      reads=[Ub.tk, XC.tk], writes=[XC.tk])
            E("pool", "tensor_copy", out=XCB.ap, in_=XC.ap, reads=[XC.tk], writes=[XCB.tk])
            for d in range(2):
                for ci, (lo, hi) in enumerate(CH):
                    for g in range(2):
                        bk = nb_()
                        MM(psum[:, bk, 0:hi - lo], g_.ap[:, d * 2 + g, :], XCB.ap[:, lo:hi], True, True,
                           reads=[g_.tk, XCB.tk], writes=[pt[bk]])
                        dstb = RA if g == 0 else IB
                        gi = (d * 2 + g) * 16 + n
                        E("act", "activation", out=dstb.ap[:, lo:hi], in_=psum[:, bk, 0:hi - lo], func=AF.Sigmoid,
                          bias=gb.ap[:, gi:gi + 1], scale=1.0, reads=[pt[bk]], writes=[dstb.tk])
                si = d * 16 + n
                E("act", "activation", out=SQ.ap, in_=RA.ap, func=AF.Exp, scale=sp16.ap[:, si:si + 1],
                  reads=[RA.tk], writes=[SQ.tk])
                E("act", "activation", out=RA.ap, in_=RA.ap, func=AF.Exp, scale=sp8.ap[:, si:si + 1],
                  reads=[RA.tk], writes=[RA.tk])
                E("act", "activation", out=SQ.ap, in_=SQ.ap, func=AF.Sqrt, scale=-1.0, bias=cols.ap[:, 1:2],
                  reads=[SQ.tk], writes=[SQ.tk])
                E("dve", "tensor_tensor", out=IB.ap, in0=IB.ap, in1=SQ.ap, op=ALU.mult,
                  reads=[IB.tk, SQ.tk], writes=[IB.tk])
                E("dve", "tensor_tensor", out=IB.ap, in0=IB.ap, in1=XC.ap, op=ALU.mult,
                  reads=[IB.tk, XC.tk], writes=[IB.tk])
                if d == 0:
                    E("dve", "tensor_tensor_scan", out=HS.ap, data0=RA.ap, data1=IB.ap, initial=0.0,
                      op0=ALU.mult, op1=ALU.add, reads=[RA.tk, IB.tk], writes=[HS.tk])
                else:
                    E("dve", "tensor_tensor_scan", out=HB.ap[:, 0:256][:, ::-1], data0=RA.ap[:, 0:256][:, ::-1],
                      data1=IB.ap[:, 0:256][:, ::-1], initial=0.0, op0=ALU.mult, op1=ALU.add,
                      reads=[RA.tk, IB.tk], writes=[HB.tk])
                    E("dve", "tensor_tensor_scan", out=HB.ap[:, 256:NT][:, ::-1], data0=RA.ap[:, 256:NT][:, ::-1],
                      data1=IB.ap[:, 256:NT][:, ::-1], initial=HB.ap[:, 0:1], op0=ALU.mult, op1=ALU.add,
                      reads=[RA.tk, IB.tk], sreads=[HB.tk], writes=[HB.tk])
            E("pool", "tensor_tensor", out=HS.ap, in0=HS.ap, in1=HB.ap, op=ALU.add,
              reads=[HS.tk, HB.tk], writes=[HS.tk])
            for ci, (lo, hi) in enumerate(CH):
                bk = nb_()
                for k in range(16):
                    MM(psum[:, bk, 0:hi - lo], w.ap[:, k, 1, :], hT.ap[:, k, lo:hi], k == 0, k == 15,
                       reads=[w.tk] + hT_tks(lo, hi), writes=[pt[bk]])
                sg = sgt[ci % 2]
                E("act", "activation", out=sg.ap[:, 0:hi - lo], in_=psum[:, bk, 0:hi - lo], func=AF.Silu,
                  reads=[pt[bk]], writes=[sg.tk])
                E("dve", "tensor_tensor", out=Yb.ap[:, lo:hi], in0=HS.ap[:, lo:hi], in1=sg.ap[:, 0:hi - lo],
                  op=ALU.mult, reads=[HS.tk, sg.tk], writes=[Yb.tk])
            DMA("sp", yT0_s[n], Yb.ap, reads=[Yb.tk], writes=[yT0_tk[n]])

    def stageC(b, l):
        W.reset()
        wo = Hh
        wo_ap = arena[:, PERS_BYTES // 4: PERS_BYTES // 4 + 16 * D // 2].bitcast(BF16).rearrange("p (n d) -> p n d", d=D)
        wo_tk = Tk()
        wsrc = (lru_w_out_d if l == 0 else att_w_out_d)[0].rearrange("(n p) d -> p n d", p=128)
        for q4 in range(4):
            DMA("pool", wo_ap[:, q4 * 4:(q4 + 1) * 4, :], wsrc[:, q4 * 4:(q4 + 1) * 4, :], writes=[wo_tk])
        yt = [W([128, 16, 512], BF16) for _ in range(2)]
        xt = [W([128, D], F32) for _ in range(2)]
        xo = [W([128, D], F32) for _ in range(2)]
        dg = [W([128, 128], F32) for _ in range(2)]

        def build_gate(r):
            for j in range(16):
                dj = dg[j % 2]
                E("dve", "tensor_scalar", out=dj.ap, in0=identf.ap, scalar1=modT.ap[:, l, 32 + j, r:r + 1], scalar2=None,
                  op0=ALU.mult, reads=[identf.tk], writes=[dj.tk])
                MM(psum[:, 7, (j % 4) * 128:(j % 4 + 1) * 128], onesf.ap, dj.ap, True, True,
                   reads=[onesf.tk, dj.tk], writes=[pt[7]])
                if j % 4 == 3:
                    E("act", "activation", out=gate_bc.ap[:, (j // 4) * 512:(j // 4 + 1) * 512], in_=psum[:, 7, :],
                      func=AF.Copy, reads=[pt[7]], writes=[gate_bc.tk])

        chunks = CH if l == 0 else CH[1:]
        ti = 0
        for ci, (lo, hi) in enumerate(chunks):
            if l == 0 and ci == 0:
                build_gate(2)
            elif (l == 0 and ci == 1) or (l == 1 and ci == 0):
                build_gate(b)
            ytb = yt[ci % 2]
            if l == 0:
                DMA("sp", ytb.ap[:, :, 0:hi - lo], yT0_s[:, :, lo:hi].rearrange("n p t -> p n t"),
                    reads=yT0_tk, writes=[ytb.tk])
            else:
                DMA("sp", ytb.ap[:, :, 0:hi - lo], yT1_s[:, :, lo - 256:hi - 256].rearrange("n p t -> p n t"),
                    reads=yT1_tk, writes=[ytb.tk])
            for sub in range((hi - lo) // 128):
                t = lo // 128 + sub
                xb = xt[ti % 2]; ob = xo[ti % 2]
                if l == 0:
                    src = ctx_d[b, t * 128:(t + 1) * 128, :] if t < 2 else x_d[b, (t - 2) * 128:(t - 1) * 128, :]
                    DMA("sp", xb.ap, src, writes=[xb.tk])
                else:
                    DMA("sp", xb.ap, x1_s[t * 128:(t + 1) * 128, :], reads=[x1_tk[t]], writes=[xb.tk])
                for dc in range(4):
                    bk = nb_()
                    for n in range(16):
                        MM(psum[:, bk, :], ytb.ap[:, n, sub * 128:(sub + 1) * 128], wo_ap[:, n, dc * 512:(dc + 1) * 512],
                           n == 0, n == 15, reads=[ytb.tk, wo_tk], writes=[pt[bk]])
                    E("dve", "tensor_tensor", out=ob.ap[:, dc * 512:(dc + 1) * 512], in0=psum[:, bk, :],
                      in1=gate_bc.ap[:, dc * 512:(dc + 1) * 512], op=ALU.mult, reads=[pt[bk], gate_bc.tk], writes=[ob.tk])
                    E("pool", "tensor_tensor", out=ob.ap[:, dc * 512:(dc + 1) * 512], in0=ob.ap[:, dc * 512:(dc + 1) * 512],
                      in1=xb.ap[:, dc * 512:(dc + 1) * 512], op=ALU.add, reads=[ob.tk, xb.tk], writes=[ob.tk])
                if l == 0:
                    DMA("pool", x1_s[t * 128:(t + 1) * 128, :], ob.ap, reads=[ob.tk], writes=[x1_tk[t]])
                else:
                    DMA("pool", out_d[b, (t - 2) * 128:(t - 1) * 128, :], ob.ap, reads=[ob.tk])
                ti += 1

    def l1B(b):
        W.reset()
        Wt = [W([128, 16, 4, 128], BF16) for _ in range(2)]
        TC = W([128, S_LAT], F32); TS = W([128, S_LAT], F32)
        QT = W([128, S_LAT], BF16); KT = W([128, NT], BF16)
        VA = W([128, NTILE, 132], BF16)
        SG = [W([128, 16, 128], BF16) for _ in range(2)]
        OALL = W([128, 16, 128], F32)
        YB = W([128, 16, 128], BF16)
        PT = [W([128, 2, 512], BF16) for _ in range(3)]
        YT = [W([128, S_LAT], BF16) for _ in range(1)]
        sqb = [W([128, 512], BF16) for _ in range(2)]
        qbb = [W([128, 512], BF16) for _ in range(2)]
        rst = [W([128, 512], F32) for _ in range(2)]
        t1b = [W([128, 512], F32) for _ in range(2)]
        t2b = [W([128, 512], F32) for _ in range(2)]
        ob_ = [W([128, 128], F32) for _ in range(4)]
        jk = W([128, 128], BF16)
        rr8 = W([128, 8], F32)
        Osb = [W([128, 3, 387], F32) for _ in range(1)]
        Osb[0].tk.small = True
        OALL.tk.small = True
        DMA("sp", TC.ap, k_tc_d, writes=[TC.tk])
        DMA("sp", TS.ap, k_ts_d, writes=[TS.tk])
        E("dve", "memset", VA.ap[:, :, 128:129], 1.0, writes=[VA.tk])
        E("dve", "memset", ssq.ap, 0.0, writes=[ssq.tk])
        win = att_w_in_d[0].rearrange("(k p) c -> p k c", p=128)
        ptO = Tk(excl=True)
        cnt = 0
        pcnt = 0
        ecnt = 0
        def load_w(h_):
            w_ = Wt[h_ % 2]
            for j_ in range(4):
                DMA("pool", w_.ap[:, :, j_, :], win[:, :, j_ * D + h_ * 128:j_ * D + (h_ + 1) * 128], writes=[w_.tk])

        def finish_a(hp):
            c0_ = 16 * (hp % 2)
            E("act", "activation", out=tsq.ap[:, c0_:c0_ + 16], in_=ssq.ap[:, c0_:c0_ + 16], func=AF.Ln,
              bias=cols.ap[:, 0:1], scale=1.0 / 128, reads=[ssq.tk], writes=[tsq.tk])
            E("act", "activation", out=rsq.ap[:, c0_:c0_ + 16], in_=tsq.ap[:, c0_:c0_ + 16], func=AF.Exp, scale=-0.5,
              reads=[tsq.tk], writes=[rsq.tk])
            for ti_ in range(16):
                o2 = ob_[ti_ % 4]
                E("dve", "scalar_tensor_tensor", out=o2.ap, in0=OALL.ap[:, ti_, :], scalar=rsq.ap[:, c0_ + ti_:c0_ + ti_ + 1],
                  in1=sublnG.ap, op0=ALU.mult, op1=ALU.mult, reads=[OALL.tk, rsq.tk], writes=[o2.tk])
                E("pool", "tensor_tensor", out=YB.ap[:, ti_, :], in0=o2.ap, in1=SG[hp % 2].ap[:, ti_, :], op=ALU.mult,
                  reads=[o2.tk, SG[hp % 2].tk], writes=[YB.tk])

        def finish_b(hp):
            Yh_ = YT[0]
            pb_ = psum[:, 7, :].bitcast(BF16)
            for ti_ in range(16):
                E("pe", "transpose", pb_[:, (ti_ % 4) * 128:(ti_ % 4 + 1) * 128], YB.ap[:, ti_, :], ident.ap,
                  reads=[YB.tk], writes=[pt[7]])
                if ti_ % 4 == 3:
                    E("act", "activation", out=Yh_.ap[:, (ti_ // 4) * 512:(ti_ // 4 + 1) * 512], in_=pb_[:, 0:512],
                      func=AF.Copy, reads=[pt[7]], writes=[Yh_.tk])
            DMA("sp", yT1_s[hp], Yh_.ap, reads=[Yh_.tk], writes=[yT1_tk[hp]])

        load_w(0)
        for h in range(nheads):
            w = Wt[h % 2]
            if h + 1 < nheads:
                load_w(h + 1)
            for j, chunks in ((0, CH[1:]), (1, CH)):
                for (lo, hi) in chunks:
                    n_ = hi - lo
                    sq = sqb[cnt % 2]; qb = qbb[cnt % 2]; rs = rst[cnt % 2]; t1 = t1b[cnt % 2]; t2 = t2b[cnt % 2]
                    cnt += 1
                    bk = nb_(4)
                    for k in range(16):
                        MM(psum[:, bk, 0:n_], w.ap[:, k, j, :], hT.ap[:, k, lo:hi], k == 0, k == 15,
                           reads=[w.tk] + hT_tks(lo, hi), writes=[pt[bk]])
                    if b1step <= -3:
                        continue
                    E("dve", "tensor_copy", out=qb.ap[:, 0:n_], in_=psum[:, bk, 0:n_], reads=[pt[bk]], writes=[qb.tk])
                    E("pool", "tensor_tensor", out=sq.ap[:, 0:n_], in0=qb.ap[:, 0:n_], in1=qb.ap[:, 0:n_], op=ALU.mult,
                      reads=[qb.tk], writes=[sq.tk])
                    if qk < 2:
                        continue
                    bk2 = nb_(4)
                    MM(psum[:, bk2, 0:n_], bones.ap, sq.ap[:, 0:n_], True, True, reads=[bones.tk, sq.tk], writes=[pt[bk2]])
                    if qk < 3:
                        continue
                    E("act", "activation", out=rs.ap[:, 0:n_], in_=psum[:, bk2, 0:n_], func=AF.Ln,
                      bias=cols.ap[:, 0:1], scale=1.0 / 64, reads=[pt[bk2]], writes=[rs.tk])
                    rs0 = rs
                    rs = t2
                    E("act", "activation", out=rs.ap[:, 0:n_], in_=rs0.ap[:, 0:n_], func=AF.Exp, scale=-0.5,
                      reads=[rs0.tk], writes=[rs.tk])
                    if qk < 5:
                        continue
                    if lo >= 256:
                        tl = lo - 256
                        bk3 = nb_(4)
                        MM(psum[:, bk3, 0:n_], pswap.ap, qb.ap[:, 0:n_], True, True, reads=[pswap.tk, qb.tk], writes=[pt[bk3]])
                        if qk < 6:
                            continue
                        E("dve", "scalar_tensor_tensor", out=t1.ap[:, 0:n_], in0=qb.ap[:, 0:n_],
                          scalar=G4.ap[:, 2 * j:2 * j + 1], in1=TC.ap[:, tl:tl + n_], op0=ALU.mult, op1=ALU.mult,
                          reads=[qb.tk, TC.tk], writes=[t1.tk])
                        if qk < 7:
                            continue
                        E("dve", "scalar_tensor_tensor", out=rs0.ap[:, 0:n_], in0=psum[:, bk3, 0:n_],
                          scalar=G4.ap[:, 2 * j + 1:2 * j + 2], in1=TS.ap[:, tl:tl + n_], op0=ALU.mult, op1=ALU.mult,
                          reads=[pt[bk3], TS.tk], writes=[rs0.tk])
                        if qk < 8:
                            continue
                        E("pool", "tensor_tensor", out=t1.ap[:, 0:n_], in0=t1.ap[:, 0:n_], in1=rs0.ap[:, 0:n_], op=ALU.add,
                          reads=[t1.tk, rs0.tk], writes=[t1.tk])
                        dst = QT.ap[:, tl:tl + n_] if j == 0 else KT.ap[:, lo:hi]
                        dtk = QT.tk if j == 0 else KT.tk
                        E("pool", "tensor_tensor", out=dst, in0=t1.ap[:, 0:n_], in1=rs.ap[:, 0:n_], op=ALU.mult,
                          reads=[t1.tk, rs.tk], writes=[dtk])
                    elif qk >= 9:
                        E("dve", "tensor_scalar", out=t1.ap[:, 0:n_], in0=qb.ap[:, 0:n_], scalar1=G4.ap[:, 2:3], scalar2=None,
                          op0=ALU.mult, reads=[qb.tk], writes=[t1.tk])
                        E("dve", "tensor_tensor", out=KT.ap[:, lo:hi], in0=t1.ap[:, 0:n_], in1=rs.ap[:, 0:n_], op=ALU.mult,
                          reads=[t1.tk, rs.tk], writes=[KT.tk])
            if h > 0:
                finish_a(h - 1)
            for t in range(NTILE):
                bk = nb_(4)
                ncol = 128 if t < 2 else 256
                for k in range(16):
                    rhs = w.ap[:, k, 2, :] if t < 2 else w.ap[:, k, 2:4, :].rearrange("p a b -> p (a b)")
                    MM(psum[:, bk, 0:ncol], hT.ap[:, k, t * 128:(t + 1) * 128], rhs, k == 0, k == 15,
                       reads=[w.tk, hT_tk[t]], writes=[pt[bk]])
                E("dve", "tensor_copy", out=VA.ap[:, t, 0:128], in_=psum[:, bk, 0:128], reads=[pt[bk]], writes=[VA.tk])
                if t >= 2:
                    E("act", "activation", out=SG[h % 2].ap[:, t - 2, :], in_=psum[:, bk, 128:256], func=AF.Silu,
                      reads=[pt[bk], VA.tk], writes=[SG[h % 2].tk])
            if h > 0:
                finish_b(h - 1)
            E("dve", "memset", ssq.ap[:, 16 * (h % 2):16 * (h % 2) + 16], 0.0, writes=[ssq.tk])
            pending_sq = []
            iters = [(qc, kb) for qc in range(4) for kb in range(NTILE)]

            def scores(i_):
                qc_, kb_ = iters[i_]
                sb_ = i_ % 2
                for c in range(2):
                    MM(psum[:, 2 * sb_ + c, :], KT.ap[c * 64:(c + 1) * 64, kb_ * 128:(kb_ + 1) * 128],
                       QT.ap[c * 64:(c + 1) * 64, qc_ * 512:qc_ * 512 + 512], True, True, reads=[KT.tk, QT.tk],
                       writes=[pt[2 * sb_], pt[2 * sb_ + 1]], tile_position=(c * 64, 0))

            scores(0)
            def flush_sq():
                for (ti_, c0_) in pending_sq:
                    E("act", "activation", out=jk.ap, in_=OALL.ap[:, ti_, :], func=AF.Square,
                      accum_out=ssq.ap[:, c0_ + ti_:c0_ + ti_ + 1], reads=[OALL.tk], writes=[jk.tk, ssq.tk])
                del pending_sq[:]

            for i_, (qc, kb) in enumerate(iters):
                if i_ + 1 < len(iters):
                    scores(i_ + 1)
                if kb == 6:
                    flush_sq()
                sb_ = i_ % 2
                Pb = PT[pcnt % 3]
                pcnt += 1
                E("act", "activation", out=Pb.ap, in_=psum[:, 2 * sb_:2 * sb_ + 2, :], func=AF.Exp,
                  reads=[pt[2 * sb_], pt[2 * sb_ + 1]], writes=[Pb.tk])
                for c in range(2):
                    for qs in range(4):
                        idx = c * 4 + qs
                        MM(psum[:, 4 + idx // 3, (idx % 3) * 129:(idx % 3 + 1) * 129],
                           Pb.ap[:, c, qs * 128:(qs + 1) * 128], VA.ap[:, kb, 0:129],
                           (kb == 0 and idx % 3 == 0), kb == NTILE - 1, reads=[Pb.tk, VA.tk],
                           writes=[ptO], skip_group_check=True)
                if kb != NTILE - 1:
                    continue
                Ob = Osb[0]
                E("dve", "tensor_copy", out=Ob.ap, in_=psum[:, 4:7, 0:387], reads=[ptO], writes=[Ob.tk])
                Of = Ob.ap.rearrange("p a b -> p (a b)")
                Or = Of[:, 0:8 * 129].rearrange("p (i n) -> p i n", n=129)
                E("dve", "reciprocal", out=rr8.ap, in_=Or[:, :, 128], reads=[Ob.tk], writes=[rr8.tk])
                E("dve", "tensor_scalar", out=rr8.ap[:, 4:8], in0=rr8.ap[:, 4:8], scalar1=lamcol.ap[:, 1:2], scalar2=None,
                  op0=ALU.mult, reads=[rr8.tk], writes=[rr8.tk])
                c0 = 16 * (h % 2)
                for qs in range(4):
                    ti = qc * 4 + qs
                    E("dve", "tensor_scalar", out=OALL.ap[:, ti, :], in0=Or[:, qs, 0:128], scalar1=rr8.ap[:, qs:qs + 1],
                      scalar2=None, op0=ALU.mult, reads=[Ob.tk, rr8.tk], writes=[OALL.tk])
                    E("dve", "scalar_tensor_tensor", out=OALL.ap[:, ti, :], in0=Or[:, 4 + qs, 0:128],
                      scalar=rr8.ap[:, 4 + qs:5 + qs], in1=OALL.ap[:, ti, :], op0=ALU.mult, op1=ALU.add,
                      reads=[Ob.tk, OALL.tk, rr8.tk], writes=[OALL.tk])
                    pending_sq.append((ti, c0))
            flush_sq()
        finish_a(nheads - 1)
        finish_b(nheads - 1)

    DBG_L = 1
    phase0()
    BAR()
    steps = [("A0", lambda b: stageA(b, 0)), ("B0", l0B), ("C0", lambda b: stageC(b, 0)),
             ("A1", lambda b: stageA(b, 1)), ("B1", l1B), ("C1", lambda b: stageC(b, 1))]
    for b in range(nb):
        if stop == "P0":
            break
        for nm, fn in steps:
            fn(b); BAR()
            if stop == nm:
                break
    S.emit()
    st.close()
    nc._sched_stats = (S.maxcount, {k: len(v) for k, v in S.dma_ops.items()})
    return nc


def _consts():
    s = np.arange(S_LAT)
    pos = np.stack([s // 64, s % 64], 0).astype(np.float32)
    inv = (10000.0 ** (-np.arange(16, dtype=np.float32) / 16)).astype(np.float32)
    p = np.arange(128)
    a = (p // 32) % 2; j = (p // 16) % 2; f = p % 16
    ang = pos[a][:, :] * inv[f][:, None]
    tc = np.cos(ang).astype(np.float32)
    ts = (np.sin(ang) * np.where(j == 0, -1.0, 1.0)[:, None]).astype(np.float32)
    ident = np.eye(128, dtype=np.float32)
    pswap = np.zeros((128, 128), np.float32); pswap[p ^ 16, p] = 1.0
    bones = np.zeros((128, 128), np.float32); bones[:64, :64] = 1.0; bones[64:, 64:] = 1.0
    return tc, ts, np.stack([ident, pswap, bones], 0)


_WNAMES = ["mod_w", "mod_b", "norm_g", "lru_w_in", "lru_conv_w", "lru_conv_b", "lru_gate_w", "lru_gate_b",
           "lru_lambda", "lru_w_out", "att_w_in", "att_q_norm", "att_k_norm", "att_lambda", "att_subln", "att_w_out"]


def make_in_maps(inputs, cores, nb=BPC):
    tc, ts, mats = _consts()
    shared = {k: np.ascontiguousarray(np.asarray(inputs[k], dtype=np.float32)) for k in _WNAMES}
    shared.update(k_tc=tc, k_ts=ts, k_mats=mats)
    x = np.asarray(inputs["x"]); ctx = np.asarray(inputs["ctx"]); c = np.asarray(inputs["c"]); c_ctx = np.asarray(inputs["c_ctx"])
    maps = []
    for i in cores:
        m = dict(shared)
        m["x"] = np.ascontiguousarray(x[i * BPC:i * BPC + nb])
        m["ctx"] = np.ascontiguousarray(ctx[i * BPC:i * BPC + nb])
        cc = np.zeros((3, D), np.float32)
        cc[0:nb] = c[i * BPC:i * BPC + nb]
        cc[2] = c_ctx
        m["cc"] = cc
        maps.append(m)
    return maps


def kernel(**inputs):
    nc = build_program()
    in_maps = make_in_maps(inputs, list(range(NCORES)))
    res = run_bass_kernel_spmd(nc, in_maps, core_ids=list(range(NCORES)))
    return np.concatenate([np.asarray(r["out"]) for r in res.results], axis=0).astype(np.float32)
```

```python
import math
import contextlib
import numpy as np
import concourse.bass as bass
import concourse.mybir as mybir
from concourse.bass_utils import run_bass_kernel_spmd

F32 = mybir.dt.float32
BF16 = mybir.dt.bfloat16
ALU = mybir.AluOpType
AF = mybir.ActivationFunctionType

NCORES = 8
BPC = 2
D = 2048
S_LAT = 2048
S_CTX = 256
NT = S_LAT + S_CTX
NTILE = NT // 128
CH = [(0, 256), (256, 768), (768, 1280), (1280, 1792), (1792, 2304)]
LAM_INIT = 0.8 - 0.6 * math.exp(-0.3 * 1)
EPS = 1e-6
ARENA_BYTES = 211600
PERS_BYTES = 18432
H_BYTES = 16 * NT * 2


_ALL_TK = []


class Tk:
    __slots__ = ("w", "r", "excl", "small")

    def __init__(self, excl=False, small=False):
        self.w = None
        self.r = {}
        self.excl = excl
        self.small = small
        _ALL_TK.append(self)


class Buf:
    __slots__ = ("ap", "tk")

    def __init__(self, ap, small=False):
        self.ap = ap
        self.tk = Tk(small=small)


class Op:
    __slots__ = ("eng", "fn", "deps", "signal", "count", "dma", "dsem", "dval", "hard", "phase")


class Sched:
    COMPUTE = ("pe", "act", "dve", "pool")
    QUEUES = ("sp", "act", "pool")

    def __init__(self, nc, ndma_sems=8):
        self.nc = nc
        self.names = ("pe", "act", "dve", "pool", "sp")
        self.ops = {k: [] for k in self.names}
        self.ndma = {k: 0 for k in self.names}
        self.ndma_sems = ndma_sems
        self.dma_ops = {k: [] for k in self.names}
        self.bar_deps = {k: [] for k in self.names}
        self.phase = 0

    def op(self, eng, fn, reads=(), writes=(), dma=False, sreads=()):
        o = Op()
        o.eng = eng; o.fn = fn; o.deps = []; o.signal = False; o.count = None; o.dma = dma; o.hard = None
        o.phase = self.phase
        if self.bar_deps[eng]:
            o.deps.extend(self.bar_deps[eng])
            self.bar_deps[eng] = []
        if dma:
            i = self.ndma[eng]; self.ndma[eng] += 1
            o.dsem = i % self.ndma_sems
            o.dval = 16 * (i // self.ndma_sems + 1)
            if i >= self.ndma_sems:
                o.deps.append(self.dma_ops[eng][i - self.ndma_sems])
            self.dma_ops[eng].append(o)
        rkey = ("d", id(o)) if dma else eng
        for t in sreads:
            if t.w is not None:
                o.deps.append(t.w)
                if t.w.eng == eng and not t.w.dma:
                    if o.hard is None:
                        o.hard = set()
                    o.hard.add(id(t.w))
                    t.w.signal = True
            t.r[rkey] = o
        for t in reads:
            if t.w is not None:
                o.deps.append(t.w)
                if t.small and t.w.eng == eng and not t.w.dma:
                    if o.hard is None:
                        o.hard = set()
                    o.hard.add(id(t.w))
                    t.w.signal = True
            if t.excl:
                for kk, ro in t.r.items():
                    if kk != rkey:
                        o.deps.append(ro)
            t.r[rkey] = o
        for t in writes:
            if t.w is not None:
                o.deps.append(t.w)
            o.deps.extend(t.r.values())
            t.w = o
            t.r = {}
        for d in o.deps:
            if not d.dma and d.eng != eng:
                d.signal = True
        self.ops[eng].append(o)
        return o

    def barrier(self):
        new = []
        for k in self.COMPUTE:
            last = None
            for o in reversed(self.ops[k]):
                if not o.dma:
                    last = o
                    break
            if last is not None and last.phase == self.phase:
                last.signal = True
                new.append(last)
        for k in self.QUEUES:
            n = len(self.dma_ops[k])
            for o in self.dma_ops[k][max(0, n - self.ndma_sems):]:
                new.append(o)
        for k in self.names:
            self.bar_deps[k] = list(new)
        for t in _ALL_TK:
            t.w = None
            t.r = {}
        self.phase += 1

    def emit(self):
        nc = self.nc
        with contextlib.ExitStack() as st:
            csem = [{k: st.enter_context(nc.semaphore(f"c{s_}_{k}")) for k in self.COMPUTE} for s_ in range(3)]
            dsem = {k: [st.enter_context(nc.semaphore(f"d_{k}{j}")) for j in range(self.ndma_sems)]
                    for k in self.QUEUES}
            block = st.enter_context(nc.Block())
            self.maxcount = {}
            for k in self.COMPUTE:
                c = 0; ph = -1; mx = 0
                for o in self.ops[k]:
                    if o.phase != ph:
                        ph = o.phase; c = 0
                    if o.signal and not o.dma:
                        c += 1
                        o.count = c
                        mx = max(mx, c)
                self.maxcount[k] = (mx, len(self.ops[k]))

            def run(k, e):
                waited = {}
                ph = 0
                for o in self.ops[k]:
                    need = {}
                    for d in o.deps:
                        if d is o:
                            continue
                        if d.dma:
                            key = ("d", d.eng, d.dsem); sem = dsem[d.eng][d.dsem]; val = d.dval
                        else:
                            if d.eng == k and (o.hard is None or id(d) not in o.hard):
                                continue
                            key = ("c", d.eng, d.phase); sem = csem[d.phase % 3][d.eng]; val = d.count
                        if waited.get(key, 0) >= val:
                            continue
                        if key not in need or need[key][1] < val:
                            need[key] = (sem, val)
                    for key, (sem, val) in need.items():
                        e.wait_ge(sem, val)
                        waited[key] = val
                    if k == "pool" and o.phase != ph:
                        assert o.phase == ph + 1, "pool needs an op in every phase"
                        ph = o.phase
                        if ph >= 2:
                            for kk in self.COMPUTE:
                                e.sem_clear(csem[(ph + 1) % 3][kk])
                    ins = o.fn(e)
                    if o.dma:
                        ins.then_inc(dsem[k][o.dsem], 16)
                    elif o.signal:
                        ins.then_inc(csem[o.phase % 3][k], 1)
                if k in self.QUEUES:
                    n = len(self.dma_ops[k])
                    for d in self.dma_ops[k][max(0, n - self.ndma_sems):]:
                        if waited.get(("d", k, d.dsem), 0) < d.dval:
                            e.wait_ge(dsem[k][d.dsem], d.dval)
                            waited[("d", k, d.dsem)] = d.dval

            block.sync(lambda e: run("sp", e))
            block.scalar(lambda e: run("act", e))
            block.vector(lambda e: run("dve", e))
            block.gpsimd(lambda e: run("pool", e))
            block.tensor(lambda e: run("pe", e))


def build_program(nb=BPC, debug=False, stop=None, nheads=16, b1step=9, qk=9, vg=9):
    nc = bass.Bass("TRN2", target_bir_lowering=False)
    del _ALL_TK[:]

    def din(name, shape):
        return nc.dram_tensor(name, list(shape), F32, kind="ExternalInput").ap()

    x_d = din("x", [nb, S_LAT, D])
    ctx_d = din("ctx", [nb, S_CTX, D])
    cc_d = din("cc", [3, D])
    mod_w_d = din("mod_w", [2, D, 3 * D])
    mod_b_d = din("mod_b", [2, 3 * D])
    norm_g_d = din("norm_g", [2, D])
    lru_w_in_d = din("lru_w_in", [1, D, 2 * D])
    lru_conv_w_d = din("lru_conv_w", [1, 4, D])
    lru_conv_b_d = din("lru_conv_b", [1, D])
    lru_gate_w_d = din("lru_gate_w", [1, 2, 2, 16, 128, 128])
    lru_gate_b_d = din("lru_gate_b", [1, 2, 2, 16, 128])
    lru_lambda_d = din("lru_lambda", [1, 2, D])
    lru_w_out_d = din("lru_w_out", [1, D, D])
    att_w_in_d = din("att_w_in", [1, D, 4 * D])
    att_q_norm_d = din("att_q_norm", [1, 64])
    att_k_norm_d = din("att_k_norm", [1, 64])
    att_lambda_d = din("att_lambda", [1, 4, 64])
    att_subln_d = din("att_subln", [1, 128])
    att_w_out_d = din("att_w_out", [1, D, D])
    k_tc_d = din("k_tc", [128, S_LAT])
    k_ts_d = din("k_ts", [128, S_LAT])
    k_mats_d = din("k_mats", [3, 128, 128])
    out_d = nc.dram_tensor("out", [nb, S_LAT, D], F32, kind="ExternalOutput").ap()
    x1_s = nc.dram_tensor("x1_s", [NT, D], F32, kind=("ExternalOutput" if debug else "Internal")).ap()
    yT0_s = nc.dram_tensor("yT0_s", [16, 128, NT], BF16, kind=("ExternalOutput" if debug else "Internal")).ap()
    yT1_s = nc.dram_tensor("yT1_s", [16, 128, S_LAT], BF16, kind=("ExternalOutput" if debug else "Internal")).ap()
    if debug:
        dbg_h = nc.dram_tensor("dbg_h", [128, 16, NT], BF16, kind="ExternalOutput").ap()
        dbg_m = nc.dram_tensor("dbg_m", [128, 2 * 48 * 3], F32, kind="ExternalOutput").ap()
        dbg_q = nc.dram_tensor("dbg_q", [128, S_LAT], BF16, kind="ExternalOutput").ap()
        dbg_k = nc.dram_tensor("dbg_k", [128, NT], BF16, kind="ExternalOutput").ap()
        dbg_v = nc.dram_tensor("dbg_v", [128, NTILE, 132], BF16, kind="ExternalOutput").ap()
        dbg_sg = nc.dram_tensor("dbg_sg", [128, 16, 128], F32, kind="ExternalOutput").ap()
        dbg_y = nc.dram_tensor("dbg_y", [128, S_LAT], BF16, kind="ExternalOutput").ap()
        dbg_o = nc.dram_tensor("dbg_o", [128, 128], F32, kind="ExternalOutput").ap()
        dbg_c = nc.dram_tensor("dbg_c", [128, 32 * 3 + 4 + 4], F32, kind="ExternalOutput").ap()

    S = Sched(nc)
    st = contextlib.ExitStack()
    arena = st.enter_context(nc.sbuf_tensor("arena", [128, ARENA_BYTES // 4], F32))
    psum = st.enter_context(nc.psum_tensor("psum", [128, 8, 512], F32))
    pt = [Tk(excl=True) for _ in range(8)]

    class Alloc:
        def __init__(self, lo, hi):
            self.lo = lo; self.hi = hi; self.cur = lo

        def reset(self):
            self.cur = self.lo

        def __call__(self, shape, dt):
            n = int(np.prod(shape[1:]))
            nbytes = n * (4 if dt == F32 else 2)
            nbytes = (nbytes + 31) // 32 * 32
            off = self.cur
            self.cur += nbytes
            assert self.cur <= self.hi, (shape, self.cur, self.hi)
            if dt == F32:
                v = arena[:, off // 4: off // 4 + n]
            else:
                v = arena[:, off // 4: off // 4 + (n + 1) // 2].bitcast(BF16)[:, 0:n]
            if len(shape) == 3:
                v = v.rearrange("p (a b) -> p a b", b=shape[2])
            elif len(shape) == 4:
                v = v.rearrange("p (a b c) -> p a b c", b=shape[2], c=shape[3])
            if shape[0] != 128:
                v = v[0:shape[0]]
            return Buf(v, small=(n <= 256))

    P = Alloc(0, PERS_BYTES)
    Hh = Alloc(PERS_BYTES, PERS_BYTES + H_BYTES)
    W = Alloc(PERS_BYTES + H_BYTES, ARENA_BYTES)

    def E(eng, meth, *args, reads=(), writes=(), sreads=(), **kw):
        return S.op(eng, lambda e: getattr(e, meth)(*args, **kw), reads=reads, writes=writes, sreads=sreads)

    def DMA(q, out, in_, reads=(), writes=(), **kw):
        return S.op(q, lambda e: e.dma_start(out=out, in_=in_, **kw), reads=reads, writes=writes, dma=True)

    def MM(out, lhsT, rhs, start, stop, reads, writes, **kw):
        return S.op("pe", lambda e: e.matmul(out, lhsT=lhsT, rhs=rhs, start=start, stop=stop, **kw),
                    reads=reads, writes=writes)

    bank_ctr = [0]

    def BAR():
        S.barrier()
        E("pool", "memset", cols.ap[:, 7:8], 0.0)

    def nb_(mod=6):
        b_ = bank_ctr[0] % mod
        bank_ctr[0] += 1
        return b_

    identf = P([128, 128], F32); pswapf = P([128, 128], F32); bonesf = P([128, 128], F32); onesf = P([128, 128], F32)
    ident = P([128, 128], BF16); pswap = P([128, 128], BF16); bones = P([128, 128], BF16)
    modT = P([128, 2, 48, 3], F32)
    AT = P([128, 2, 16, 3], F32)
    gcol = P([128, 2, 16], F32)
    cw = P([128, 16, 4], F32); cb = P([128, 16], F32); gb = P([128, 64], F32)
    lamc = P([128, 32], F32); sp8 = P([128, 32], F32); sp16 = P([128, 32], F32)
    cols = P([128, 8], F32)
    G4 = P([128, 4], F32); gcraw = P([128, 2], F32)
    lamcol = P([128, 4], F32)
    sublnG = P([128, 128], F32)
    lv = P([128, 256], F32); lvp = P([128, 128], F32); lvs = P([128, 2], F32)
    scT = P([128, 16, 3], F32)
    ssq = P([128, 32], F32); rsq = P([128, 32], F32); tsq = P([128, 32], F32)
    gate_bc = P([128, D], F32)
    hT = Hh([128, 16, NT], BF16)
    hT_tk = [Tk() for _ in range(NTILE)]
    CONST = Tk()

    def hT_tks(lo, hi):
        return hT_tk[lo // 128: hi // 128]

    def phase0():
        W.reset()
        wb = [W([128, 16, 512], F32) for _ in range(2)]
        mb = [W([3, 512], F32) for _ in range(2)]
        rowc = [W([3, 512], F32) for _ in range(2)]
        DMA("sp", identf.ap, k_mats_d[0], writes=[identf.tk])
        DMA("sp", pswapf.ap, k_mats_d[1], writes=[pswapf.tk])
        DMA("sp", bonesf.ap, k_mats_d[2], writes=[bonesf.tk])
        E("pool", "memset", onesf.ap, 1.0, writes=[onesf.tk])
        E("pool", "memset", cols.ap[:, 0:1], EPS, writes=[cols.tk])
        E("pool", "memset", cols.ap[:, 1:2], 1.0, writes=[cols.tk])
        E("pool", "tensor_copy", out=ident.ap, in_=identf.ap, reads=[identf.tk], writes=[ident.tk])
        E("pool", "tensor_copy", out=pswap.ap, in_=pswapf.ap, reads=[pswapf.tk], writes=[pswap.tk])
        E("pool", "tensor_copy", out=bones.ap, in_=bonesf.ap, reads=[bonesf.tk], writes=[bones.tk])
        nsl = dict(allow_slow_non_contiguous=True)
        for r in range(3):
            DMA("sp", scT.ap[:, :, r], cc_d[r].rearrange("(k p) -> p k", p=128), writes=[scT.tk], **nsl)
        for l in range(2):
            DMA("sp", gcol.ap[:, l, :], norm_g_d[l].rearrange("(k p) -> p k", p=128), writes=[gcol.tk], **nsl)
        for j in range(4):
            DMA("sp", cw.ap[:, :, j], lru_conv_w_d[0, j].rearrange("(n p) -> p n", p=128), writes=[cw.tk], **nsl)
        DMA("sp", cb.ap, lru_conv_b_d[0].rearrange("(n p) -> p n", p=128), writes=[cb.tk], **nsl)
        DMA("sp", gb.ap, lru_gate_b_d[0].rearrange("d g n p -> p (d g n)"), writes=[gb.tk], **nsl)
        DMA("sp", lamc.ap, lru_lambda_d[0].rearrange("d (n p) -> p (d n)", p=128), writes=[lamc.tk], **nsl)
        for c in range(2):
            DMA("sp", gcraw.ap[c * 64:(c + 1) * 64, 0:1], att_q_norm_d[0].rearrange("(d o) -> d o", o=1),
                writes=[gcraw.tk], **nsl)
            DMA("sp", gcraw.ap[c * 64:(c + 1) * 64, 1:2], att_k_norm_d[0].rearrange("(d o) -> d o", o=1),
                writes=[gcraw.tk], **nsl)
        DMA("sp", lv.ap, att_lambda_d[0].rearrange("a d -> (a d)").partition_broadcast(128), writes=[lv.tk])
        DMA("sp", sublnG.ap, att_subln_d[0].partition_broadcast(128), writes=[sublnG.tk])
        E("act", "activation", out=scT.ap, in_=scT.ap, func=AF.Silu, reads=[scT.tk], writes=[scT.tk])
        E("act", "activation", out=sp8.ap, in_=lamc.ap, func=AF.Exp, scale=-1.0, reads=[lamc.tk], writes=[sp8.tk])
        E("act", "activation", out=sp8.ap, in_=sp8.ap, func=AF.Ln, bias=cols.ap[:, 1:2], scale=1.0,
          reads=[sp8.tk, cols.tk], writes=[sp8.tk])
        E("dve", "tensor_scalar", out=sp16.ap, in0=sp8.ap, scalar1=-16.0, scalar2=None, op0=ALU.mult,
          reads=[sp8.tk], writes=[sp16.tk])
        E("dve", "tensor_scalar", out=sp8.ap, in0=sp8.ap, scalar1=-8.0, scalar2=None, op0=ALU.mult,
          reads=[sp8.tk, sp16.tk], writes=[sp8.tk])
        MM(psum[:, 7, 0:2], pswapf.ap, gcraw.ap, True, True, reads=[pswapf.tk, gcraw.tk], writes=[pt[7]])
        E("dve", "tensor_scalar", out=G4.ap[:, 0:1], in0=gcraw.ap[:, 0:1], scalar1=0.125, scalar2=None, op0=ALU.mult,
          reads=[gcraw.tk], writes=[G4.tk])
        E("dve", "tensor_scalar", out=G4.ap[:, 1:2], in0=psum[:, 7, 0:1], scalar1=0.125, scalar2=None, op0=ALU.mult,
          reads=[pt[7]], writes=[G4.tk])
        E("dve", "tensor_copy", out=G4.ap[:, 2:3], in_=gcraw.ap[:, 1:2], reads=[gcraw.tk], writes=[G4.tk])
        E("dve", "tensor_copy", out=G4.ap[:, 3:4], in_=psum[:, 7, 1:2], reads=[pt[7]], writes=[G4.tk])
        lv4 = lv.ap.rearrange("p (a b d) -> p a b d", a=2, b=2)
        E("dve", "tensor_tensor", out=lvp.ap.rearrange("p (a d) -> p a d", a=2), in0=lv4[:, :, 0, :], in1=lv4[:, :, 1, :],
          op=ALU.mult, reads=[lv.tk], writes=[lvp.tk])
        E("dve", "reduce_sum", out=lvs.ap, in_=lvp.ap.rearrange("p (a d) -> p a d", a=2), axis=mybir.AxisListType.X,
          reads=[lvp.tk], writes=[lvs.tk])
        E("act", "activation", out=lvs.ap, in_=lvs.ap, func=AF.Exp, reads=[lvs.tk], writes=[lvs.tk])
        E("dve", "tensor_tensor", out=lamcol.ap[:, 0:1], in0=lvs.ap[:, 0:1], in1=lvs.ap[:, 1:2], op=ALU.subtract,
          reads=[lvs.tk], writes=[lamcol.tk])
        E("dve", "tensor_scalar", out=lamcol.ap[:, 0:1], in0=lamcol.ap[:, 0:1], scalar1=LAM_INIT, scalar2=None,
          op0=ALU.add, reads=[lamcol.tk], writes=[lamcol.tk])
        E("dve", "tensor_scalar", out=lamcol.ap[:, 1:2], in0=lamcol.ap[:, 0:1], scalar1=-1.0, scalar2=None,
          op0=ALU.mult, reads=[lamcol.tk], writes=[lamcol.tk])
        E("dve", "tensor_scalar", out=sublnG.ap, in0=sublnG.ap, scalar1=1.0 - LAM_INIT, scalar2=None, op0=ALU.mult,
          reads=[sublnG.tk], writes=[sublnG.tk])
        i = 0
        for l in range(2):
            psT = psum[:, 2 + l, 0:144].rearrange("p (j r) -> p j r", r=3)
            mw = mod_w_d[l].rearrange("(k p) n -> p k n", p=128)
            for c_ in range(12):
                w = wb[i % 2]; m = mb[i % 2]; rc = rowc[i % 2]; bk = i % 2
                DMA("sp", w.ap, mw[:, :, c_ * 512:(c_ + 1) * 512], writes=[w.tk])
                DMA("sp", m.ap, mod_b_d[l, c_ * 512:(c_ + 1) * 512].partition_broadcast(3), writes=[m.tk])
                for k in range(16):
                    MM(psum[0:3, bk, :], scT.ap[:, k, :], w.ap[:, k, :], k == 0, k == 15,
                       reads=[scT.tk, w.tk], writes=[pt[bk]])
                E("dve", "tensor_tensor", out=rc.ap, in0=psum[0:3, bk, :], in1=m.ap, op=ALU.add,
                  reads=[pt[bk], m.tk], writes=[rc.tk])
                for j in range(4):
                    MM(psT[:, c_ * 4 + j, :], rc.ap[:, j * 128:(j + 1) * 128], identf.ap[0:3, 0:3], True, True,
                       reads=[rc.tk, identf.tk], writes=[pt[2 + l]])
                i += 1
            E("dve", "tensor_copy", out=modT.ap[:, l], in_=psT, reads=[pt[2 + l]], writes=[modT.tk])
            for r in range(3):
                E("dve", "scalar_tensor_tensor", out=AT.ap[:, l, :, r], in0=modT.ap[:, l, 16:32, r], scalar=1.0,
                  in1=gcol.ap[:, l, :], op0=ALU.add, op1=ALU.mult, reads=[modT.tk, gcol.tk], writes=[AT.tk])
        if debug:
            DMA("sp", dbg_m, modT.ap.rearrange("p l j r -> p (l j r)"), reads=[modT.tk])

    def stageA(b, l):
        W.reset()
        xt = [W([128, D], F32) for _ in range(2)]
        xn = [W([128, D], BF16) for _ in range(2)]
        junk = W([128, D], BF16)
        E("dve", "memset", ssq.ap, 0.0, writes=[ssq.tk])
        for t in range(NTILE):
            r = 2 if t < 2 else b
            if l == 0:
                src = ctx_d[b, t * 128:(t + 1) * 128, :] if t < 2 else x_d[b, (t - 2) * 128:(t - 1) * 128, :]
                rds = []
            else:
                src = x1_s[t * 128:(t + 1) * 128, :]
                rds = [x1_tk[t]]
            xb = xt[t % 2]; nb2 = xn[t % 2]
            DMA("sp", xb.ap, src, reads=rds, writes=[xb.tk])
            E("act", "activation", out=junk.ap, in_=xb.ap, func=AF.Square, accum_out=ssq.ap[:, t:t + 1],
              reads=[xb.tk], writes=[junk.tk, ssq.tk])
            E("act", "activation", out=tsq.ap[:, t:t + 1], in_=ssq.ap[:, t:t + 1], func=AF.Sqrt,
              bias=cols.ap[:, 0:1], scale=1.0 / D, reads=[ssq.tk], writes=[tsq.tk])
            E("dve", "reciprocal", out=rsq.ap[:, t:t + 1], in_=tsq.ap[:, t:t + 1], reads=[tsq.tk], writes=[rsq.tk])
            E("dve", "tensor_scalar", out=nb2.ap, in0=xb.ap, scalar1=rsq.ap[:, t:t + 1], scalar2=None, op0=ALU.mult,
              reads=[xb.tk], sreads=[rsq.tk], writes=[nb2.tk])
            for half in range(2):
                bk = 6 + half
                pb = psum[:, bk, :].bitcast(BF16)
                for k in range(8):
                    kk = half * 8 + k
                    E("pe", "transpose", pb[:, k * 128:(k + 1) * 128], nb2.ap[:, kk * 128:(kk + 1) * 128], ident.ap,
                      reads=[nb2.tk], writes=[pt[bk]])
                for k in range(8):
                    kk = half * 8 + k
                    o_ = hT.ap[:, kk, t * 128:(t + 1) * 128]
                    i_ = pb[:, k * 128:(k + 1) * 128]
                    if half == 0:
                        E("dve", "tensor_scalar", out=o_, in0=i_, scalar1=AT.ap[:, l, kk, r:r + 1],
                          scalar2=modT.ap[:, l, kk, r:r + 1], op0=ALU.mult, op1=ALU.add,
                          reads=[pt[bk]], writes=[hT_tk[t]])
                    else:
                        E("act", "activation", out=o_, in_=i_, func=AF.Identity, scale=AT.ap[:, l, kk, r:r + 1],
                          bias=modT.ap[:, l, kk, r:r + 1], reads=[pt[bk]], writes=[hT_tk[t]])
        if debug and (l == DBG_L or stop == 'A0'):
            DMA("sp", dbg_h, hT.ap, reads=hT_tk)

    x1_tk = [Tk() for _ in range(NTILE)]
    yT0_tk = [Tk() for _ in range(16)]
    yT1_tk = [Tk() for _ in range(16)]

    def l0B(b):
        W.reset()
        wug = [W([128, 16, 2, 128], BF16) for _ in range(2)]
        gw = [W([128, 4, 128], BF16) for _ in range(2)]
        U = [W([128, 2312], F32) for _ in range(2)]
        SGL = [W([128, NT], BF16) for _ in range(2)]
        XC = W([128, NT], F32); XCB = W([128, NT], BF16)
        RA = W([128, NT], F32); IB = W([128, NT], F32); SQ = W([128, NT], F32)
        HS = W([128, NT], F32); HB = W([128, NT], F32)
        Y = [W([128, NT], BF16) for _ in range(2)]
        win = lru_w_in_d[0].rearrange("(k p) c -> p k c", p=128)
        for u_ in U:
            E("dve", "memset", u_.ap, 0.0, writes=[u_.tk])
        segs = [(0, S_CTX, 0), (259, S_LAT, 256)]

        def load_wug(n):
            w = wug[n % 2]
            DMA("pool", w.ap[:, :, 0, :], win[:, :, n * 128:(n + 1) * 128], writes=[w.tk])
            DMA("pool", w.ap[:, :, 1, :], win[:, :, D + n * 128:D + (n + 1) * 128], writes=[w.tk])

        def load_gw(n):
            DMA("pool", gw[n % 2].ap, lru_gate_w_d[0, :, :, n].rearrange("d g i o -> i (d g) o"), writes=[gw[n % 2].tk])

        def stage1(n):
            w = wug[n % 2]; Ub = U[n % 2]; Sg = SGL[n % 2]
            for ci, (lo, hi) in enumerate(CH):
                bk = nb_()
                for k in range(16):
                    MM(psum[:, bk, 0:hi - lo], w.ap[:, k, 0, :], hT.ap[:, k, lo:hi], k == 0, k == 15,
                       reads=[w.tk] + hT_tks(lo, hi), writes=[pt[bk]])
                dst = Ub.ap[:, 2 + lo:2 + hi] if ci == 0 else Ub.ap[:, 259 + 2 + lo - 256:259 + 2 + hi - 256]
                E("act", "activation", out=dst, in_=psum[:, bk, 0:hi - lo], func=AF.Copy,
                  reads=[pt[bk]], writes=[Ub.tk])
                yield
            for ci, (lo, hi) in enumerate(CH):
                bk = nb_()
                for k in range(16):
                    MM(psum[:, bk, 0:hi - lo], w.ap[:, k, 1, :], hT.ap[:, k, lo:hi], k == 0, k == 15,
                       reads=[w.tk] + hT_tks(lo, hi), writes=[pt[bk]])
                E("act", "activation", out=Sg.ap[:, lo:hi], in_=psum[:, bk, 0:hi - lo], func=AF.Silu,
                  reads=[pt[bk]], writes=[Sg.tk])
                yield

        def stage2(n):
            g_ = gw[n % 2]; Ub = U[n % 2]; Yb = Y[n % 2]; Sg = SGL[n % 2]
            for (uo, L, to) in segs:
                E("act", "activation", out=XC.ap[:, to:to + L], in_=Ub.ap[:, uo:uo + L], func=AF.Identity,
                  scale=cw.ap[:, n, 0:1], bias=cb.ap[:, n:n + 1], reads=[Ub.tk], writes=[XC.tk])
                for j in range(1, 4):
                    E("dve", "scalar_tensor_tensor", out=XC.ap[:, to:to + L], in0=Ub.ap[:, uo + j:uo + j + L],
                      scalar=cw.ap[:, n, j:j + 1], in1=XC.ap[:, to:to + L], op0=ALU.mult, op1=ALU.add,
                      reads=[Ub.tk, XC.tk], writes=[XC.tk])
            E("pool", "tensor_copy", out=XCB.ap[:, 0:256], in_=XC.ap[:, 0:256], reads=[XC.tk], writes=[XCB.tk])
            E("act", "activation", out=XCB.ap[:, 256:NT], in_=XC.ap[:, 256:NT], func=AF.Copy, reads=[XC.tk], writes=[XCB.tk])
            yield
            for d in range(2):
                for ci, (lo, hi) in enumerate(CH):
                    for g in range(2):
                        bk = nb_()
                        MM(psum[:, bk, 0:hi - lo], g_.ap[:, d * 2 + g, :], XCB.ap[:, lo:hi], True, True,
                           reads=[g_.tk, XCB.tk], writes=[pt[bk]])
                        dstb = RA if g == 0 else IB
                        gi = (d * 2 + g) * 16 + n
                        E("act", "activation", out=dstb.ap[:, lo:hi], in_=psum[:, bk, 0:hi - lo], func=AF.Sigmoid,
                          bias=gb.ap[:, gi:gi + 1], scale=1.0, reads=[pt[bk]], writes=[dstb.tk])
                yield
                si = d * 16 + n
                E("act", "activation", out=SQ.ap, in_=RA.ap, func=AF.Exp, scale=sp16.ap[:, si:si + 1],
                  reads=[RA.tk], writes=[SQ.tk])
                E("act", "activation", out=RA.ap, in_=RA.ap, func=AF.Exp, scale=sp8.ap[:, si:si + 1],
                  reads=[RA.tk], writes=[RA.tk])
                E("act", "activation", out=SQ.ap, in_=SQ.ap, func=AF.Sqrt, scale=-1.0, bias=cols.ap[:, 1:2],
                  reads=[SQ.tk], writes=[SQ.tk])
                E("dve", "tensor_tensor", out=IB.ap, in0=IB.ap, in1=SQ.ap, op=ALU.mult,
                  reads=[IB.tk, SQ.tk], writes=[IB.tk])
                E("dve", "tensor_tensor", out=IB.ap, in0=IB.ap, in1=XC.ap, op=ALU.mult,
                  reads=[IB.tk, XC.tk], writes=[IB.tk])
                if d == 0:
                    E("dve", "tensor_tensor_scan", out=HS.ap, data0=RA.ap, data1=IB.ap, initial=0.0,
                      op0=ALU.mult, op1=ALU.add, reads=[RA.tk, IB.tk], writes=[HS.tk])
                else:
                    E("dve", "tensor_tensor_scan", out=HB.ap[:, 0:256][:, ::-1], data0=RA.ap[:, 0:256][:, ::-1],
                      data1=IB.ap[:, 0:256][:, ::-1], initial=0.0, op0=ALU.mult, op1=ALU.add,
                      reads=[RA.tk, IB.tk], writes=[HB.tk])
                    E("dve", "tensor_tensor_scan", out=HB.ap[:, 256:NT][:, ::-1], data0=RA.ap[:, 256:NT][:, ::-1],
                      data1=IB.ap[:, 256:NT][:, ::-1], initial=HB.ap[:, 0:1], op0=ALU.mult, op1=ALU.add,
                      reads=[RA.tk, IB.tk], sreads=[HB.tk], writes=[HB.tk])
                    E("dve", "tensor_tensor", out=HS.ap, in0=HS.ap, in1=HB.ap, op=ALU.add,
                      reads=[HS.tk, HB.tk], writes=[HS.tk])
                yield
            E("dve", "tensor_tensor", out=Yb.ap[:, 0:1152], in0=HS.ap[:, 0:1152], in1=Sg.ap[:, 0:1152], op=ALU.mult,
              reads=[HS.tk, Sg.tk], writes=[Yb.tk])
            E("pool", "tensor_tensor", out=Yb.ap[:, 1152:NT], in0=HS.ap[:, 1152:NT], in1=Sg.ap[:, 1152:NT], op=ALU.mult,
              reads=[HS.tk, Sg.tk], writes=[Yb.tk])
            DMA("sp", yT0_s[n], Yb.ap, reads=[Yb.tk], writes=[yT0_tk[n]])
            yield

        load_wug(0); load_gw(0)
        for _ in stage1(0):
            pass
        load_wug(1)
        order = "q p q q p q p q q p q p q q p q".split()
        for n in range(16):
            if n + 2 < 16:
                load_wug(n + 2)
            if n + 1 < 16:
                load_gw(n + 1)
            g1 = stage1(n + 1) if n + 1 < 16 else iter(())
            g2 = stage2(n)
            for tok in order:
                next(g1 if tok == "q" else g2, None)
            for _ in g1:
                pass
            for _ in g2:
                pass

    def stageC(b, l):
        W.reset()
        wo = Hh
        wo_ap = arena[:, PERS_BYTES // 4: PERS_BYTES // 4 + 16 * D // 2].bitcast(BF16).rearrange("p (n d) -> p n d", d=D)
        wo_tk = Tk()
        wsrc = (lru_w_out_d if l == 0 else att_w_out_d)[0].rearrange("(n p) d -> p n d", p=128)
        for q4 in range(4):
            DMA("pool", wo_ap[:, q4 * 4:(q4 + 1) * 4, :], wsrc[:, q4 * 4:(q4 + 1) * 4, :], writes=[wo_tk])
        yt = [W([128, 16, 512], BF16) for _ in range(2)]
        xt = [W([128, D], F32) for _ in range(2)]
        xo = [W([128, D], F32) for _ in range(2)]
        dg = [W([128, 128], F32) for _ in range(2)]

        def build_gate(r):
            for j in range(16):
                dj = dg[j % 2]
                E("dve", "tensor_scalar", out=dj.ap, in0=identf.ap, scalar1=modT.ap[:, l, 32 + j, r:r + 1], scalar2=None,
                  op0=ALU.mult, reads=[identf.tk], writes=[dj.tk])
                MM(psum[:, 7, (j % 4) * 128:(j % 4 + 1) * 128], onesf.ap, dj.ap, True, True,
                   reads=[onesf.tk, dj.tk], writes=[pt[7]])
                if j % 4 == 3:
                    E("act", "activation", out=gate_bc.ap[:, (j // 4) * 512:(j // 4 + 1) * 512], in_=psum[:, 7, :],
                      func=AF.Copy, reads=[pt[7]], writes=[gate_bc.tk])

        chunks = CH if l == 0 else CH[1:]
        ti = 0
        for ci, (lo, hi) in enumerate(chunks):
            if l == 0 and ci == 0:
                build_gate(2)
            elif (l == 0 and ci == 1) or (l == 1 and ci == 0):
                build_gate(b)
            ytb = yt[ci % 2]
            if l == 0:
                DMA("sp", ytb.ap[:, :, 0:hi - lo], yT0_s[:, :, lo:hi].rearrange("n p t -> p n t"),
                    reads=yT0_tk, writes=[ytb.tk])
            else:
                DMA("sp", ytb.ap[:, :, 0:hi - lo], yT1_s[:, :, lo - 256:hi - 256].rearrange("n p t -> p n t"),
                    reads=yT1_tk, writes=[ytb.tk])
            for sub in range((hi - lo) // 128):
                t = lo // 128 + sub
                xb = xt[ti % 2]; ob = xo[ti % 2]
                if l == 0:
                    src = ctx_d[b, t * 128:(t + 1) * 128, :] if t < 2 else x_d[b, (t - 2) * 128:(t - 1) * 128, :]
                    DMA("sp", xb.ap, src, writes=[xb.tk])
                else:
                    DMA("sp", xb.ap, x1_s[t * 128:(t + 1) * 128, :], reads=[x1_tk[t]], writes=[xb.tk])
                for dc in range(4):
                    bk = nb_()
                    for n in range(16):
                        MM(psum[:, bk, :], ytb.ap[:, n, sub * 128:(sub + 1) * 128], wo_ap[:, n, dc * 512:(dc + 1) * 512],
                           n == 0, n == 15, reads=[ytb.tk, wo_tk], writes=[pt[bk]])
                    E("dve", "tensor_tensor", out=ob.ap[:, dc * 512:(dc + 1) * 512], in0=psum[:, bk, :],
                      in1=gate_bc.ap[:, dc * 512:(dc + 1) * 512], op=ALU.mult, reads=[pt[bk], gate_bc.tk], writes=[ob.tk])
                    E("pool", "tensor_tensor", out=ob.ap[:, dc * 512:(dc + 1) * 512], in0=ob.ap[:, dc * 512:(dc + 1) * 512],
                      in1=xb.ap[:, dc * 512:(dc + 1) * 512], op=ALU.add, reads=[ob.tk, xb.tk], writes=[ob.tk])
                if l == 0:
                    DMA("pool", x1_s[t * 128:(t + 1) * 128, :], ob.ap, reads=[ob.tk], writes=[x1_tk[t]])
                else:
                    DMA("pool", out_d[b, (t - 2) * 128:(t - 1) * 128, :], ob.ap, reads=[ob.tk])
                ti += 1

    def l1B(b):
        W.reset()
        Wt = [W([128, 16, 4, 128], BF16) for _ in range(2)]
        TC = W([128, S_LAT], F32); TS = W([128, S_LAT], F32)
        QT = W([128, S_LAT], BF16); KT = W([128, NT], BF16)
        VA = W([128, NTILE, 132], BF16)
        SG = [W([128, 16, 128], BF16) for _ in range(2)]
        OALL = W([128, 16, 128], F32)
        YB = W([128, 16, 128], BF16)
        PT = [W([128, 2, 512], BF16) for _ in range(3)]
        YT = [W([128, S_LAT], BF16) for _ in range(1)]
        sqb = [W([128, 512], BF16) for _ in range(2)]
        qbb = [W([128, 512], BF16) for _ in range(2)]
        rst = [W([128, 512], F32) for _ in range(2)]
        t1b = [W([128, 512], F32) for _ in range(2)]
        t2b = [W([128, 512], F32) for _ in range(2)]
        ob_ = [W([128, 128], F32) for _ in range(4)]
        jk = W([128, 128], BF16)
        rr8 = W([128, 8], F32)
        Osb = [W([128, 3, 387], F32) for _ in range(1)]
        Osb[0].tk.small = True
        OALL.tk.small = True
        DMA("sp", TC.ap, k_tc_d, writes=[TC.tk])
        DMA("sp", TS.ap, k_ts_d, writes=[TS.tk])
        E("dve", "memset", VA.ap[:, :, 128:129], 1.0, writes=[VA.tk])
        E("dve", "memset", ssq.ap, 0.0, writes=[ssq.tk])
        win = att_w_in_d[0].rearrange("(k p) c -> p k c", p=128)
        ptO = Tk(excl=True)
        cnt = 0
        pcnt = 0
        ecnt = 0
        def load_w(h_):
            w_ = Wt[h_ % 2]
            for j_ in range(4):
                DMA("pool", w_.ap[:, :, j_, :], win[:, :, j_ * D + h_ * 128:j_ * D + (h_ + 1) * 128], writes=[w_.tk])

        def finish_a(hp):
            c0_ = 16 * (hp % 2)
            E("act", "activation", out=tsq.ap[:, c0_:c0_ + 16], in_=ssq.ap[:, c0_:c0_ + 16], func=AF.Ln,
              bias=cols.ap[:, 0:1], scale=1.0 / 128, reads=[ssq.tk], writes=[tsq.tk])
            E("act", "activation", out=rsq.ap[:, c0_:c0_ + 16], in_=tsq.ap[:, c0_:c0_ + 16], func=AF.Exp, scale=-0.5,
              reads=[tsq.tk], writes=[rsq.tk])
            for ti_ in range(16):
                o2 = ob_[ti_ % 4]
                E("dve", "scalar_tensor_tensor", out=o2.ap, in0=OALL.ap[:, ti_, :], scalar=rsq.ap[:, c0_ + ti_:c0_ + ti_ + 1],
                  in1=sublnG.ap, op0=ALU.mult, op1=ALU.mult, reads=[OALL.tk, rsq.tk], writes=[o2.tk])
                E("pool", "tensor_tensor", out=YB.ap[:, ti_, :], in0=o2.ap, in1=SG[hp % 2].ap[:, ti_, :], op=ALU.mult,
                  reads=[o2.tk, SG[hp % 2].tk], writes=[YB.tk])

        def finish_b(hp):
            Yh_ = YT[0]
            pb_ = psum[:, 7, :].bitcast(BF16)
            for ti_ in range(16):
                E("pe", "transpose", pb_[:, (ti_ % 4) * 128:(ti_ % 4 + 1) * 128], YB.ap[:, ti_, :], ident.ap,
                  reads=[YB.tk], writes=[pt[7]])
                if ti_ % 4 == 3:
                    E("act", "activation", out=Yh_.ap[:, (ti_ // 4) * 512:(ti_ // 4 + 1) * 512], in_=pb_[:, 0:512],
                      func=AF.Copy, reads=[pt[7]], writes=[Yh_.tk])
            DMA("sp", yT1_s[hp], Yh_.ap, reads=[Yh_.tk], writes=[yT1_tk[hp]])

        load_w(0)
        for h in range(nheads):
            w = Wt[h % 2]
            if h + 1 < nheads:
                load_w(h + 1)
            for j, chunks in ((0, CH[1:]), (1, CH)):
                for (lo, hi) in chunks:
                    n_ = hi - lo
                    sq = sqb[cnt % 2]; qb = qbb[cnt % 2]; rs = rst[cnt % 2]; t1 = t1b[cnt % 2]; t2 = t2b[cnt % 2]
                    cnt += 1
                    bk = nb_(4)
                    for k in range(16):
                        MM(psum[:, bk, 0:n_], w.ap[:, k, j, :], hT.ap[:, k, lo:hi], k == 0, k == 15,
                           reads=[w.tk] + hT_tks(lo, hi), writes=[pt[bk]])
                    if b1step <= -3:
                        continue
                    E("dve", "tensor_copy", out=qb.ap[:, 0:n_], in_=psum[:, bk, 0:n_], reads=[pt[bk]], writes=[qb.tk])
                    E("pool", "tensor_tensor", out=sq.ap[:, 0:n_], in0=qb.ap[:, 0:n_], in1=qb.ap[:, 0:n_], op=ALU.mult,
                      reads=[qb.tk], writes=[sq.tk])
                    if qk < 2:
                        continue
                    bk2 = nb_(4)
                    MM(psum[:, bk2, 0:n_], bones.ap, sq.ap[:, 0:n_], True, True, reads=[bones.tk, sq.tk], writes=[pt[bk2]])
                    if qk < 3:
                        continue
                    E("act", "activation", out=rs.ap[:, 0:n_], in_=psum[:, bk2, 0:n_], func=AF.Ln,
                      bias=cols.ap[:, 0:1], scale=1.0 / 64, reads=[pt[bk2]], writes=[rs.tk])
                    rs0 = rs
                    rs = t2
                    E("act", "activation", out=rs.ap[:, 0:n_], in_=rs0.ap[:, 0:n_], func=AF.Exp, scale=-0.5,
                      reads=[rs0.tk], writes=[rs.tk])
                    if qk < 5:
                        continue
                    if lo >= 256:
                        tl = lo - 256
                        bk3 = nb_(4)
                        MM(psum[:, bk3, 0:n_], pswap.ap, qb.ap[:, 0:n_], True, True, reads=[pswap.tk, qb.tk], writes=[pt[bk3]])
                        if qk < 6:
                            continue
                        E("dve", "scalar_tensor_tensor", out=t1.ap[:, 0:n_], in0=qb.ap[:, 0:n_],
                          scalar=G4.ap[:, 2 * j:2 * j + 1], in1=TC.ap[:, tl:tl + n_], op0=ALU.mult, op1=ALU.mult,
                          reads=[qb.tk, TC.tk], writes=[t1.tk])
                        if qk < 7:
                            continue
                        E("dve", "scalar_tensor_tensor", out=rs0.ap[:, 0:n_], in0=psum[:, bk3, 0:n_],
                          scalar=G4.ap[:, 2 * j + 1:2 * j + 2], in1=TS.ap[:, tl:tl + n_], op0=ALU.mult, op1=ALU.mult,
                          reads=[pt[bk3], TS.tk], writes=[rs0.tk])
                        if qk < 8:
                            continue
                        E("pool", "tensor_tensor", out=t1.ap[:, 0:n_], in0=t1.ap[:, 0:n_], in1=rs0.ap[:, 0:n_], op=ALU.add,
                          reads=[t1.tk, rs0.tk], writes=[t1.tk])
                        dst = QT.ap[:, tl:tl + n_] if j == 0 else KT.ap[:, lo:hi]
                        dtk = QT.tk if j == 0 else KT.tk
                        E("pool", "tensor_tensor", out=dst, in0=t1.ap[:, 0:n_], in1=rs.ap[:, 0:n_], op=ALU.mult,
                          reads=[t1.tk, rs.tk], writes=[dtk])
                    elif qk >= 9:
                        E("dve", "tensor_scalar", out=t1.ap[:, 0:n_], in0=qb.ap[:, 0:n_], scalar1=G4.ap[:, 2:3], scalar2=None,
                          op0=ALU.mult, reads=[qb.tk], writes=[t1.tk])
                        E("dve", "tensor_tensor", out=KT.ap[:, lo:hi], in0=t1.ap[:, 0:n_], in1=rs.ap[:, 0:n_], op=ALU.mult,
                          reads=[t1.tk, rs.tk], writes=[KT.tk])
            if h > 0:
                finish_a(h - 1)
            for t in range(NTILE):
                bk = nb_(4)
                ncol = 128 if t < 2 else 256
                for k in range(16):
                    rhs = w.ap[:, k, 2, :] if t < 2 else w.ap[:, k, 2:4, :].rearrange("p a b -> p (a b)")
                    MM(psum[:, bk, 0:ncol], hT.ap[:, k, t * 128:(t + 1) * 128], rhs, k == 0, k == 15,
                       reads=[w.tk, hT_tk[t]], writes=[pt[bk]])
                E("dve", "tensor_copy", out=VA.ap[:, t, 0:128], in_=psum[:, bk, 0:128], reads=[pt[bk]], writes=[VA.tk])
                if t >= 2:
                    E("act", "activation", out=SG[h % 2].ap[:, t - 2, :], in_=psum[:, bk, 128:256], func=AF.Silu,
                      reads=[pt[bk], VA.tk], writes=[SG[h % 2].tk])
            if h > 0:
                finish_b(h - 1)
            E("dve", "memset", ssq.ap[:, 16 * (h % 2):16 * (h % 2) + 16], 0.0, writes=[ssq.tk])
            pending_sq = []
            iters = [(qc, kb) for qc in range(4) for kb in range(NTILE)]

            def scores(i_):
                qc_, kb_ = iters[i_]
                sb_ = i_ % 2
                for c in range(2):
                    MM(psum[:, 2 * sb_ + c, :], KT.ap[c * 64:(c + 1) * 64, kb_ * 128:(kb_ + 1) * 128],
                       QT.ap[c * 64:(c + 1) * 64, qc_ * 512:qc_ * 512 + 512], True, True, reads=[KT.tk, QT.tk],
                       writes=[pt[2 * sb_], pt[2 * sb_ + 1]], tile_position=(c * 64, 0))

            scores(0)
            def flush_sq():
                for (ti_, c0_) in pending_sq:
                    E("act", "activation", out=jk.ap, in_=OALL.ap[:, ti_, :], func=AF.Square,
                      accum_out=ssq.ap[:, c0_ + ti_:c0_ + ti_ + 1], reads=[OALL.tk], writes=[jk.tk, ssq.tk])
                del pending_sq[:]

            for i_, (qc, kb) in enumerate(iters):
                if i_ + 1 < len(iters):
                    scores(i_ + 1)
                if kb == 6:
                    flush_sq()
                sb_ = i_ % 2
                Pb = PT[pcnt % 3]
                pcnt += 1
                E("act", "activation", out=Pb.ap, in_=psum[:, 2 * sb_:2 * sb_ + 2, :], func=AF.Exp,
                  reads=[pt[2 * sb_], pt[2 * sb_ + 1]], writes=[Pb.tk])
                for c in range(2):
                    for qs in range(4):
                        idx = c * 4 + qs
                        MM(psum[:, 4 + idx // 3, (idx % 3) * 129:(idx % 3 + 1) * 129],
                           Pb.ap[:, c, qs * 128:(qs + 1) * 128], VA.ap[:, kb, 0:129],
                           (kb == 0 and idx % 3 == 0), kb == NTILE - 1, reads=[Pb.tk, VA.tk],
                           writes=[ptO], skip_group_check=True)
                if kb != NTILE - 1:
                    continue
                Ob = Osb[0]
                E("dve", "tensor_copy", out=Ob.ap, in_=psum[:, 4:7, 0:387], reads=[ptO], writes=[Ob.tk])
                Of = Ob.ap.rearrange("p a b -> p (a b)")
                Or = Of[:, 0:8 * 129].rearrange("p (i n) -> p i n", n=129)
                E("dve", "reciprocal", out=rr8.ap, in_=Or[:, :, 128], reads=[Ob.tk], writes=[rr8.tk])
                E("dve", "tensor_scalar", out=rr8.ap[:, 4:8], in0=rr8.ap[:, 4:8], scalar1=lamcol.ap[:, 1:2], scalar2=None,
                  op0=ALU.mult, reads=[rr8.tk], writes=[rr8.tk])
                c0 = 16 * (h % 2)
                for qs in range(4):
                    ti = qc * 4 + qs
                    E("dve", "tensor_scalar", out=OALL.ap[:, ti, :], in0=Or[:, qs, 0:128], scalar1=rr8.ap[:, qs:qs + 1],
                      scalar2=None, op0=ALU.mult, reads=[Ob.tk, rr8.tk], writes=[OALL.tk])
                    E("dve", "scalar_tensor_tensor", out=OALL.ap[:, ti, :], in0=Or[:, 4 + qs, 0:128],
                      scalar=rr8.ap[:, 4 + qs:5 + qs], in1=OALL.ap[:, ti, :], op0=ALU.mult, op1=ALU.add,
                      reads=[Ob.tk, OALL.tk, rr8.tk], writes=[OALL.tk])
                    pending_sq.append((ti, c0))
            flush_sq()
        finish_a(nheads - 1)
        finish_b(nheads - 1)

    DBG_L = 1
    phase0()
    BAR()
    steps = [("A0", lambda b: stageA(b, 0)), ("B0", l0B), ("C0", lambda b: stageC(b, 0)),
             ("A1", lambda b: stageA(b, 1)), ("B1", l1B), ("C1", lambda b: stageC(b, 1))]
    for b in range(nb):
        if stop == "P0":
            break
        for nm, fn in steps:
            fn(b); BAR()
            if stop == nm:
                break
    S.emit()
    st.close()
    nc._sched_stats = (S.maxcount, {k: len(v) for k, v in S.dma_ops.items()})
    return nc


def _consts():
    s = np.arange(S_LAT)
    pos = np.stack([s // 64, s % 64], 0).astype(np.float32)
    inv = (10000.0 ** (-np.arange(16, dtype=np.float32) / 16)).astype(np.float32)
    p = np.arange(128)
    a = (p // 32) % 2; j = (p // 16) % 2; f = p % 16
    ang = pos[a][:, :] * inv[f][:, None]
    tc = np.cos(ang).astype(np.float32)
    ts = (np.sin(ang) * np.where(j == 0, -1.0, 1.0)[:, None]).astype(np.float32)
    ident = np.eye(128, dtype=np.float32)
    pswap = np.zeros((128, 128), np.float32); pswap[p ^ 16, p] = 1.0
    bones = np.zeros((128, 128), np.float32); bones[:64, :64] = 1.0; bones[64:, 64:] = 1.0
    return tc, ts, np.stack([ident, pswap, bones], 0)


_WNAMES = ["mod_w", "mod_b", "norm_g", "lru_w_in", "lru_conv_w", "lru_conv_b", "lru_gate_w", "lru_gate_b",
           "lru_lambda", "lru_w_out", "att_w_in", "att_q_norm", "att_k_norm", "att_lambda", "att_subln", "att_w_out"]


def make_in_maps(inputs, cores, nb=BPC):
    tc, ts, mats = _consts()
    shared = {k: np.ascontiguousarray(np.asarray(inputs[k], dtype=np.float32)) for k in _WNAMES}
    shared.update(k_tc=tc, k_ts=ts, k_mats=mats)
    x = np.asarray(inputs["x"]); ctx = np.asarray(inputs["ctx"]); c = np.asarray(inputs["c"]); c_ctx = np.asarray(inputs["c_ctx"])
    maps = []
    for i in cores:
        m = dict(shared)
        m["x"] = np.ascontiguousarray(x[i * BPC:i * BPC + nb])
        m["ctx"] = np.ascontiguousarray(ctx[i * BPC:i * BPC + nb])
        cc = np.zeros((3, D), np.float32)
        cc[0:nb] = c[i * BPC:i * BPC + nb]
        cc[2] = c_ctx
        m["cc"] = cc
        maps.append(m)
    return maps


def kernel(**inputs):
    nc = build_program()
    in_maps = make_in_maps(inputs, list(range(NCORES)))
    res = run_bass_kernel_spmd(nc, in_maps, core_ids=list(range(NCORES)))
    return np.concatenate([np.asarray(r["out"]) for r in res.results], axis=0).astype(np.float32)
```
